# Optimizing a Trainium2 kernel written in Bass

```python
import jax, jax.numpy as jnp
from jax import lax
import numpy as np

D_MODEL = 1024
BATCH = 4
SEQ = 4096
DEPTH = 2
DEC_BATCH = 8
DEC_SEQ = 64
PAST_LEN = 4096

CHUNK = 64
P_DIM = 256
EPS = 1e-6
NEG = -1e30
HA = 4
DK_A = 128
DV_A = 256
GATE_RANK = 16
GATE_TAU = 16.0
WA_K = HA * DK_A
WA_V = HA * DV_A
HB = 16
DH_B = 64
LEFT_CHUNKS = 8
WINDOW = LEFT_CHUNKS * CHUNK
BAND = (LEFT_CHUNKS + 1) * CHUNK
MAX_REL = 128
WB = HB * DH_B
IN_SIZES = (WA_K, WA_K, WA_V, GATE_RANK, WA_V, WB, WB, WB, WB, D_MODEL, D_MODEL)
N_IN = 2 * WA_K + 2 * WA_V + GATE_RANK + 4 * WB + 2 * D_MODEL

kernel_name = 'hybrid_gla_chunkband_stream_step'


def _rmsnorm(x, g):
    xf = x.astype(jnp.float32)
    y = xf * lax.rsqrt(jnp.mean(xf * xf, axis=-1, keepdims=True) + EPS)
    return (y * g.astype(jnp.float32)).astype(x.dtype)


def _split_cols(z):
    offsets = [int(o) for o in np.cumsum(IN_SIZES)[:-1]]
    return jnp.split(z, offsets, axis=-1)


def _gla_block(q, k, v, lg, s0):
    q, k, v, lg = (a.astype(jnp.float32) for a in (q, k, v, lg))
    s0 = s0.astype(jnp.float32)
    L = q.shape[1]
    b = jnp.cumsum(lg, axis=1)
    causal = jnp.tril(jnp.ones((L, L), dtype=bool))
    diff = b[:, :, None] - b[:, None, :]
    decay = jnp.exp(jnp.where(causal[None, :, :, None, None], diff, -jnp.inf))
    attn = jnp.einsum('bihd,bjhd,bijhd->bhij', q, k, decay)
    o = jnp.einsum('bhij,bjhv->bihv', attn, v)
    o = o + jnp.einsum('bihd,bhdv->bihv', q * jnp.exp(b), s0)
    b_last = b[:, -1]
    k_dec = k * jnp.exp(b_last[:, None] - b)
    s1 = jnp.exp(b_last)[..., None] * s0 + jnp.einsum('bjhd,bjhv->bhdv', k_dec, v)
    return o, s1


def _gla_prompt(q, k, v, lg):
    B, T, H, DK = q.shape
    DV = v.shape[-1]
    NC = T // CHUNK

    def to_chunks(a):
        return jnp.moveaxis(a.reshape(B, NC, CHUNK, H, a.shape[-1]), 1, 0)

    def step(s, blk):
        qc, kc, vc, lc = blk
        o, s = _gla_block(qc, kc, vc, lc, s)
        return s, o

    s0 = jnp.zeros((B, H, DK, DV), jnp.float32)
    s, o = lax.scan(step, s0, (to_chunks(q), to_chunks(k), to_chunks(v), to_chunks(lg)))
    return jnp.moveaxis(o, 0, 1).reshape(B, T, H, DV), s


def _band_prompt(q, k, v, bias_tab):
    B, T, H, D = q.shape
    NC = T // CHUNK
    pad = jnp.zeros((B, WINDOW, H, D), k.dtype)
    kp = jnp.concatenate([pad, k], axis=1)
    vp = jnp.concatenate([pad, v], axis=1)
    qc = jnp.moveaxis(q.reshape(B, NC, CHUNK, H, D), 1, 0)
    rel = WINDOW + jnp.arange(CHUNK)[:, None] - jnp.arange(BAND)[None, :]
    bias = bias_tab[:, jnp.clip(rel, -MAX_REL, MAX_REL) + MAX_REL].astype(jnp.float32)
    scale = D ** -0.5

    def one_chunk(args):
        n, qn = args
        kn = lax.dynamic_slice_in_dim(kp, n * CHUNK, BAND, axis=1)
        vn = lax.dynamic_slice_in_dim(vp, n * CHUNK, BAND, axis=1)
        s = jnp.einsum('bihd,bjhd->bhij', qn, kn).astype(jnp.float32) * scale + bias
        valid = n * CHUNK + jnp.arange(BAND) >= WINDOW
        s = jnp.where(valid[None, None, None, :], s, NEG)
        pr = jax.nn.softmax(s, axis=-1).astype(vn.dtype)
        return jnp.einsum('bhij,bjhd->bihd', pr, vn)

    o = lax.map(one_chunk, (jnp.arange(NC), qc))
    return jnp.moveaxis(o, 0, 1).reshape(B, T, H, D)


def _band_sample(q, k_new, v_new, k_cache, v_cache, bias_tab):
    S = q.shape[1]
    Lc = k_cache.shape[1]
    kk = jnp.concatenate([k_cache.astype(k_new.dtype), k_new], axis=1)
    vv = jnp.concatenate([v_cache.astype(v_new.dtype), v_new], axis=1)
    qpos = PAST_LEN + jnp.arange(S)
    kpos = jnp.concatenate([PAST_LEN - Lc + jnp.arange(Lc), PAST_LEN + jnp.arange(S)])
    rel = qpos[:, None] - kpos[None, :]
    bias = bias_tab[:, jnp.clip(rel, -MAX_REL, MAX_REL) + MAX_REL].astype(jnp.float32)
    s = jnp.einsum('bihd,bjhd->bhij', q, kk).astype(jnp.float32) * (q.shape[-1] ** -0.5) + bias
    pr = jax.nn.softmax(s, axis=-1).astype(vv.dtype)
    return jnp.einsum('bhij,bjhd->bihd', pr, vv)


def _layer(x, p, s0, kc, vc, lw, prompt):
    (norm_g, w_in, w_gate_up, b_gate, gla_norm_g, q_norm_g, k_norm_g, rel_bias,
     w_branch_a, w_branch_b, w_out, ple_norm_g, w_ple_gate, w_ple) = lw
    B, T, _ = x.shape
    h = _rmsnorm(x, norm_g)
    z = h @ w_in
    qa, ka, va, ra, ga, qb, kb, vb, gb, mga, mgb = _split_cols(z)

    qa = qa.reshape(B, T, HA, DK_A) * (DK_A ** -0.5)
    ka = ka.reshape(B, T, HA, DK_A)
    va = va.reshape(B, T, HA, DV_A)
    lg = jax.nn.log_sigmoid((ra @ w_gate_up + b_gate).astype(jnp.float32)) / GATE_TAU
    lg = lg.reshape(B, T, HA, DK_A)
    if prompt:
        oa, sa = _gla_prompt(qa, ka, va, lg)
    else:
        oa, sa = _gla_block(qa, ka, va, lg, s0)
    oa = _rmsnorm(oa.astype(x.dtype), gla_norm_g).reshape(B, T, WA_V)
    ya = (oa * jax.nn.silu(ga)) @ w_branch_a

    qb = _rmsnorm(qb.reshape(B, T, HB, DH_B), q_norm_g)
    kb = _rmsnorm(kb.reshape(B, T, HB, DH_B), k_norm_g)
    vb = vb.reshape(B, T, HB, DH_B)
    if prompt:
        ob = _band_prompt(qb, kb, vb, rel_bias)
        keep = min(WINDOW, T)
        kbuf = kb[:, T - keep:]
        vbuf = vb[:, T - keep:]
    else:
        ob = _band_sample(qb, kb, vb, kc, vc, rel_bias)
        Lc = kc.shape[1]
        kbuf = jnp.concatenate([kc.astype(kb.dtype), kb], axis=1)[:, -Lc:]
        vbuf = jnp.concatenate([vc.astype(vb.dtype), vb], axis=1)[:, -Lc:]
    yb = (ob.reshape(B, T, WB) * jax.nn.silu(gb)) @ w_branch_b

    m = jax.nn.sigmoid(mga) * ya + jax.nn.sigmoid(mgb) * yb
    x = x + m @ w_out
    x = x + jax.nn.sigmoid(_rmsnorm(x, ple_norm_g) @ w_ple_gate) * (p @ w_ple)
    return x, sa.astype(x.dtype), kbuf, vbuf


def setup_inputs(seed: int = 0) -> dict:
    key = jax.random.key(seed)
    ks = jax.random.split(key, 24)
    L_WIN = min(WINDOW, PAST_LEN)
    f32 = jnp.float32
    nrm = lambda k, shape, s=1.0: (jax.random.normal(k, shape, f32) * s)
    return {
        'x_prompt': nrm(ks[0], (BATCH, SEQ, D_MODEL)),
        'x_sample': nrm(ks[1], (DEC_BATCH, DEC_SEQ, D_MODEL)),
        'state_gla': nrm(ks[2], (DEPTH, DEC_BATCH, HA, DK_A, DV_A)),
        'cache_band_k': nrm(ks[3], (DEPTH, DEC_BATCH, L_WIN, HB, DH_B)),
        'cache_band_v': nrm(ks[4], (DEPTH, DEC_BATCH, L_WIN, HB, DH_B)),
        'p_prompt': nrm(ks[5], (DEPTH, BATCH, SEQ, P_DIM)),
        'p_sample': nrm(ks[6], (DEPTH, DEC_BATCH, DEC_SEQ, P_DIM)),
        'norm_g': 1.0 + nrm(ks[7], (DEPTH, D_MODEL), 0.02),
        'w_in': nrm(ks[8], (DEPTH, D_MODEL, N_IN), D_MODEL ** -0.5),
        'w_gate_up': nrm(ks[9], (DEPTH, GATE_RANK, WA_K), GATE_RANK ** -0.5),
        'b_gate': nrm(ks[10], (DEPTH, WA_K), 0.01),
        'gla_norm_g': 1.0 + nrm(ks[11], (DEPTH, DV_A), 0.02),
        'q_norm_g': 1.0 + nrm(ks[12], (DEPTH, DH_B), 0.02),
        'k_norm_g': 1.0 + nrm(ks[13], (DEPTH, DH_B), 0.02),
        'rel_bias': nrm(ks[14], (DEPTH, HB, 2 * MAX_REL + 1), 0.1),
        'w_branch_a': nrm(ks[15], (DEPTH, WA_V, D_MODEL), WA_V ** -0.5),
        'w_branch_b': nrm(ks[16], (DEPTH, WB, D_MODEL), WB ** -0.5),
        'w_out': nrm(ks[17], (DEPTH, D_MODEL, D_MODEL), D_MODEL ** -0.5),
        'ple_norm_g': 1.0 + nrm(ks[18], (DEPTH, D_MODEL), 0.02),
        'w_ple_gate': nrm(ks[19], (DEPTH, D_MODEL, D_MODEL), D_MODEL ** -0.5),
        'w_ple': nrm(ks[20], (DEPTH, P_DIM, D_MODEL), P_DIM ** -0.5),
    }


def reference(x_prompt, x_sample, state_gla, cache_band_k, cache_band_v, p_prompt, p_sample,
              norm_g, w_in, w_gate_up, b_gate, gla_norm_g, q_norm_g, k_norm_g, rel_bias,
              w_branch_a, w_branch_b, w_out, ple_norm_g, w_ple_gate, w_ple):
    xp = x_prompt
    xs = x_sample
    sp_list, kp_list, vp_list = [], [], []
    ss_list, ksl, vsl = [], [], []
    for i in range(DEPTH):
        lw = (norm_g[i], w_in[i], w_gate_up[i], b_gate[i], gla_norm_g[i], q_norm_g[i],
              k_norm_g[i], rel_bias[i], w_branch_a[i], w_branch_b[i], w_out[i],
              ple_norm_g[i], w_ple_gate[i], w_ple[i])
        xp, sp, kbp, vbp = _layer(xp, p_prompt[i], None, None, None, lw, True)
        xs, ss, kbs, vbs = _layer(xs, p_sample[i], state_gla[i], cache_band_k[i],
                                  cache_band_v[i], lw, False)
        sp_list.append(sp); kp_list.append(kbp); vp_list.append(vbp)
        ss_list.append(ss); ksl.append(kbs); vsl.append(vbs)
    new_state_gla_prompt = jnp.stack(sp_list)
    new_band_k_prompt = jnp.stack(kp_list)
    new_band_v_prompt = jnp.stack(vp_list)
    new_state_gla_sample = jnp.stack(ss_list)
    new_band_k_sample = jnp.stack(ksl)
    new_band_v_sample = jnp.stack(vsl)
    return (xp, xs, new_state_gla_prompt, new_band_k_prompt, new_band_v_prompt,
            new_state_gla_sample, new_band_k_sample, new_band_v_sample)
```

```python
import numpy as np
import concourse.bass as bass
import concourse.mybir as mybir
from concourse.bass_utils import run_bass_kernel_spmd

F32 = mybir.dt.float32
BF16 = mybir.dt.bfloat16
AF = mybir.ActivationFunctionType
ALU = mybir.AluOpType

D = 1024
N_IN = 9232
OFF_QA, OFF_KA, OFF_VA, OFF_RA, OFF_GA = 0, 512, 1024, 2048, 2064
OFF_QB, OFF_KB, OFF_VB, OFF_GB, OFF_MGA, OFF_MGB = 3088, 4112, 5136, 6160, 7184, 8208
EPS = 1e-6
TT = 512
QK_DEPTH = 1
import os
GLA_PAT = 'SPSSPSP'
NG = 18

ENGS = ("pe", "act", "dve", "pool", "sp")


class Op:
    __slots__ = ("eng", "fn", "deps", "inc", "count", "dma_sem", "dma_val", "tag", "meta")

    def __init__(self, eng, fn):
        self.eng = eng
        self.fn = fn
        self.deps = []
        self.inc = False
        self.count = 0
        self.dma_sem = None
        self.dma_val = 0


class Prog:
    def __init__(self, nc):
        self.nc = nc
        self.ops = {e: [] for e in ENGS}
        self.last_w = {}
        self.readers = {}
        self.dma_cnt = {}
        self.cur_tag = ""

    def op(self, eng, fn, reads=(), writes=(), dma_sem=None):
        o = Op(eng, fn)
        o.tag = self.cur_tag
        o.meta = None
        is_dma = dma_sem is not None
        deps = {}
        for r in reads:
            w = self.last_w.get(r)
            if w is not None:
                deps[id(w)] = (w, "raw")
        for r in writes:
            w = self.last_w.get(r)
            if w is not None and id(w) not in deps:
                deps[id(w)] = (w, "waw")
            for rd in self.readers.get(r, ()):
                if id(rd) not in deps:
                    deps[id(rd)] = (rd, "war")
        for d, kind in deps.values():
            d_is_dma = d.dma_sem is not None
            if d.eng == eng and not d_is_dma and not is_dma:
                if eng == "pe" or kind != "raw":
                    continue
            if not d_is_dma:
                d.inc = True
            o.deps.append(d)
        for r in reads:
            self.readers.setdefault(r, []).append(o)
        for r in writes:
            self.last_w[r] = o
            self.readers[r] = []
        if is_dma:
            o.dma_sem = dma_sem
            c = self.dma_cnt.get(dma_sem, 0) + 16
            self.dma_cnt[dma_sem] = c
            o.dma_val = c
        self.ops[eng].append(o)
        return o

    def emit(self, final_wait_sems=()):
        nc = self.nc
        esem = {e: nc.alloc_semaphore("es_" + e) for e in ENGS}
        dsem = {}
        for i, k in enumerate(self.dma_cnt):
            dsem[k] = nc.alloc_semaphore("ds%d" % i)
        for e in ENGS:
            c = 0
            for o in self.ops[e]:
                if o.dma_sem is None and o.inc:
                    c += 1
                    o.count = c
        ops = self.ops
        dma_cnt = self.dma_cnt

        def run(e, eng):
            known = {}
            for o in ops[e]:
                need = {}
                for d in o.deps:
                    if d.dma_sem is not None:
                        key = ("d", d.dma_sem)
                        val = d.dma_val
                    else:
                        key = ("e", d.eng)
                        val = d.count
                    if val > need.get(key, 0):
                        need[key] = val
                for key, val in need.items():
                    if known.get(key, 0) >= val:
                        continue
                    known[key] = val
                    sem = dsem[key[1]] if key[0] == "d" else esem[key[1]]
                    eng.wait_ge(sem, val)
                ins = o.fn(eng)
                if o.dma_sem is not None:
                    ins.then_inc(dsem[o.dma_sem], 16)
                elif o.inc:
                    ins.then_inc(esem[e], 1)
            if e == "sp":
                for k in final_wait_sems:
                    eng.wait_ge(dsem[k], dma_cnt[k])

        with nc.Block() as block:
            @block.tensor
            def _(eng):
                run("pe", eng)

            @block.scalar
            def _(eng):
                run("act", eng)

            @block.vector
            def _(eng):
                run("dve", eng)

            @block.gpsimd
            def _(eng):
                run("pool", eng)

            @block.sync
            def _(eng):
                run("sp", eng)


def build_program(SEQ, DEPTH, do_sample=True):
    nc = bass.Bass("TRN2", target_bir_lowering=False)
    P = Prog(nc)
    NT = SEQ // TT

    def din(name, shape):
        return nc.dram_tensor(name, list(shape), F32, kind="ExternalInput").ap()

    def dout(name, shape):
        return nc.dram_tensor(name, list(shape), F32, kind="ExternalOutput").ap()

    xp = din("xp", [SEQ, D]); pp = din("pp", [DEPTH, SEQ, 256])
    xs = din("xs", [64, D]); ps = din("ps", [DEPTH, 64, 256])
    sg = din("sg", [DEPTH, 4, 128, 256])
    ck = din("ck", [DEPTH, 512, D]); cv = din("cv", [DEPTH, 512, D])
    w_in = din("w_in", [DEPTH, D, N_IN])
    wgu_d = din("wgu", [DEPTH, 17, 512])
    wba = din("wba", [DEPTH, D, D]); wbb = din("wbb", [DEPTH, D, D])
    wo = din("wo", [DEPTH, D, D]); wpg = din("wpg", [DEPTH, D, D])
    wp = din("wp", [DEPTH, 256, D])
    gv_d = din("gv", [128, DEPTH * NG])
    gA_d = din("gA", [128, DEPTH, 256])
    tabext = nc.dram_tensor("tabext", [DEPTH, 16, 384], F32, kind="ExternalInput")
    cb_d = din("cb", [128, DEPTH, 16])
    cst_d = din("cst", [128, 6, 128])

    yp = dout("yp", [SEQ, D]); ys = dout("ys", [64, D])
    sgp = dout("sgp", [DEPTH, 4, 128, 256])
    kbp = dout("kbp", [DEPTH, 512, D]); vbp = dout("vbp", [DEPTH, 512, D])
    sgs = dout("sgs", [DEPTH, 4, 128, 256])
    kbs = dout("kbs", [DEPTH, 512, D]); vbs = dout("vbs", [DEPTH, 512, D])

    kspill = [nc.dram_tensor("kspill%d" % l, [128, 8 * 512], BF16, kind="Internal").ap() for l in range(DEPTH)]
    vspill = [nc.dram_tensor("vspill%d" % l, [128, 4 * 1056], BF16, kind="Internal").ap() for l in range(DEPTH)]

    def sb(name, shape, dt):
        return nc.alloc_sbuf_tensor("s_" + name, list(shape), dt)

    xT = sb("xT", [128, 8, TT], F32)
    hT = sb("hT", [128, 8, TT], BF16)
    R1 = sb("R1", [128, 8, TT], BF16)
    Qz = sb("Qz", [128, 8, 2, TT], BF16)
    va = sb("va", [128, 4, 1024], BF16)
    M2 = va[:].rearrange("p a b -> p (a b)")
    Ua = sb("Ua", [128, 8, TT], BF16)
    Ub = sb("Ub", [128, 8, TT], BF16)
    M = sb("M", [128, 8, TT], BF16)
    Kcur = sb("Kcur", [128, 8, TT], BF16)
    Kprev = sb("Kprev", [128, 8, TT], BF16)
    Vcur = sb("Vcur", [128, 4, 1056], BF16)
    Vprev = sb("Vprev", [128, 4, 1056], BF16)
    S = [sb("S%d" % l, [128, 4, 256], F32) for l in range(DEPTH)]
    Sbf = [sb("Sbf%d" % l, [128, 4, 256], BF16) for l in range(DEPTH)]
    NW = 3
    W = [sb("W%d" % i, [128, 8, 256], BF16) for i in range(NW)]
    Gp = sb("Gp", [128, DEPTH * 2, 16, 128], BF16)
    PT = [[sb("PT%d_%d" % (i, k), [128, 4, 128], BF16) for i in range(5)] for k in range(2)]
    e1 = sb("e1", [128, 512], F32)
    eb = sb("eb", [128, 4, 128], F32)
    enb = sb("enb", [128, 4, 128], F32)
    qt = sb("qt", [128, 4, 128], BF16)
    kt = sb("kt", [128, 4, 128], BF16)
    am = sb("am", [128, 4, 128], BF16)
    ktTs = sb("ktTs", [128, 512], BF16)
    on = sb("on", [128, 1024], BF16)
    ssm = sb("ssm", [128, 8], F32)
    rec = sb("rec", [128, 16], F32)
    sq = [sb("sq%d" % i, [128, TT], BF16) for i in range(3)]
    rstd = [sb("rstd%d" % i, [128, TT], F32) for i in range(2)]
    gt = [sq[0], sq[1]]
    junk = sq[2]
    qt2 = [qt, sb("qtB", [128, 4, 128], BF16)]
    am2 = [am, sb("amB", [128, 4, 128], BF16)]
    ktTs2 = [ktTs, sb("ktTsB", [128, 512], BF16)]
    ebl = sb("ebl", [128, 2, 4], F32)
    ftmp = sb("ftmp", [128, TT], F32)
    xin = [sb("xin%d" % i, [128, 1024], F32) for i in range(2)]
    pin = [sb("pin%d" % i, [128, 256], F32) for i in range(2)]
    pT = sb("pT", [128, 2, TT], BF16)
    cst = sb("cst", [128, 2, 128], F32)
    cstb = sb("cstb", [128, 6, 128], BF16)
    gvec = sb("gvec", [128, DEPTH * NG], F32)
    gq = sb("gq", [128, DEPTH], F32)
    gA = sb("gA", [128, DEPTH, 256], F32)
    cbt = sb("cbt", [128, DEPTH, 16], F32)
    wgu = sb("wgu", [32, DEPTH, 512], BF16)
    raT = sb("raT", [32, TT], BF16)
    pb = [nc.alloc_psum_tensor("pb%d" % i, [128, 512], F32) for i in range(8)]
    pbv = [b.bitcast(BF16) for b in pb]
    UaF = Ua[:].rearrange("p a b -> p (a b)").bitcast(F32)
    UbF = Ub[:].rearrange("p a b -> p (a b)").bitcast(F32)
    def xstage(tb):
        v = UaF if tb < 2 else UbF
        return v[:, (tb % 2) * 1024:(tb % 2 + 1) * 1024], [(("Ua" if tb < 2 else "Ub"), (tb % 2) * 4 + k) for k in range(4)]

    identf = cst[:, 0, :]
    tri = cst[:, 1, :]
    identb = cstb[:, 0, :]
    Jb = cstb[:, 1, :]
    cmaskb = cstb[:, 2, :]
    bonesb = cstb[:, 4, :]
    onesb = cstb[:, 5, :]

    st = {"bank": 0, "ws": 0, "ev": 0, "fb": 0, "mb": 0, "split": False, "kind": "mix"}
    out_sems = []

    def nb(kind=None):
        if st["split"]:
            if (kind or st["kind"]) == "fill":
                b = 5 + st["fb"] % 3
                st["fb"] += 1
            else:
                b = st["mb"] % 5
                st["mb"] += 1
            return b
        b = st["bank"]
        st["bank"] = (b + 1) % 8
        return b

    def interleave(gen, fillers, n_yields, name="", front=0):
        st["split"] = True
        P.cur_tag = "mix:" + name
        nf = len(fillers)
        done = 0
        y = 0
        for _ in gen:
            y += 1
            if y <= front:
                want = min(nf, y)
            else:
                want = min(nf, max(front, front + ((y - front) * (nf - front) + (n_yields - front) - 1) // max(1, n_yields - front)))
            while done < want:
                st["kind"] = "fill"
                P.cur_tag = "fill:" + name
                fillers[done]()
                P.cur_tag = "mix:" + name
                st["kind"] = "mix"
                done += 1
        while done < nf:
            st["kind"] = "fill"
            P.cur_tag = "fill:" + name
            fillers[done]()
            st["kind"] = "mix"
            done += 1
        st["split"] = False

    def evac_eng():
        st["ev"] += 1
        return "act" if st["ev"] % 2 == 0 else "dve"

    def copy_op(eng, out, in_, reads, writes, scale=None):
        if eng == "act":
            if scale is None:
                P.op("act", lambda e: e.activation(out=out, in_=in_, func=AF.Copy), reads, writes)
            else:
                P.op("act", lambda e: e.activation(out=out, in_=in_, func=AF.Copy, scale=scale), reads, writes)
        else:
            if scale is None:
                P.op("dve", lambda e: e.tensor_copy(out=out, in_=in_), reads, writes)
            else:
                P.op("dve", lambda e: e.tensor_scalar(out=out, in0=in_, scalar1=scale, scalar2=None, op0=ALU.mult), reads, writes)

    def mm(out, lhsT, rhs, start, stop, reads, writes):
        o = P.op("pe", lambda e: e.matmul(out, lhsT=lhsT, rhs=rhs, start=start, stop=stop, skip_group_check=True), reads, writes)
        o.meta = int(np.prod(rhs.shape[1:]))

    def tp(out, in_, ident, reads, writes):
        o = P.op("pe", lambda e: e.transpose(out, in_, ident), reads, writes)
        o.meta = int(np.prod(ident.shape[1:]))

    wcache = {}

    def wload(w2d, r0, nk, c0, ncols):
        s = st["ws"] % NW
        st["ws"] += 1
        dst = W[s][:, 0:nk, 0:ncols]
        key = (w2d.name, int(w2d.offset), r0, nk, c0, ncols)
        if key not in wcache:
            src = w2d[r0:r0 + nk * 128, c0:c0 + ncols].rearrange("(k p) c -> p k c", p=128)
            P.op("pool", lambda e: e.dma_start(out=dst, in_=src), writes=[("W", s)], dma_sem=("w", s))
            sc = nc.dram_tensor("wsc%d" % len(wcache), [128, nk * ncols], BF16, kind="Internal").ap()
            wcache[key] = sc
            P.op("sp", lambda e: e.dma_start(out=sc.rearrange("p (k c) -> p k c", k=nk), in_=dst),
                 reads=[("W", s)], writes=[("wsc", key)], dma_sem=("wst", s))
        else:
            sc = wcache[key]
            P.op("pool", lambda e: e.dma_start(out=dst, in_=sc.rearrange("p (k c) -> p k c", k=nk)),
                 reads=[("wsc", key)], writes=[("W", s)], dma_sem=("w", s))
        return s

    def rstd_from_psum(bank, i, ntok, n):
        P.op("act", lambda e: e.activation(out=rstd[i][:, 0:ntok], in_=pb[bank][:, 0:ntok], func=AF.Ln, scale=1.0 / n, bias=EPS),
             reads=[("pb", bank)], writes=[("rstd", i)])
        P.op("act", lambda e: e.activation(out=rstd[i][:, 0:ntok], in_=rstd[i][:, 0:ntok], func=AF.Exp, scale=-0.5),
             reads=[("rstd", i)], writes=[("rstd", i)])

    P.op("sp", lambda e: e.dma_start(out=cst[:, 0, :], in_=cst_d[:, 0, :]), writes=["init_sp"], dma_sem="init_sp")
    P.op("sp", lambda e: e.dma_start(out=cst[:, 1, :], in_=cst_d[:, 3, :]), writes=["init_sp"], dma_sem="init_sp")
    P.op("sp", lambda e: e.dma_start(out=gvec[:], in_=gv_d), writes=["init_sp"], dma_sem="init_sp")
    P.op("sp", lambda e: e.dma_start(out=gA[:], in_=gA_d), writes=["init_sp"], dma_sem="init_sp")
    P.op("sp", lambda e: e.dma_start(out=cbt[:], in_=cb_d), writes=["init_sp"], dma_sem="init_sp")
    P.op("pool", lambda e: e.dma_start(out=cstb[:], in_=cst_d), writes=["init_pool"], dma_sem="init_pool")
    P.op("pool", lambda e: e.dma_start(out=wgu[0:17, :, :], in_=wgu_d.rearrange("l k c -> k l c")), writes=["init_pool"], dma_sem="init_pool")
    P.op("dve", lambda e: e.memset(raT[:], 1.0), writes=["raT"])
    P.op("dve", lambda e: e.memset(Vcur[:], 1.0), writes=[("Vcur", tb, g) for tb in range(4) for g in range(4)])
    P.op("dve", lambda e: e.memset(Vprev[:], 1.0), writes=["Vprev"])
    P.op("dve", lambda e: e.memset(Qz[:], 0.0), writes=[("Qz", hp) for hp in range(8)])
    for k in range(2):
        for i in range(5):
            P.op("dve", lambda e, i=i, k=k: e.memset(PT[k][i][:], 0.0), writes=[("PT", k, i)])
    for l in range(DEPTH):
        P.op("dve", lambda e, l=l: e.memset(S[l][:], 0.0), writes=[("S", l)])
        P.op("dve", lambda e, l=l: e.memset(Sbf[l][:], 0.0), writes=[("Sbf", l)])
        P.op("dve", lambda e, l=l: e.tensor_scalar(out=gq[:, l:l + 1], in0=gvec[:, l * NG + 16:l * NG + 17], scalar1=0.125, scalar2=None, op0=ALU.mult),
             reads=["init_sp"], writes=["gq"])
        for C in range(2):
            stg = xin[C][:].rearrange("p (h i) -> p h i", h=8)
            for half in range(2):
                src = bass.AP(tabext, l * 16 * 384 + half * 8 * 384 + 1 + 128 * C, [[1, 128], [384, 8], [1, 128]])
                P.op("sp", lambda e, stg=stg, src=src: e.dma_start(out=stg, in_=src), writes=[("xin", C)], dma_sem=("ginit", C))
                cbb = cbt[:, l, half * 8:(half + 1) * 8].unsqueeze(2).broadcast_to([128, 8, 128])
                P.op("dve", lambda e, stg=stg, cbb=cbb, l=l, C=C, half=half: e.tensor_tensor(
                    out=Gp[:, l * 2 + C, half * 8:(half + 1) * 8, :], in0=stg, in1=cbb, op=ALU.subtract),
                    reads=[("xin", C), "init_sp"], writes=[("Gp", l)])

    def rmsnorm_to_hT(l, ntok, gcol0):
        bank = nb()
        for kc in range(8):
            if kc % 2 == 0:
                P.op("act", lambda e, kc=kc: e.activation(out=sq[kc % 2][:, 0:ntok], in_=xT[:, kc, 0:ntok], func=AF.Square),
                     reads=[("xT", kc)], writes=[("sq", kc % 2)])
            else:
                P.op("dve", lambda e, kc=kc: e.tensor_tensor(out=sq[kc % 2][:, 0:ntok], in0=xT[:, kc, 0:ntok], in1=xT[:, kc, 0:ntok], op=ALU.mult),
                     reads=[("xT", kc)], writes=[("sq", kc % 2)])
            mm(pb[bank][:, 0:ntok], onesb, sq[kc % 2][:, 0:ntok], kc == 0, kc == 7,
               reads=[("sq", kc % 2), "init_pool"], writes=[("pb", bank)])
        rstd_from_psum(bank, 0, ntok, D)
        for kc in range(8):
            P.op("dve", lambda e, kc=kc: e.scalar_tensor_tensor(
                out=hT[:, kc, 0:ntok], in0=xT[:, kc, 0:ntok], scalar=gvec[:, gcol0 + kc:gcol0 + kc + 1],
                in1=rstd[0][:, 0:ntok], op0=ALU.mult, op1=ALU.mult),
                reads=[("xT", kc), ("rstd", 0), "init_sp"], writes=[("hT", kc)])

    def formB(w2d, c0, nk, ncols, rhs_fn, rhs_res, evac):
        s = wload(w2d, 0, nk, c0, ncols)
        for cbi in range((ncols + 127) // 128):
            m = min(128, ncols - cbi * 128)
            bank = nb()
            for kc in range(nk):
                rhs = rhs_fn(kc)
                mm(pb[bank][0:m, 0:rhs.shape[-1]], W[s][:, kc, cbi * 128:cbi * 128 + m], rhs, kc == 0, kc == nk - 1,
                   reads=[("W", s), rhs_res(kc)], writes=[("pb", bank)])
            evac(bank, cbi)

    def formA(w2d, c0, ncols, ntok, evac):
        s = wload(w2d, 0, 8, c0, ncols)
        for tb in range((ntok + 127) // 128):
            nt_b = min(128, ntok - tb * 128)
            bank = nb()
            for kc in range(8):
                mm(pb[bank][0:nt_b, 0:ncols], hT[:, kc, tb * 128:tb * 128 + nt_b], W[s][:, kc, 0:ncols], kc == 0, kc == 7,
                   reads=[("W", s), ("hT", kc)], writes=[("pb", bank)])
            evac(bank, tb, nt_b)

    def formB_thunks(w2d, c0, nk, ncols, rhs_fn, rhs_res, evac):
        box = {}
        th = []
        ncb = (ncols + 127) // 128
        for cbi in range(ncb):
            def t(cbi=cbi):
                if cbi == 0:
                    box["s"] = wload(w2d, 0, nk, c0, ncols)
                s_ = box["s"]
                m = min(128, ncols - cbi * 128)
                bank = nb()
                for kc in range(nk):
                    rhs = rhs_fn(kc)
                    mm(pb[bank][0:m, 0:rhs.shape[-1]], W[s_][:, kc, cbi * 128:cbi * 128 + m], rhs, kc == 0, kc == nk - 1,
                       reads=[("W", s_), rhs_res(kc)], writes=[("pb", bank)])
                evac(bank, cbi)
            th.append(t)
        return th

    def formA_thunks(w2d, c0, ncols, ntok, evac):
        box = {}
        th = []
        for tb in range((ntok + 127) // 128):
            def t(tb=tb):
                if tb == 0:
                    box["s"] = wload(w2d, 0, 8, c0, ncols)
                s_ = box["s"]
                nt_b = min(128, ntok - tb * 128)
                bank = nb()
                for kc in range(8):
                    mm(pb[bank][0:nt_b, 0:ncols], hT[:, kc, tb * 128:tb * 128 + nt_b], W[s_][:, kc, 0:ncols], kc == 0, kc == 7,
                       reads=[("W", s_), ("hT", kc)], writes=[("pb", bank)])
                evac(bank, tb, nt_b)
            th.append(t)
        return th

    def tile_layer(l, ntok, tok0, xsrc, psrc, is_sample, t, last_tile, prefetched=False, nxt=None):
        nblk = (ntok + 127) // 128
        chunks = [(c * 128, min(128, ntok - c * 128)) for c in range(nblk)]
        w_in_l = w_in[l]
        hT_res = lambda kc: ("hT", kc)
        hT_rhs = lambda kc: hT[:, kc, 0:ntok]

        P.cur_tag = "pload"
        p_stage_xin = (not is_sample) and (not last_tile) and (prefetched or l > 0)
        for tb, (c0, nt) in enumerate(chunks):
            if tb < 2:
                P.op("sp", lambda e, tb=tb, c0=c0, nt=nt: e.dma_start(out=pin[tb % 2][0:nt, :], in_=psrc[l, tok0 + c0:tok0 + c0 + nt, :]),
                     writes=[("pin", tb % 2)], dma_sem=("pin", tb % 2))
            elif p_stage_xin:
                P.op("sp", lambda e, tb=tb, c0=c0, nt=nt: e.dma_start(out=xin[tb % 2][0:nt, 0:256], in_=psrc[l, tok0 + c0:tok0 + c0 + nt, :]),
                     writes=[("xin", tb % 2)], dma_sem=("pinx", tb % 2))
        P.cur_tag = "cache"
        have_prev = False
        if is_sample:
            have_prev = True
            P.op("sp", lambda e: e.dma_start(out=S[l][:], in_=sg[l].rearrange("h d v -> d h v")), writes=[("S", l)], dma_sem=("Sin", l))
            copy_op("act", Sbf[l][:], S[l][:], reads=[("S", l)], writes=[("Sbf", l)])
            P.op("pool", lambda e: e.dma_start(out=va[:], in_=ck[l].rearrange("(b p) c -> p b c", p=128)),
                 writes=[("va", tb, g) for tb in range(4) for g in range(4)], dma_sem="ckld")
            for tb in range(4):
                bank = nb()
                for hp in range(8):
                    tp(pbv[bank][:, hp * 128:(hp + 1) * 128], va[:, tb, hp * 128:(hp + 1) * 128], identb,
                       reads=[("va", tb, 0), "init_pool"], writes=[("pb", bank)])
                copy_op(evac_eng(), Kprev[:, :, tb * 128:(tb + 1) * 128], pbv[bank][:].rearrange("p (h t) -> p h t", h=8),
                        reads=[("pb", bank)], writes=["Kprev"])
            Uv = Ub[:].rearrange("p a b -> p (a b)").rearrange("p (t c) -> p t c", c=1024)
            P.op("pool", lambda e: e.dma_start(out=Uv, in_=cv[l].rearrange("(b p) c -> p b c", p=128)),
                 writes=[("Ub", kc) for kc in range(8)], dma_sem="cvld")
            for tb in range(4):
                copy_op("dve", Vprev[:, tb, :].rearrange("p (h e) -> p h e", e=66)[:, :, 0:64],
                        Uv[:, tb, :].rearrange("p (h e) -> p h e", e=64),
                        reads=[("Ub", kc) for kc in range(8)], writes=["Vprev"])
            P.op("sp", lambda e: e.dma_start(out=kbs[l, 0:448, :], in_=ck[l, 64:512, :]), dma_sem="kroll")
            P.op("sp", lambda e: e.dma_start(out=vbs[l, 0:448, :], in_=cv[l, 64:512, :]), dma_sem="vroll")
        elif t > 0:
            have_prev = True
            P.op("sp", lambda e: e.dma_start(out=Kprev[:].rearrange("p a b -> p (a b)"), in_=kspill[l]),
                 reads=[("kspill", l)], writes=["Kprev"], dma_sem="kprev")
            P.op("sp", lambda e: e.dma_start(out=Vprev[:].rearrange("p a b -> p (a b)"), in_=vspill[l]),
                 reads=[("vspill", l)], writes=["Vprev"], dma_sem="vprev")

        P.cur_tag = "xload"
        if l == 0:
            for tb, (c0, nt) in enumerate(chunks):
                if prefetched:
                    xsrc_sb, xres = xstage(tb)
                else:
                    P.op("pool", lambda e, tb=tb, c0=c0, nt=nt: e.dma_start(out=xin[tb % 2][0:nt, :], in_=xsrc[tok0 + c0:tok0 + c0 + nt, :]),
                         writes=[("xin", tb % 2)], dma_sem=("xin", tb % 2))
                    xsrc_sb, xres = xin[tb % 2], [("xin", tb % 2)]
                for g in range(2):
                    bank = nb()
                    for kk in range(4):
                        kc = g * 4 + kk
                        tp(pb[bank][:, kk * 128:kk * 128 + nt], xsrc_sb[0:nt, kc * 128:(kc + 1) * 128], identf[0:nt, 0:nt],
                           reads=xres + ["init_sp"], writes=[("pb", bank)])
                    copy_op("dve", xT[:, g * 4:(g + 1) * 4, c0:c0 + nt],
                            pb[bank][:].rearrange("p (k t) -> p k t", k=4)[:, :, 0:nt],
                            reads=[("pb", bank)], writes=[("xT", g * 4 + kk) for kk in range(4)])

        P.cur_tag = "norm1"
        rmsnorm_to_hT(l, ntok, l * NG)

        P.cur_tag = "inprojA"
        for g in range(2):
            def ev(bank, cbi, g=g):
                h = g * 2 + cbi
                copy_op("act", R1[:, h, 0:ntok], pb[bank][:, 0:ntok], reads=[("pb", bank)], writes=[("R1", h)], scale=128.0 ** -0.5)
            formB(w_in_l, OFF_QA + g * 256, 8, 256, hT_rhs, hT_res, ev)
        for g in range(2):
            def ev(bank, cbi, g=g):
                h = g * 2 + cbi
                copy_op("dve", R1[:, 4 + h, 0:ntok], pb[bank][:, 0:ntok], reads=[("pb", bank)], writes=[("R1", 4 + h)])
            formB(w_in_l, OFF_KA + g * 256, 8, 256, hT_rhs, hT_res, ev)

        def ev_ra(bank, cbi):
            copy_op("dve", raT[0:16, 0:ntok], pb[bank][0:16, 0:ntok], reads=[("pb", bank)], writes=["raT"])
        formB(w_in_l, OFF_RA, 8, 16, hT_rhs, hT_res, ev_ra)
        for g in range(4):
            def ev(bank, tb, nt_b, g=g):
                copy_op(evac_eng(), va[0:nt_b, tb, g * 256:(g + 1) * 256], pb[bank][0:nt_b, 0:256],
                        reads=[("pb", bank)], writes=[("va", tb, g)])
            formA(w_in_l, OFF_VA + g * 256, 256, ntok, ev)

        def gate_silu_group(c_off, g, Ux, Uname):
            def ev(bank, cbi):
                kc = g * 2 + cbi
                P.op("act", lambda e: e.activation(out=Ux[:, kc, 0:ntok], in_=pb[bank][:, 0:ntok], func=AF.Silu),
                     reads=[("pb", bank)], writes=[(Uname, kc)])
            formB(w_in_l, c_off + g * 256, 8, 256, hT_rhs, hT_res, ev)

        for g in range(4):
            gate_silu_group(OFF_GA, g, Ua, "Ua")

        def gla_gen():
            def prep(ci):
                c0, nt = chunks[ci]
                par = ci % 2
                mm(pb[0][0:nt, 0:512], raT[0:17, c0:c0 + nt], wgu[0:17, l, :], True, True,
                   reads=["raT", "init_pool"], writes=[("pb", 0)])
                P.op("act", lambda e: e.activation(out=e1[0:nt, :], in_=pb[0][0:nt, :], func=AF.Exp, scale=-1.0),
                     reads=[("pb", 0)], writes=["e1"])
                P.op("act", lambda e: e.activation(out=e1[0:nt, :], in_=e1[0:nt, :], func=AF.Ln, bias=1.0),
                     reads=["e1"], writes=["e1"])
                yield
                for h in range(4):
                    mm(pb[1][:, h * 128:h * 128 + nt], e1[0:nt, h * 128:(h + 1) * 128], tri[0:nt, 0:nt], True, True,
                       reads=["e1", "init_sp"], writes=[("pb", 1)])
                bB3 = pb[1][:].rearrange("p (h t) -> p h t", h=4)[:, :, 0:nt]
                P.op("act", lambda e: e.activation(out=eb[:, :, 0:nt], in_=bB3, func=AF.Exp), reads=[("pb", 1)], writes=["eb"])
                P.op("act", lambda e: e.activation(out=enb[:, :, 0:nt], in_=bB3, func=AF.Exp, scale=-1.0), reads=[("pb", 1)], writes=["enb"])
                P.op("dve", lambda e: e.tensor_copy(out=ebl[:, par, :].unsqueeze(2), in_=eb[:, :, nt - 1:nt]),
                     reads=["eb"], writes=[("ebl", par)])
                P.op("dve", lambda e: e.tensor_tensor(out=qt2[par][:, :, 0:nt], in0=R1[:, 0:4, c0:c0 + nt], in1=eb[:, :, 0:nt], op=ALU.mult),
                     reads=["eb"] + [("R1", h) for h in range(4)], writes=[("qt", par)])
                P.op("dve", lambda e: e.tensor_tensor(out=kt[:, :, 0:nt], in0=R1[:, 4:8, c0:c0 + nt], in1=enb[:, :, 0:nt], op=ALU.mult),
                     reads=["enb"] + [("R1", 4 + h) for h in range(4)], writes=["kt"])
                yield
                for h in range(4):
                    mm(pb[0][0:nt, h * 128:h * 128 + nt], kt[:, h, 0:nt], qt2[par][:, h, 0:nt], True, True,
                       reads=["kt", ("qt", par)], writes=[("pb", 0)])
                for h in range(4):
                    tp(pbv[1][0:nt, h * 128:(h + 1) * 128], kt[:, h, 0:nt], identb, reads=["kt", "init_pool"], writes=[("pb", 1)])
                cm = cmaskb[0:nt, 0:nt].unsqueeze(1).broadcast_to([nt, 4, nt])
                P.op("dve", lambda e: e.tensor_tensor(
                    out=am2[par][0:nt, :, 0:nt], in0=pb[0][0:nt, :].rearrange("p (h t) -> p h t", h=4)[:, :, 0:nt], in1=cm, op=ALU.mult),
                    reads=[("pb", 0), "init_pool"], writes=[("am", par)])
                copy_op("act", ktTs2[par][0:nt, :], pbv[1][0:nt, 0:512], reads=[("pb", 1)], writes=[("ktTs", par)])
                yield

            def seq(ci):
                c0, nt = chunks[ci]
                par = ci % 2
                bO = [2, 3]
                for h in range(4):
                    bank = bO[h // 2]
                    reg = pb[bank][0:nt, (h % 2) * 256:(h % 2 + 1) * 256]
                    mm(reg, am2[par][0:nt, h, 0:nt], va[0:nt, ci, h * 256:(h + 1) * 256], True, False,
                       reads=[("am", par), ("va", ci, h)], writes=[("pb", bank)])
                    mm(reg, qt2[par][:, h, 0:nt], Sbf[l][:, h, :], False, True, reads=[("qt", par), ("Sbf", l)], writes=[("pb", bank)])
                yield
                for hh in range(2):
                    for h in (hh * 2, hh * 2 + 1):
                        mm(pb[4][:, (h % 2) * 256:(h % 2 + 1) * 256], ktTs2[par][0:nt, h * 128:(h + 1) * 128], va[0:nt, ci, h * 256:(h + 1) * 256], True, True,
                           reads=[("ktTs", par), ("va", ci, h)], writes=[("pb", 4)])
                    Sv = S[l][:, hh * 2:(hh + 1) * 2, :].rearrange("p h v -> p (h v)")
                    P.op("dve", lambda e, Sv=Sv: e.tensor_tensor(out=Sv, in0=pb[4][:, :], in1=Sv, op=ALU.add),
                         reads=[("pb", 4), ("S", l)], writes=[("S", l)])
                eblb = ebl[:, par, :].unsqueeze(2).broadcast_to([128, 4, 256])
                P.op("dve", lambda e: e.tensor_tensor(out=S[l][:], in0=S[l][:], in1=eblb, op=ALU.mult),
                     reads=[("ebl", par), ("S", l)], writes=[("S", l)])
                copy_op("act", Sbf[l][:], S[l][:], reads=[("S", l)], writes=[("Sbf", l)])
                yield
                for h in range(4):
                    bank = bO[h // 2]
                    P.op("act", lambda e, bank=bank, h=h: e.activation(
                        out=junk[0:nt, 0:256], in_=pb[bank][0:nt, (h % 2) * 256:(h % 2 + 1) * 256], func=AF.Square, accum_out=ssm[0:nt, h:h + 1]),
                        reads=[("pb", bank)], writes=["ssm", ("sq", 2)])
                P.op("act", lambda e: e.activation(out=ssm[0:nt, 4:8], in_=ssm[0:nt, 0:4], func=AF.Ln, scale=1.0 / 256, bias=EPS),
                     reads=["ssm"], writes=["ssr"])
                P.op("act", lambda e: e.activation(out=ssm[0:nt, 4:8], in_=ssm[0:nt, 4:8], func=AF.Exp, scale=-0.5),
                     reads=["ssr"], writes=["ssr"])
                for h in range(4):
                    bank = bO[h // 2]
                    P.op("dve", lambda e, bank=bank, h=h: e.scalar_tensor_tensor(
                        out=on[0:nt, h * 256:(h + 1) * 256], in0=pb[bank][0:nt, (h % 2) * 256:(h % 2 + 1) * 256],
                        scalar=ssm[0:nt, 4 + h:5 + h], in1=gA[0:nt, l, :], op0=ALU.mult, op1=ALU.mult),
                        reads=[("pb", bank), "ssr", "init_sp"], writes=["on"])
                yield
                for blk in range(8):
                    tp(pbv[4][:, blk * 128:blk * 128 + nt], on[0:nt, blk * 128:(blk + 1) * 128], identb[0:nt, 0:nt],
                       reads=["on", "init_pool"], writes=[("pb", 4)])
                P.op("dve", lambda e: e.tensor_tensor(
                    out=Ua[:, :, c0:c0 + nt], in0=pbv[4][:].rearrange("p (b t) -> p b t", b=8)[:, :, 0:nt], in1=Ua[:, :, c0:c0 + nt], op=ALU.mult),
                    reads=[("pb", 4)] + [("Ua", kc) for kc in range(8)], writes=[("Ua", kc) for kc in range(8)])
                yield

            n = len(chunks)
            for _ in prep(0):
                yield
            for ci in range(n):
                sg_ = seq(ci)
                pg_ = prep(ci + 1) if ci + 1 < n else None
                for ch in GLA_PAT:
                    if ch == "S":
                        next(sg_)
                        yield
                    elif pg_ is not None:
                        next(pg_)
                        yield

        def qk_flush(keep=0):
            q = st.setdefault("qk_q", [])
            while len(q) > keep:
                q.pop(0)()

        def qk_group(c_off, g, is_q):
            def ev(bank, cbi):
                hp = g * 2 + cbi
                st["qk_n"] = st.get("qk_n", 0) + 1
                i = st["qk_n"] % 3
                P.op("act", lambda e: e.activation(out=sq[i][:, 0:ntok], in_=pb[bank][:, 0:ntok], func=AF.Square),
                     reads=[("pb", bank)], writes=[("sq", i)])
                qk_flush(QK_DEPTH - 1)

                def finish():
                    b2 = nb()
                    mm(pb[b2][:, 0:ntok], bonesb, sq[i][:, 0:ntok], True, True, reads=[("sq", i), "init_pool"], writes=[("pb", b2)])
                    ri = i % 2
                    rstd_from_psum(b2, ri, ntok, 64)
                    if is_q:
                        for par in range(2):
                            ps_ = slice(par * 64, (par + 1) * 64)
                            P.op("dve", lambda e, ps_=ps_, par=par: e.scalar_tensor_tensor(
                                out=Qz[ps_, hp, par, 0:ntok], in0=pb[bank][ps_, 0:ntok], scalar=gq[ps_, l:l + 1],
                                in1=rstd[ri][ps_, 0:ntok], op0=ALU.mult, op1=ALU.mult),
                                reads=[("pb", bank), ("rstd", ri), "gq"], writes=[("Qz", hp)])
                    else:
                        P.op("dve", lambda e: e.scalar_tensor_tensor(
                            out=Kcur[:, hp, 0:ntok], in0=pb[bank][:, 0:ntok], scalar=gvec[:, l * NG + 17:l * NG + 18],
                            in1=rstd[ri][:, 0:ntok], op0=ALU.mult, op1=ALU.mult),
                            reads=[("pb", bank), ("rstd", ri), "init_sp"], writes=[("Kcur", hp)])
                st["qk_q"].append(finish)
            formB(w_in_l, c_off + g * 256, 8, 256, hT_rhs, hT_res, ev)

        def vb_group(g):
            def ev(bank, tb, nt_b):
                copy_op(evac_eng(), Vcur[0:nt_b, tb, :].rearrange("p (h e) -> p h e", e=66)[:, g * 4:(g + 1) * 4, 0:64],
                        pb[bank][0:nt_b, 0:256].rearrange("p (h e) -> p h e", e=64),
                        reads=[("pb", bank)], writes=[("Vcur", tb, g)])
            formA(w_in_l, OFF_VB + g * 256, 256, ntok, ev)

        def mga_group(g):
            def ev(bank, cbi):
                kc = g * 2 + cbi
                P.op("act", lambda e: e.activation(out=M[:, kc, 0:ntok], in_=pb[bank][:, 0:ntok], func=AF.Sigmoid),
                     reads=[("pb", bank)], writes=[("M", kc)])
            formB(w_in_l, OFF_MGA + g * 256, 8, 256, hT_rhs, hT_res, ev)

        def mgb_group(g):
            def ev(bank, cbi):
                kc = g * 2 + cbi
                P.op("act", lambda e: e.activation(out=M2[:, kc * TT:kc * TT + ntok], in_=pb[bank][:, 0:ntok], func=AF.Sigmoid),
                     reads=[("pb", bank)], writes=[("va", kc // 2, (kc % 2) * 2), ("va", kc // 2, (kc % 2) * 2 + 1)])
            formB(w_in_l, OFF_MGB + g * 256, 8, 256, hT_rhs, hT_res, ev)

        Ua_rhs = lambda kc: Ua[:, kc, 0:ntok]
        Ua_res = lambda kc: ("Ua", kc)
        Ub_rhs = lambda kc: Ub[:, kc, 0:ntok]
        Ub_res = lambda kc: ("Ub", kc)

        def bra_group(g):
            def ev(bank, cbi):
                kc = g * 2 + cbi
                P.op("dve", lambda e: e.tensor_tensor(out=M[:, kc, 0:ntok], in0=pb[bank][:, 0:ntok], in1=M[:, kc, 0:ntok], op=ALU.mult),
                     reads=[("pb", bank), ("M", kc)], writes=[("M", kc)])
            formB(wba[l], g * 256, 8, 256, Ua_rhs, Ua_res, ev)

        P.cur_tag = "inprojB"
        for g in range(4):
            qk_group(OFF_QB, g, True)
        for g in range(4):
            qk_group(OFF_KB, g, False)
        qk_flush()
        fill1 = []
        for g in range(4):
            def evv(bank, tb, nt_b, g=g):
                copy_op(evac_eng(), Vcur[0:nt_b, tb, :].rearrange("p (h e) -> p h e", e=66)[:, g * 4:(g + 1) * 4, 0:64],
                        pb[bank][0:nt_b, 0:256].rearrange("p (h e) -> p h e", e=64),
                        reads=[("pb", bank)], writes=[("Vcur", tb, g)])
            fill1.extend(formA_thunks(w_in_l, OFF_VB + g * 256, 256, ntok, evv))
        interleave(gla_gen(), fill1, 7 * nblk, "gla")
        qk_flush()

        P.cur_tag = "kvout"
        if is_sample or last_tile:
            dst = (sgs if is_sample else sgp)[l].rearrange("h d v -> d h v")
            key = ("Sout", l, is_sample)
            P.op("sp", lambda e, dst=dst: e.dma_start(out=dst, in_=S[l][:]), reads=[("S", l)], dma_sem=key)
            out_sems.append(key)
            if is_sample:
                P.op("dve", lambda e: e.memset(S[l][:], 0.0), writes=[("S", l)])
                P.op("dve", lambda e: e.memset(Sbf[l][:], 0.0), writes=[("Sbf", l)])

        Kcur_all = [("Kcur", hp) for hp in range(8)]
        Vcur_all = [("Vcur", tb, g) for tb in range(4) for g in range(4)]
        if (not is_sample) and (not last_tile):
            P.op("sp", lambda e: e.dma_start(out=kspill[l], in_=Kcur[:].rearrange("p a b -> p (a b)")),
                 reads=Kcur_all, writes=[("kspill", l)], dma_sem=("kst", l))
            P.op("sp", lambda e: e.dma_start(out=vspill[l], in_=Vcur[:].rearrange("p a b -> p (a b)")),
                 reads=Vcur_all, writes=[("vspill", l)], dma_sem=("vst", l))
        if is_sample or last_tile:
            kdst = kbs if is_sample else kbp
            vdst = vbs if is_sample else vbp
            r0 = 448 if is_sample else 0
            for tb, (c0, nt) in enumerate(chunks):
                bank = nb()
                for hp in range(8):
                    tp(pbv[bank][0:nt, hp * 128:(hp + 1) * 128], Kcur[:, hp, c0:c0 + nt], identb, reads=[("Kcur", hp), "init_pool"], writes=[("pb", bank)])
                copy_op("act", xin[0][0:nt, :], pbv[bank][0:nt, :], reads=[("pb", bank)], writes=[("xin", 0)])
                key = ("kout", l, is_sample)
                P.op("sp", lambda e, c0=c0, nt=nt: e.dma_start(out=kdst[l, r0 + c0:r0 + c0 + nt, :], in_=xin[0][0:nt, :]),
                     reads=[("xin", 0)], dma_sem=key)
                copy_op("dve", xin[1][0:nt, :].rearrange("p (h e) -> p h e", e=64),
                        Vcur[0:nt, tb, :].rearrange("p (h e) -> p h e", e=66)[:, :, 0:64],
                        reads=[("Vcur", tb, g) for g in range(4)], writes=[("xin", 1)])
                key2 = ("vout", l, is_sample)
                P.op("sp", lambda e, c0=c0, nt=nt: e.dma_start(out=vdst[l, r0 + c0:r0 + c0 + nt, :], in_=xin[1][0:nt, :]),
                     reads=[("xin", 1)], dma_sem=key2)
            out_sems.extend([("kout", l, is_sample), ("vout", l, is_sample)])

        def band_gen():
            for pi, (q0, nq) in enumerate(chunks):
                blocks = []
                for C in range(4, -1, -1):
                    b = pi - C
                    if b >= 0:
                        nk = min(128, ntok - b * 128)
                        blocks.append((C, Kcur, "cur", b, nk))
                    elif have_prev:
                        blocks.append((C, Kprev, "prev", 4 + b, 128))

                def qk_exp(hg):
                    for (C, Kb, which, b, nk) in blocks:
                        bank = nb()
                        Kres = (lambda hp: ("Kcur", hp)) if which == "cur" else (lambda hp: "Kprev")
                        if C <= 1:
                            mm(pb[bank][0:nk, 0:4 * nq], Jb[:, 0:nk], Gp[:, l * 2 + C, hg * 4:(hg + 1) * 4, 0:nq], True, False,
                               reads=[("Gp", l), "init_pool"], writes=[("pb", bank)])
                        for h2 in range(2):
                            hp = hg * 2 + h2
                            mm(pb[bank][0:nk, h2 * 2 * nq:(h2 + 1) * 2 * nq], Kb[:, hp, b * 128:b * 128 + nk], Qz[:, hp, :, q0:q0 + nq],
                               C > 1, h2 == 1, reads=[Kres(hp), ("Qz", hp)], writes=[("pb", bank)])
                        src3 = pb[bank][:, 0:4 * nq].rearrange("p (h q) -> p h q", h=4)
                        P.op("act", lambda e, src3=src3, C=C, hg=hg, nk=nk: e.activation(
                            out=PT[hg % 2][C][0:nk, :, 0:nq], in_=src3[0:nk, :, 0:nq], func=AF.Exp),
                            reads=[("pb", bank)], writes=[("PT", hg % 2, C)])
                        if nq == 128 and C == 4:
                            P.op("dve", lambda e, C=C, hg=hg: e.memset(PT[hg % 2][C][0:64, :, 64:128], 0.0), writes=[("PT", hg % 2, C)])
                        elif nq == 128 and C == 0:
                            P.op("dve", lambda e, C=C, hg=hg: e.memset(PT[hg % 2][C][64:128, :, 0:64], 0.0), writes=[("PT", hg % 2, C)])
                        yield

                def pv_norm(hg):
                    ob_bank = nb()
                    for hh in range(4):
                        h = hg * 4 + hh
                        for bi, (C, Kb, which, b, nk) in enumerate(blocks):
                            Vb = Vcur if which == "cur" else Vprev
                            vres = [("Vcur", b, h // 4)] if which == "cur" else ["Vprev"]
                            mm(pb[ob_bank][0:nq, hh * 66:(hh + 1) * 66], PT[hg % 2][C][0:nk, hh, 0:nq], Vb[0:nk, b, h * 66:(h + 1) * 66],
                               bi == 0, bi == len(blocks) - 1, reads=[("PT", hg % 2, C)] + vres, writes=[("pb", ob_bank)])
                    ob3 = pb[ob_bank][0:nq, 0:264].rearrange("p (h e) -> p h e", e=66)
                    P.op("dve", lambda e, ob3=ob3, hg=hg: e.reciprocal(out=rec[0:nq, hg * 4:(hg + 1) * 4].unsqueeze(2), in_=ob3[:, :, 64:65]),
                         reads=[("pb", ob_bank)], writes=["rec"])
                    recb = rec[0:nq, hg * 4:(hg + 1) * 4].unsqueeze(2).broadcast_to([nq, 4, 64])
                    P.op("dve", lambda e, ob3=ob3, hg=hg, recb=recb: e.tensor_tensor(
                        out=on[0:nq, hg * 256:(hg + 1) * 256].rearrange("p (h e) -> p h e", e=64), in0=ob3[:, :, 0:64], in1=recb, op=ALU.mult),
                        reads=[("pb", ob_bank), "rec"], writes=["on"])

                for hg in range(5):
                    if hg < 4:
                        yield from qk_exp(hg)
                    if hg >= 1:
                        pv_norm(hg - 1)
                        yield
                bI = nb()
                for blk in range(8):
                    tp(pbv[bI][:, blk * 128:blk * 128 + nq], on[0:nq, blk * 128:(blk + 1) * 128], identb[0:nq, 0:nq],
                       reads=["on", "init_pool"], writes=[("pb", bI)])
                P.op("dve", lambda e, bI=bI, q0=q0, nq=nq: e.tensor_tensor(
                    out=Ub[:, :, q0:q0 + nq], in0=pbv[bI][:].rearrange("p (b t) -> p b t", b=8)[:, :, 0:nq], in1=Ub[:, :, q0:q0 + nq], op=ALU.mult),
                    reads=[("pb", bI)] + [("Ub", kc) for kc in range(8)], writes=[("Ub", kc) for kc in range(8)])
                yield

        fill2 = []
        for g in range(4):
            def evg(bank, cbi, g=g):
                kc = g * 2 + cbi
                i = kc % 2
                P.op("act", lambda e: e.activation(out=gt[i][:, 0:ntok], in_=pb[bank][:, 0:ntok], func=AF.Tanh, scale=0.5),
                     reads=[("pb", bank)], writes=[("sq", i)])
                P.op("dve", lambda e: e.scalar_tensor_tensor(out=Ub[:, kc, 0:ntok], in0=gt[i][:, 0:ntok], scalar=1.0, in1=pb[bank][:, 0:ntok],
                                                             op0=ALU.add, op1=ALU.mult),
                     reads=[("sq", i), ("pb", bank)], writes=[("Ub", kc)])
            fill2.extend(formB_thunks(w_in_l, OFF_GB + g * 256, 8, 256, hT_rhs, hT_res, evg))
        n_front = len(fill2)
        for g in range(4):
            def evm(bank, cbi, g=g):
                kc = g * 2 + cbi
                P.op("act", lambda e: e.activation(out=M[:, kc, 0:ntok], in_=pb[bank][:, 0:ntok], func=AF.Tanh, scale=0.5),
                     reads=[("pb", bank)], writes=[("M", kc)])
            fill2.extend(formB_thunks(w_in_l, OFF_MGA + g * 256, 8, 256, hT_rhs, hT_res, evm))
        for g in range(4):
            def evb(bank, cbi, g=g):
                kc = g * 2 + cbi
                P.op("act", lambda e: e.activation(out=M2[:, kc * TT:kc * TT + ntok], in_=pb[bank][:, 0:ntok], func=AF.Tanh, scale=0.5),
                     reads=[("pb", bank)], writes=[("va", kc // 2, (kc % 2) * 2), ("va", kc // 2, (kc % 2) * 2 + 1)])
            fill2.extend(formB_thunks(w_in_l, OFF_MGB + g * 256, 8, 256, hT_rhs, hT_res, evb))
        for g in range(4):
            def eva(bank, cbi, g=g):
                kc = g * 2 + cbi
                P.op("dve", lambda e: e.scalar_tensor_tensor(out=M[:, kc, 0:ntok], in0=M[:, kc, 0:ntok], scalar=1.0, in1=pb[bank][:, 0:ntok],
                                                             op0=ALU.add, op1=ALU.mult),
                     reads=[("pb", bank), ("M", kc)], writes=[("M", kc)])
            fill2.extend(formB_thunks(wba[l], g * 256, 8, 256, Ua_rhs, Ua_res, eva))
        ny = 0
        for pi in range(nblk):
            nbk = sum(1 for C in range(5) if (pi - C >= 0) or have_prev)
            ny += 4 * (nbk + 1) + 1
        interleave(band_gen(), fill2, ny, "band", front=n_front)

        P.cur_tag = "branchB"
        for g in range(4):
            def ev(bank, cbi, g=g):
                kc = g * 2 + cbi
                P.op("dve", lambda e: e.scalar_tensor_tensor(out=ftmp[:, 0:ntok], in0=M2[:, kc * TT:kc * TT + ntok], scalar=1.0, in1=pb[bank][:, 0:ntok],
                                                             op0=ALU.add, op1=ALU.mult),
                     reads=[("pb", bank), ("va", kc // 2, (kc % 2) * 2), ("va", kc // 2, (kc % 2) * 2 + 1)], writes=["ftmp"])
                P.op("dve", lambda e: e.scalar_tensor_tensor(out=M[:, kc, 0:ntok], in0=ftmp[:, 0:ntok], scalar=0.5, in1=M[:, kc, 0:ntok],
                                                             op0=ALU.mult, op1=ALU.add),
                     reads=["ftmp", ("M", kc)], writes=[("M", kc)])
            formB(wbb[l], g * 256, 8, 256, Ub_rhs, Ub_res, ev)

        if nxt is not None and l == DEPTH - 1:
            nsrc, ntok0, nntok = nxt
            for tb in range((nntok + 127) // 128):
                nt2 = min(128, nntok - tb * 128)
                dstv, dres = xstage(tb)
                P.op("sp", lambda e, tb=tb, nt2=nt2, dstv=dstv: e.dma_start(out=dstv[0:nt2, :], in_=nsrc[ntok0 + tb * 128:ntok0 + tb * 128 + nt2, :]),
                     writes=dres, dma_sem=("xpre", tb))
        P.cur_tag = "pload"
        for tb, (c0, nt) in enumerate(chunks):
            psb, pres = pin[tb % 2], ("pin", tb % 2)
            if tb >= 2:
                if p_stage_xin:
                    psb, pres = xin[tb % 2], ("xin", tb % 2)
                else:
                    P.op("sp", lambda e, tb=tb, c0=c0, nt=nt: e.dma_start(out=pin[tb % 2][0:nt, :], in_=psrc[l, tok0 + c0:tok0 + c0 + nt, :]),
                         writes=[("pin", tb % 2)], dma_sem=("pin", tb % 2))
            bank = nb()
            for j in range(2):
                tp(pb[bank][:, j * 128:j * 128 + nt], psb[0:nt, j * 128:(j + 1) * 128], identf[0:nt, 0:nt],
                   reads=[pres, "init_sp"], writes=[("pb", bank)])
            copy_op("act", pT[:, 0:2, c0:c0 + nt], pb[bank][:, 0:256].rearrange("p (j t) -> p j t", j=2)[:, :, 0:nt],
                    reads=[("pb", bank)], writes=["pT"])
        P.cur_tag = "wout"
        M_rhs = lambda kc: M[:, kc, 0:ntok]
        M_res = lambda kc: ("M", kc)
        for g in range(4):
            def ev(bank, cbi, g=g):
                kc = g * 2 + cbi
                P.op("dve", lambda e: e.scalar_tensor_tensor(out=xT[:, kc, 0:ntok], in0=pb[bank][:, 0:ntok], scalar=0.5, in1=xT[:, kc, 0:ntok],
                                                             op0=ALU.mult, op1=ALU.add),
                     reads=[("pb", bank), ("xT", kc)], writes=[("xT", kc)])
            formB(wo[l], g * 256, 8, 256, M_rhs, M_res, ev)

        P.cur_tag = "ple"
        rmsnorm_to_hT(l, ntok, l * NG + 8)
        for g in range(4):
            def ev(bank, cbi, g=g):
                kc = g * 2 + cbi
                P.op("act", lambda e: e.activation(out=M[:, kc, 0:ntok], in_=pb[bank][:, 0:ntok], func=AF.Sigmoid),
                     reads=[("pb", bank)], writes=[("M", kc)])
            formB(wpg[l], g * 256, 8, 256, hT_rhs, hT_res, ev)
        pT_rhs = lambda kc: pT[:, kc, 0:ntok]
        pT_res = lambda kc: "pT"
        for g in range(4):
            def ev(bank, cbi, g=g):
                kc = g * 2 + cbi
                P.op("dve", lambda e: e.tensor_tensor(out=ftmp[:, 0:ntok], in0=pb[bank][:, 0:ntok], in1=M[:, kc, 0:ntok], op=ALU.mult),
                     reads=[("pb", bank), ("M", kc)], writes=["ftmp"])
                P.op("dve", lambda e: e.tensor_tensor(out=xT[:, kc, 0:ntok], in0=ftmp[:, 0:ntok], in1=xT[:, kc, 0:ntok], op=ALU.add),
                     reads=["ftmp", ("xT", kc)], writes=[("xT", kc)])
            formB(wp[l], g * 256, 2, 256, pT_rhs, pT_res, ev)

        P.cur_tag = "yout"
        if l == DEPTH - 1:
            ydst = ys if is_sample else yp
            for tb, (c0, nt) in enumerate(chunks):
                buf = tb % 2
                for g in range(2):
                    bank = nb()
                    for kk in range(4):
                        kc = g * 4 + kk
                        tp(pb[bank][0:nt, kk * 128:(kk + 1) * 128], xT[:, kc, c0:c0 + nt], identf, reads=[("xT", kc), "init_sp"], writes=[("pb", bank)])
                    copy_op("dve", xin[buf][0:nt, g * 512:(g + 1) * 512], pb[bank][0:nt, :], reads=[("pb", bank)], writes=[("xin", buf)])
                key = ("yout", buf, is_sample)
                P.op("sp", lambda e, c0=c0, nt=nt, buf=buf: e.dma_start(out=ydst[tok0 + c0:tok0 + c0 + nt, :], in_=xin[buf][0:nt, :]),
                     reads=[("xin", buf)], dma_sem=key)
                if key not in out_sems:
                    out_sems.append(key)

    tiles = [(TT, t * TT, xp, pp, False, t, t == NT - 1) for t in range(NT)]
    if do_sample:
        tiles.append((64, 0, xs, ps, True, 0, False))
    for ti, (ntok_, tok0_, xsrc_, psrc_, iss_, t_, last_) in enumerate(tiles):
        nxt = None
        if ti + 1 < len(tiles):
            n2 = tiles[ti + 1]
            nxt = (n2[2], n2[1], n2[0])
        for l in range(DEPTH):
            tile_layer(l, ntok_, tok0_, xsrc_, psrc_, iss_, t_, last_, prefetched=(ti > 0), nxt=nxt)
    if do_sample:
        out_sems.extend(["kroll", "vroll"])
    P.emit(final_wait_sems=out_sems)
    return nc


_CACHE = {}


def _consts():
    c = np.zeros((128, 6, 128), np.float32)
    c[:, 0, :] = np.eye(128)
    c[:, 1, :] = np.eye(128)[::-1]
    j = np.arange(128)[:, None]
    i = np.arange(128)[None, :]
    c[:, 2, :] = (j <= i)
    c[:, 3, :] = (j <= i) * (-1.0 / 16.0)
    blk = np.zeros((128, 128), np.float32)
    blk[0:64, 0:64] = 1
    blk[64:128, 64:128] = 1
    c[:, 4, :] = blk
    c[:, 5, :] = 1
    return c


def make_in_maps(inp, SEQ, DEPTH, n_cores=8):
    f = lambda a: np.ascontiguousarray(np.asarray(a, dtype=np.float32))
    gv = np.zeros((128, DEPTH * NG), np.float32)
    for l in range(DEPTH):
        gv[:, l * NG:l * NG + 8] = f(inp["norm_g"])[l].reshape(8, 128).T
        gv[:, l * NG + 8:l * NG + 16] = f(inp["ple_norm_g"])[l].reshape(8, 128).T
        gv[:, l * NG + 16] = np.tile(f(inp["q_norm_g"])[l], 2)
        gv[:, l * NG + 17] = np.tile(f(inp["k_norm_g"])[l], 2)
    gA = np.ascontiguousarray(np.broadcast_to(f(inp["gla_norm_g"])[None, :, :], (128, DEPTH, 256)))
    rb = f(inp["rel_bias"])
    tabext = np.ascontiguousarray(np.concatenate([rb, np.repeat(rb[..., -1:], 127, axis=-1)], axis=-1))
    cb = np.ascontiguousarray(np.broadcast_to(rb[None, :, :, 256], (128, DEPTH, 16)))
    wgu = np.ascontiguousarray(np.concatenate([f(inp["w_gate_up"]), f(inp["b_gate"])[:, None, :]], axis=1))
    common = {
        "w_in": f(inp["w_in"]), "wgu": wgu, "wba": f(inp["w_branch_a"]), "wbb": f(inp["w_branch_b"]),
        "wo": f(inp["w_out"]), "wpg": f(inp["w_ple_gate"]), "wp": f(inp["w_ple"]),
        "gv": gv, "gA": gA, "tabext": tabext, "cb": cb, "cst": _consts(),
    }
    xp = f(inp["x_prompt"]); pp = f(inp["p_prompt"]); xs = f(inp["x_sample"]); ps = f(inp["p_sample"])
    sg = f(inp["state_gla"]); ck = f(inp["cache_band_k"]); cv = f(inp["cache_band_v"])
    maps = []
    for c in range(n_cores):
        b = c % xp.shape[0]
        m = dict(common)
        m["xp"] = np.ascontiguousarray(xp[b])
        m["pp"] = np.ascontiguousarray(pp[:, b])
        m["xs"] = np.ascontiguousarray(xs[c])
        m["ps"] = np.ascontiguousarray(ps[:, c])
        m["sg"] = np.ascontiguousarray(sg[:, c])
        m["ck"] = np.ascontiguousarray(ck[:, c].reshape(DEPTH, 512, 1024))
        m["cv"] = np.ascontiguousarray(cv[:, c].reshape(DEPTH, 512, 1024))
        maps.append(m)
    return maps


def run(inp, SEQ, DEPTH):
    key = (SEQ, DEPTH)
    if key not in _CACHE:
        _CACHE[key] = build_program(SEQ, DEPTH)
    nc = _CACHE[key]
    maps = make_in_maps(inp, SEQ, DEPTH)
    res = run_bass_kernel_spmd(nc, maps, core_ids=list(range(8)))
    r = res.results
    B = np.asarray(inp["x_prompt"]).shape[0]
    y_prompt = np.stack([r[b]["yp"] for b in range(B)])
    y_sample = np.stack([r[c]["ys"] for c in range(8)])
    sgp = np.stack([r[b]["sgp"] for b in range(B)], axis=1)
    kbp = np.stack([r[b]["kbp"] for b in range(B)], axis=1).reshape(DEPTH, B, 512, 16, 64)
    vbp = np.stack([r[b]["vbp"] for b in range(B)], axis=1).reshape(DEPTH, B, 512, 16, 64)
    sgs = np.stack([r[c]["sgs"] for c in range(8)], axis=1)
    kbs = np.stack([r[c]["kbs"] for c in range(8)], axis=1).reshape(DEPTH, 8, 512, 16, 64)
    vbs = np.stack([r[c]["vbs"] for c in range(8)], axis=1).reshape(DEPTH, 8, 512, 16, 64)
    return (y_prompt, y_sample, sgp, kbp, vbp, sgs, kbs, vbs)


def kernel(**inputs):
    return run(inputs, 4096, 2)
```

```python
import numpy as np
import concourse.bass as bass
import concourse.mybir as mybir
from concourse.bass_utils import run_bass_kernel_spmd

F32 = mybir.dt.float32
BF16 = mybir.dt.bfloat16
AF = mybir.ActivationFunctionType
ALU = mybir.AluOpType

D = 1024
N_IN = 9232
OFF_QA, OFF_KA, OFF_VA, OFF_RA, OFF_GA = 0, 512, 1024, 2048, 2064
OFF_QB, OFF_KB, OFF_VB, OFF_GB, OFF_MGA, OFF_MGB = 3088, 4112, 5136, 6160, 7184, 8208
EPS = 1e-6
TT = 512
QK_DEPTH = 1
import os
GLA_PAT = 'SPSSPSP'
NG = 18

ENGS = ("pe", "act", "dve", "pool", "sp")


class Op:
    __slots__ = ("eng", "fn", "deps", "inc", "count", "dma_sem", "dma_val", "tag", "meta")

    def __init__(self, eng, fn):
        self.eng = eng
        self.fn = fn
        self.deps = []
        self.inc = False
        self.count = 0
        self.dma_sem = None
        self.dma_val = 0


class Prog:
    def __init__(self, nc):
        self.nc = nc
        self.ops = {e: [] for e in ENGS}
        self.last_w = {}
        self.readers = {}
        self.dma_cnt = {}
        self.cur_tag = ""

    def op(self, eng, fn, reads=(), writes=(), dma_sem=None):
        o = Op(eng, fn)
        o.tag = self.cur_tag
        o.meta = None
        is_dma = dma_sem is not None
        deps = {}
        for r in reads:
            w = self.last_w.get(r)
            if w is not None:
                deps[id(w)] = (w, "raw")
        for r in writes:
            w = self.last_w.get(r)
            if w is not None and id(w) not in deps:
                deps[id(w)] = (w, "waw")
            for rd in self.readers.get(r, ()):
                if id(rd) not in deps:
                    deps[id(rd)] = (rd, "war")
        for d, kind in deps.values():
            d_is_dma = d.dma_sem is not None
            if d.eng == eng and not d_is_dma and not is_dma:
                if eng == "pe" or kind != "raw":
                    continue
            if not d_is_dma:
                d.inc = True
            o.deps.append(d)
        for r in reads:
            self.readers.setdefault(r, []).append(o)
        for r in writes:
            self.last_w[r] = o
            self.readers[r] = []
        if is_dma:
            o.dma_sem = dma_sem
            c = self.dma_cnt.get(dma_sem, 0) + 16
            self.dma_cnt[dma_sem] = c
            o.dma_val = c
        self.ops[eng].append(o)
        return o

    def emit(self, final_wait_sems=()):
        nc = self.nc
        esem = {e: nc.alloc_semaphore("es_" + e) for e in ENGS}
        dsem = {}
        for i, k in enumerate(self.dma_cnt):
            dsem[k] = nc.alloc_semaphore("ds%d" % i)
        for e in ENGS:
            c = 0
            for o in self.ops[e]:
                if o.dma_sem is None and o.inc:
                    c += 1
                    o.count = c
        ops = self.ops
        dma_cnt = self.dma_cnt

        def run(e, eng):
            known = {}
            for o in ops[e]:
                need = {}
                for d in o.deps:
                    if d.dma_sem is not None:
                        key = ("d", d.dma_sem)
                        val = d.dma_val
                    else:
                        key = ("e", d.eng)
                        val = d.count
                    if val > need.get(key, 0):
                        need[key] = val
                for key, val in need.items():
                    if known.get(key, 0) >= val:
                        continue
                    known[key] = val
                    sem = dsem[key[1]] if key[0] == "d" else esem[key[1]]
                    eng.wait_ge(sem, val)
                ins = o.fn(eng)
                if o.dma_sem is not None:
                    ins.then_inc(dsem[o.dma_sem], 16)
                elif o.inc:
                    ins.then_inc(esem[e], 1)
            if e == "sp":
                for k in final_wait_sems:
                    eng.wait_ge(dsem[k], dma_cnt[k])

        with nc.Block() as block:
            @block.tensor
            def _(eng):
                run("pe", eng)

            @block.scalar
            def _(eng):
                run("act", eng)

            @block.vector
            def _(eng):
                run("dve", eng)

            @block.gpsimd
            def _(eng):
                run("pool", eng)

            @block.sync
            def _(eng):
                run("sp", eng)


def build_program(SEQ, DEPTH, do_sample=True):
    nc = bass.Bass("TRN2", target_bir_lowering=False)
    P = Prog(nc)
    NT = SEQ // TT

    def din(name, shape):
        return nc.dram_tensor(name, list(shape), F32, kind="ExternalInput").ap()

    def dout(name, shape):
        return nc.dram_tensor(name, list(shape), F32, kind="ExternalOutput").ap()

    xp = din("xp", [SEQ, D]); pp = din("pp", [DEPTH, SEQ, 256])
    xs = din("xs", [64, D]); ps = din("ps", [DEPTH, 64, 256])
    sg = din("sg", [DEPTH, 4, 128, 256])
    ck = din("ck", [DEPTH, 512, D]); cv = din("cv", [DEPTH, 512, D])
    w_in = din("w_in", [DEPTH, D, N_IN])
    wgu_d = din("wgu", [DEPTH, 17, 512])
    wba = din("wba", [DEPTH, D, D]); wbb = din("wbb", [DEPTH, D, D])
    wo = din("wo", [DEPTH, D, D]); wpg = din("wpg", [DEPTH, D, D])
    wp = din("wp", [DEPTH, 256, D])
    gv_d = din("gv", [128, DEPTH * NG])
    gA_d = din("gA", [128, DEPTH, 256])
    tabext = nc.dram_tensor("tabext", [DEPTH, 16, 384], F32, kind="ExternalInput")
    cb_d = din("cb", [128, DEPTH, 16])
    cst_d = din("cst", [128, 6, 128])

    yp = dout("yp", [SEQ, D]); ys = dout("ys", [64, D])
    sgp = dout("sgp", [DEPTH, 4, 128, 256])
    kbp = dout("kbp", [DEPTH, 512, D]); vbp = dout("vbp", [DEPTH, 512, D])
    sgs = dout("sgs", [DEPTH, 4, 128, 256])
    kbs = dout("kbs", [DEPTH, 512, D]); vbs = dout("vbs", [DEPTH, 512, D])

    kspill = [nc.dram_tensor("kspill%d" % l, [128, 8 * 512], BF16, kind="Internal").ap() for l in range(DEPTH)]
    vspill = [nc.dram_tensor("vspill%d" % l, [128, 4 * 1056], BF16, kind="Internal").ap() for l in range(DEPTH)]

    def sb(name, shape, dt):
        return nc.alloc_sbuf_tensor("s_" + name, list(shape), dt)

    xT = sb("xT", [128, 8, TT], F32)
    hT = sb("hT", [128, 8, TT], BF16)
    R1 = sb("R1", [128, 8, TT], BF16)
    Qz = sb("Qz", [128, 8, 2, TT], BF16)
    va = sb("va", [128, 4, 1024], BF16)
    M2 = va[:].rearrange("p a b -> p (a b)")
    Ua = sb("Ua", [128, 8, TT], BF16)
    Ub = sb("Ub", [128, 8, TT], BF16)
    M = sb("M", [128, 8, TT], BF16)
    Kcur = sb("Kcur", [128, 8, TT], BF16)
    Kprev = sb("Kprev", [128, 8, TT], BF16)
    Vcur = sb("Vcur", [128, 4, 1056], BF16)
    Vprev = sb("Vprev", [128, 4, 1056], BF16)
    S = [sb("S%d" % l, [128, 4, 256], F32) for l in range(DEPTH)]
    Sbf = [sb("Sbf%d" % l, [128, 4, 256], BF16) for l in range(DEPTH)]
    NW = 3
    W = [sb("W%d" % i, [128, 8, 256], BF16) for i in range(NW)]
    Gp = sb("Gp", [128, DEPTH * 2, 16, 128], BF16)
    PT = [[sb("PT%d_%d" % (i, k), [128, 4, 128], BF16) for i in range(5)] for k in range(2)]
    e1 = sb("e1", [128, 512], F32)
    eb = sb("eb", [128, 4, 128], F32)
    enb = sb("enb", [128, 4, 128], F32)
    qt = sb("qt", [128, 4, 128], BF16)
    kt = sb("kt", [128, 4, 128], BF16)
    am = sb("am", [128, 4, 128], BF16)
    ktTs = sb("ktTs", [128, 512], BF16)
    on = sb("on", [128, 1024], BF16)
    ssm = sb("ssm", [128, 8], F32)
    rec = sb("rec", [128, 16], F32)
    sq = [sb("sq%d" % i, [128, TT], BF16) for i in range(3)]
    rstd = [sb("rstd%d" % i, [128, TT], F32) for i in range(2)]
    gt = [sq[0], sq[1]]
    junk = sq[2]
    qt2 = [qt, sb("qtB", [128, 4, 128], BF16)]
    am2 = [am, sb("amB", [128, 4, 128], BF16)]
    ktTs2 = [ktTs, sb("ktTsB", [128, 512], BF16)]
    ebl = sb("ebl", [128, 2, 4], F32)
    ftmp = sb("ftmp", [128, TT], F32)
    xin = [sb("xin%d" % i, [128, 1024], F32) for i in range(2)]
    pin = [sb("pin%d" % i, [128, 256], F32) for i in range(2)]
    pT = sb("pT", [128, 2, TT], BF16)
    cst = sb("cst", [128, 2, 128], F32)
    cstb = sb("cstb", [128, 6, 128], BF16)
    gvec = sb("gvec", [128, DEPTH * NG], F32)
    gq = sb("gq", [128, DEPTH], F32)
    gA = sb("gA", [128, DEPTH, 256], F32)
    cbt = sb("cbt", [128, DEPTH, 16], F32)
    wgu = sb("wgu", [32, DEPTH, 512], BF16)
    raT = sb("raT", [32, TT], BF16)
    pb = [nc.alloc_psum_tensor("pb%d" % i, [128, 512], F32) for i in range(8)]
    pbv = [b.bitcast(BF16) for b in pb]
    UaF = Ua[:].rearrange("p a b -> p (a b)").bitcast(F32)
    UbF = Ub[:].rearrange("p a b -> p (a b)").bitcast(F32)
    def xstage(tb):
        v = UaF if tb < 2 else UbF
        return v[:, (tb % 2) * 1024:(tb % 2 + 1) * 1024], [(("Ua" if tb < 2 else "Ub"), (tb % 2) * 4 + k) for k in range(4)]

    identf = cst[:, 0, :]
    tri = cst[:, 1, :]
    identb = cstb[:, 0, :]
    Jb = cstb[:, 1, :]
    cmaskb = cstb[:, 2, :]
    bonesb = cstb[:, 4, :]
    onesb = cstb[:, 5, :]

    st = {"bank": 0, "ws": 0, "ev": 0, "fb": 0, "mb": 0, "split": False, "kind": "mix"}
    out_sems = []

    def nb(kind=None):
        if st["split"]:
            if (kind or st["kind"]) == "fill":
                b = 5 + st["fb"] % 3
                st["fb"] += 1
            else:
                b = st["mb"] % 5
                st["mb"] += 1
            return b
        b = st["bank"]
        st["bank"] = (b + 1) % 8
        return b

    def interleave(gen, fillers, n_yields, name="", front=0):
        st["split"] = True
        P.cur_tag = "mix:" + name
        nf = len(fillers)
        done = 0
        y = 0
        for _ in gen:
            y += 1
            if y <= front:
                want = min(nf, y)
            else:
                want = min(nf, max(front, front + ((y - front) * (nf - front) + (n_yields - front) - 1) // max(1, n_yields - front)))
            while done < want:
                st["kind"] = "fill"
                P.cur_tag = "fill:" + name
                fillers[done]()
                P.cur_tag = "mix:" + name
                st["kind"] = "mix"
                done += 1
        while done < nf:
            st["kind"] = "fill"
            P.cur_tag = "fill:" + name
            fillers[done]()
            st["kind"] = "mix"
            done += 1
        st["split"] = False

    def evac_eng():
        st["ev"] += 1
        return "act" if st["ev"] % 2 == 0 else "dve"

    def copy_op(eng, out, in_, reads, writes, scale=None):
        if eng == "act":
            if scale is None:
                P.op("act", lambda e: e.activation(out=out, in_=in_, func=AF.Copy), reads, writes)
            else:
                P.op("act", lambda e: e.activation(out=out, in_=in_, func=AF.Copy, scale=scale), reads, writes)
        else:
            if scale is None:
                P.op("dve", lambda e: e.tensor_copy(out=out, in_=in_), reads, writes)
            else:
                P.op("dve", lambda e: e.tensor_scalar(out=out, in0=in_, scalar1=scale, scalar2=None, op0=ALU.mult), reads, writes)

    def mm(out, lhsT, rhs, start, stop, reads, writes):
        o = P.op("pe", lambda e: e.matmul(out, lhsT=lhsT, rhs=rhs, start=start, stop=stop, skip_group_check=True), reads, writes)
        o.meta = int(np.prod(rhs.shape[1:]))

    def tp(out, in_, ident, reads, writes):
        o = P.op("pe", lambda e: e.transpose(out, in_, ident), reads, writes)
        o.meta = int(np.prod(ident.shape[1:]))

    wcache = {}

    def wload(w2d, r0, nk, c0, ncols):
        s = st["ws"] % NW
        st["ws"] += 1
        dst = W[s][:, 0:nk, 0:ncols]
        key = (w2d.name, int(w2d.offset), r0, nk, c0, ncols)
        if key not in wcache:
            src = w2d[r0:r0 + nk * 128, c0:c0 + ncols].rearrange("(k p) c -> p k c", p=128)
            P.op("pool", lambda e: e.dma_start(out=dst, in_=src), writes=[("W", s)], dma_sem=("w", s))
            sc = nc.dram_tensor("wsc%d" % len(wcache), [128, nk * ncols], BF16, kind="Internal").ap()
            wcache[key] = sc
            P.op("sp", lambda e: e.dma_start(out=sc.rearrange("p (k c) -> p k c", k=nk), in_=dst),
                 reads=[("W", s)], writes=[("wsc", key)], dma_sem=("wst", s))
        else:
            sc = wcache[key]
            P.op("pool", lambda e: e.dma_start(out=dst, in_=sc.rearrange("p (k c) -> p k c", k=nk)),
                 reads=[("wsc", key)], writes=[("W", s)], dma_sem=("w", s))
        return s

    def rstd_from_psum(bank, i, ntok, n):
        P.op("act", lambda e: e.activation(out=rstd[i][:, 0:ntok], in_=pb[bank][:, 0:ntok], func=AF.Ln, scale=1.0 / n, bias=EPS),
             reads=[("pb", bank)], writes=[("rstd", i)])
        P.op("act", lambda e: e.activation(out=rstd[i][:, 0:ntok], in_=rstd[i][:, 0:ntok], func=AF.Exp, scale=-0.5),
             reads=[("rstd", i)], writes=[("rstd", i)])

    P.op("sp", lambda e: e.dma_start(out=cst[:, 0, :], in_=cst_d[:, 0, :]), writes=["init_sp"], dma_sem="init_sp")
    P.op("sp", lambda e: e.dma_start(out=cst[:, 1, :], in_=cst_d[:, 3, :]), writes=["init_sp"], dma_sem="init_sp")
    P.op("sp", lambda e: e.dma_start(out=gvec[:], in_=gv_d), writes=["init_sp"], dma_sem="init_sp")
    P.op("sp", lambda e: e.dma_start(out=gA[:], in_=gA_d), writes=["init_sp"], dma_sem="init_sp")
    P.op("sp", lambda e: e.dma_start(out=cbt[:], in_=cb_d), writes=["init_sp"], dma_sem="init_sp")
    P.op("pool", lambda e: e.dma_start(out=cstb[:], in_=cst_d), writes=["init_pool"], dma_sem="init_pool")
    P.op("pool", lambda e: e.dma_start(out=wgu[0:17, :, :], in_=wgu_d.rearrange("l k c -> k l c")), writes=["init_pool"], dma_sem="init_pool")
    P.op("dve", lambda e: e.memset(raT[:], 1.0), writes=["raT"])
    P.op("dve", lambda e: e.memset(Vcur[:], 1.0), writes=[("Vcur", tb, g) for tb in range(4) for g in range(4)])
    P.op("dve", lambda e: e.memset(Vprev[:], 1.0), writes=["Vprev"])
    P.op("dve", lambda e: e.memset(Qz[:], 0.0), writes=[("Qz", hp) for hp in range(8)])
    for k in range(2):
        for i in range(5):
            P.op("dve", lambda e, i=i, k=k: e.memset(PT[k][i][:], 0.0), writes=[("PT", k, i)])
    for l in range(DEPTH):
        P.op("dve", lambda e, l=l: e.memset(S[l][:], 0.0), writes=[("S", l)])
        P.op("dve", lambda e, l=l: e.memset(Sbf[l][:], 0.0), writes=[("Sbf", l)])
        P.op("dve", lambda e, l=l: e.tensor_scalar(out=gq[:, l:l + 1], in0=gvec[:, l * NG + 16:l * NG + 17], scalar1=0.125, scalar2=None, op0=ALU.mult),
             reads=["init_sp"], writes=["gq"])
        for C in range(2):
            stg = xin[C][:].rearrange("p (h i) -> p h i", h=8)
            for half in range(2):
                src = bass.AP(tabext, l * 16 * 384 + half * 8 * 384 + 1 + 128 * C, [[1, 128], [384, 8], [1, 128]])
                P.op("sp", lambda e, stg=stg, src=src: e.dma_start(out=stg, in_=src), writes=[("xin", C)], dma_sem=("ginit", C))
                cbb = cbt[:, l, half * 8:(half + 1) * 8].unsqueeze(2).broadcast_to([128, 8, 128])
                P.op("dve", lambda e, stg=stg, cbb=cbb, l=l, C=C, half=half: e.tensor_tensor(
                    out=Gp[:, l * 2 + C, half * 8:(half + 1) * 8, :], in0=stg, in1=cbb, op=ALU.subtract),
                    reads=[("xin", C), "init_sp"], writes=[("Gp", l)])

    def rmsnorm_to_hT(l, ntok, gcol0):
        bank = nb()
        for kc in range(8):
            if kc % 2 == 0:
                P.op("act", lambda e, kc=kc: e.activation(out=sq[kc % 2][:, 0:ntok], in_=xT[:, kc, 0:ntok], func=AF.Square),
                     reads=[("xT", kc)], writes=[("sq", kc % 2)])
            else:
                P.op("dve", lambda e, kc=kc: e.tensor_tensor(out=sq[kc % 2][:, 0:ntok], in0=xT[:, kc, 0:ntok], in1=xT[:, kc, 0:ntok], op=ALU.mult),
                     reads=[("xT", kc)], writes=[("sq", kc % 2)])
            mm(pb[bank][:, 0:ntok], onesb, sq[kc % 2][:, 0:ntok], kc == 0, kc == 7,
               reads=[("sq", kc % 2), "init_pool"], writes=[("pb", bank)])
        rstd_from_psum(bank, 0, ntok, D)
        for kc in range(8):
            P.op("dve", lambda e, kc=kc: e.scalar_tensor_tensor(
                out=hT[:, kc, 0:ntok], in0=xT[:, kc, 0:ntok], scalar=gvec[:, gcol0 + kc:gcol0 + kc + 1],
                in1=rstd[0][:, 0:ntok], op0=ALU.mult, op1=ALU.mult),
                reads=[("xT", kc), ("rstd", 0), "init_sp"], writes=[("hT", kc)])

    def formB(w2d, c0, nk, ncols, rhs_fn, rhs_res, evac):
        s = wload(w2d, 0, nk, c0, ncols)
        for cbi in range((ncols + 127) // 128):
            m = min(128, ncols - cbi * 128)
            bank = nb()
            for kc in range(nk):
                rhs = rhs_fn(kc)
                mm(pb[bank][0:m, 0:rhs.shape[-1]], W[s][:, kc, cbi * 128:cbi * 128 + m], rhs, kc == 0, kc == nk - 1,
                   reads=[("W", s), rhs_res(kc)], writes=[("pb", bank)])
            evac(bank, cbi)

    def formA(w2d, c0, ncols, ntok, evac):
        s = wload(w2d, 0, 8, c0, ncols)
        for tb in range((ntok + 127) // 128):
            nt_b = min(128, ntok - tb * 128)
            bank = nb()
            for kc in range(8):
                mm(pb[bank][0:nt_b, 0:ncols], hT[:, kc, tb * 128:tb * 128 + nt_b], W[s][:, kc, 0:ncols], kc == 0, kc == 7,
                   reads=[("W", s), ("hT", kc)], writes=[("pb", bank)])
            evac(bank, tb, nt_b)

    def formB_thunks(w2d, c0, nk, ncols, rhs_fn, rhs_res, evac):
        box = {}
        th = []
        ncb = (ncols + 127) // 128
        for cbi in range(ncb):
            def t(cbi=cbi):
                if cbi == 0:
                    box["s"] = wload(w2d, 0, nk, c0, ncols)
                s_ = box["s"]
                m = min(128, ncols - cbi * 128)
                bank = nb()
                for kc in range(nk):
                    rhs = rhs_fn(kc)
                    mm(pb[bank][0:m, 0:rhs.shape[-1]], W[s_][:, kc, cbi * 128:cbi * 128 + m], rhs, kc == 0, kc == nk - 1,
                       reads=[("W", s_), rhs_res(kc)], writes=[("pb", bank)])
                evac(bank, cbi)
            th.append(t)
        return th

    def formA_thunks(w2d, c0, ncols, ntok, evac):
        box = {}
        th = []
        for tb in range((ntok + 127) // 128):
            def t(tb=tb):
                if tb == 0:
                    box["s"] = wload(w2d, 0, 8, c0, ncols)
                s_ = box["s"]
                nt_b = min(128, ntok - tb * 128)
                bank = nb()
                for kc in range(8):
                    mm(pb[bank][0:nt_b, 0:ncols], hT[:, kc, tb * 128:tb * 128 + nt_b], W[s_][:, kc, 0:ncols], kc == 0, kc == 7,
                       reads=[("W", s_), ("hT", kc)], writes=[("pb", bank)])
                evac(bank, tb, nt_b)
            th.append(t)
        return th

    def tile_layer(l, ntok, tok0, xsrc, psrc, is_sample, t, last_tile, prefetched=False, nxt=None):
        nblk = (ntok + 127) // 128
        chunks = [(c * 128, min(128, ntok - c * 128)) for c in range(nblk)]
        w_in_l = w_in[l]
        hT_res = lambda kc: ("hT", kc)
        hT_rhs = lambda kc: hT[:, kc, 0:ntok]

        P.cur_tag = "pload"
        p_stage_xin = (not is_sample) and (not last_tile) and (prefetched or l > 0)
        for tb, (c0, nt) in enumerate(chunks):
            if tb < 2:
                P.op("sp", lambda e, tb=tb, c0=c0, nt=nt: e.dma_start(out=pin[tb % 2][0:nt, :], in_=psrc[l, tok0 + c0:tok0 + c0 + nt, :]),
                     writes=[("pin", tb % 2)], dma_sem=("pin", tb % 2))
            elif p_stage_xin:
                P.op("sp", lambda e, tb=tb, c0=c0, nt=nt: e.dma_start(out=xin[tb % 2][0:nt, 0:256], in_=psrc[l, tok0 + c0:tok0 + c0 + nt, :]),
                     writes=[("xin", tb % 2)], dma_sem=("pinx", tb % 2))
        P.cur_tag = "cache"
        have_prev = False
        if is_sample:
            have_prev = True
            P.op("sp", lambda e: e.dma_start(out=S[l][:], in_=sg[l].rearrange("h d v -> d h v")), writes=[("S", l)], dma_sem=("Sin", l))
            copy_op("act", Sbf[l][:], S[l][:], reads=[("S", l)], writes=[("Sbf", l)])
            P.op("pool", lambda e: e.dma_start(out=va[:], in_=ck[l].rearrange("(b p) c -> p b c", p=128)),
                 writes=[("va", tb, g) for tb in range(4) for g in range(4)], dma_sem="ckld")
            for tb in range(4):
                bank = nb()
                for hp in range(8):
                    tp(pbv[bank][:, hp * 128:(hp + 1) * 128], va[:, tb, hp * 128:(hp + 1) * 128], identb,
                       reads=[("va", tb, 0), "init_pool"], writes=[("pb", bank)])
                copy_op(evac_eng(), Kprev[:, :, tb * 128:(tb + 1) * 128], pbv[bank][:].rearrange("p (h t) -> p h t", h=8),
                        reads=[("pb", bank)], writes=["Kprev"])
            Uv = Ub[:].rearrange("p a b -> p (a b)").rearrange("p (t c) -> p t c", c=1024)
            P.op("pool", lambda e: e.dma_start(out=Uv, in_=cv[l].rearrange("(b p) c -> p b c", p=128)),
                 writes=[("Ub", kc) for kc in range(8)], dma_sem="cvld")
            for tb in range(4):
                copy_op("dve", Vprev[:, tb, :].rearrange("p (h e) -> p h e", e=66)[:, :, 0:64],
                        Uv[:, tb, :].rearrange("p (h e) -> p h e", e=64),
                        reads=[("Ub", kc) for kc in range(8)], writes=["Vprev"])
            P.op("sp", lambda e: e.dma_start(out=kbs[l, 0:448, :], in_=ck[l, 64:512, :]), dma_sem="kroll")
            P.op("sp", lambda e: e.dma_start(out=vbs[l, 0:448, :], in_=cv[l, 64:512, :]), dma_sem="vroll")
        elif t > 0:
            have_prev = True
            P.op("sp", lambda e: e.dma_start(out=Kprev[:].rearrange("p a b -> p (a b)"), in_=kspill[l]),
                 reads=[("kspill", l)], writes=["Kprev"], dma_sem="kprev")
            P.op("sp", lambda e: e.dma_start(out=Vprev[:].rearrange("p a b -> p (a b)"), in_=vspill[l]),
                 reads=[("vspill", l)], writes=["Vprev"], dma_sem="vprev")

        P.cur_tag = "xload"
        if l == 0:
            for tb, (c0, nt) in enumerate(chunks):
                if prefetched:
                    xsrc_sb, xres = xstage(tb)
                else:
                    P.op("pool", lambda e, tb=tb, c0=c0, nt=nt: e.dma_start(out=xin[tb % 2][0:nt, :], in_=xsrc[tok0 + c0:tok0 + c0 + nt, :]),
                         writes=[("xin", tb % 2)], dma_sem=("xin", tb % 2))
                    xsrc_sb, xres = xin[tb % 2], [("xin", tb % 2)]
                for g in range(2):
                    bank = nb()
                    for kk in range(4):
                        kc = g * 4 + kk
                        tp(pb[bank][:, kk * 128:kk * 128 + nt], xsrc_sb[0:nt, kc * 128:(kc + 1) * 128], identf[0:nt, 0:nt],
                           reads=xres + ["init_sp"], writes=[("pb", bank)])
                    copy_op("dve", xT[:, g * 4:(g + 1) * 4, c0:c0 + nt],
                            pb[bank][:].rearrange("p (k t) -> p k t", k=4)[:, :, 0:nt],
                            reads=[("pb", bank)], writes=[("xT", g * 4 + kk) for kk in range(4)])

        P.cur_tag = "norm1"
        rmsnorm_to_hT(l, ntok, l * NG)

        P.cur_tag = "inprojA"
        for g in range(2):
            def ev(bank, cbi, g=g):
                h = g * 2 + cbi
                copy_op("act", R1[:, h, 0:ntok], pb[bank][:, 0:ntok], reads=[("pb", bank)], writes=[("R1", h)], scale=128.0 ** -0.5)
            formB(w_in_l, OFF_QA + g * 256, 8, 256, hT_rhs, hT_res, ev)
        for g in range(2):
            def ev(bank, cbi, g=g):
                h = g * 2 + cbi
                copy_op("dve", R1[:, 4 + h, 0:ntok], pb[bank][:, 0:ntok], reads=[("pb", bank)], writes=[("R1", 4 + h)])
            formB(w_in_l, OFF_KA + g * 256, 8, 256, hT_rhs, hT_res, ev)

        def ev_ra(bank, cbi):
            copy_op("dve", raT[0:16, 0:ntok], pb[bank][0:16, 0:ntok], reads=[("pb", bank)], writes=["raT"])
        formB(w_in_l, OFF_RA, 8, 16, hT_rhs, hT_res, ev_ra)
        for g in range(4):
            def ev(bank, tb, nt_b, g=g):
                copy_op(evac_eng(), va[0:nt_b, tb, g * 256:(g + 1) * 256], pb[bank][0:nt_b, 0:256],
                        reads=[("pb", bank)], writes=[("va", tb, g)])
            formA(w_in_l, OFF_VA + g * 256, 256, ntok, ev)

        def gate_silu_group(c_off, g, Ux, Uname):
            def ev(bank, cbi):
                kc = g * 2 + cbi
                P.op("act", lambda e: e.activation(out=Ux[:, kc, 0:ntok], in_=pb[bank][:, 0:ntok], func=AF.Silu),
                     reads=[("pb", bank)], writes=[(Uname, kc)])
            formB(w_in_l, c_off + g * 256, 8, 256, hT_rhs, hT_res, ev)

        for g in range(4):
            gate_silu_group(OFF_GA, g, Ua, "Ua")

        def gla_gen():
            def prep(ci):
                c0, nt = chunks[ci]
                par = ci % 2
                mm(pb[0][0:nt, 0:512], raT[0:17, c0:c0 + nt], wgu[0:17, l, :], True, True,
                   reads=["raT", "init_pool"], writes=[("pb", 0)])
                P.op("act", lambda e: e.activation(out=e1[0:nt, :], in_=pb[0][0:nt, :], func=AF.Exp, scale=-1.0),
                     reads=[("pb", 0)], writes=["e1"])
                P.op("act", lambda e: e.activation(out=e1[0:nt, :], in_=e1[0:nt, :], func=AF.Ln, bias=1.0),
                     reads=["e1"], writes=["e1"])
                yield
                for h in range(4):
                    mm(pb[1][:, h * 128:h * 128 + nt], e1[0:nt, h * 128:(h + 1) * 128], tri[0:nt, 0:nt], True, True,
                       reads=["e1", "init_sp"], writes=[("pb", 1)])
                bB3 = pb[1][:].rearrange("p (h t) -> p h t", h=4)[:, :, 0:nt]
                P.op("act", lambda e: e.activation(out=eb[:, :, 0:nt], in_=bB3, func=AF.Exp), reads=[("pb", 1)], writes=["eb"])
                P.op("act", lambda e: e.activation(out=enb[:, :, 0:nt], in_=bB3, func=AF.Exp, scale=-1.0), reads=[("pb", 1)], writes=["enb"])
                P.op("dve", lambda e: e.tensor_copy(out=ebl[:, par, :].unsqueeze(2), in_=eb[:, :, nt - 1:nt]),
                     reads=["eb"], writes=[("ebl", par)])
                P.op("dve", lambda e: e.tensor_tensor(out=qt2[par][:, :, 0:nt], in0=R1[:, 0:4, c0:c0 + nt], in1=eb[:, :, 0:nt], op=ALU.mult),
                     reads=["eb"] + [("R1", h) for h in range(4)], writes=[("qt", par)])
                P.op("dve", lambda e: e.tensor_tensor(out=kt[:, :, 0:nt], in0=R1[:, 4:8, c0:c0 + nt], in1=enb[:, :, 0:nt], op=ALU.mult),
                     reads=["enb"] + [("R1", 4 + h) for h in range(4)], writes=["kt"])
                yield
                for h in range(4):
                    mm(pb[0][0:nt, h * 128:h * 128 + nt], kt[:, h, 0:nt], qt2[par][:, h, 0:nt], True, True,
                       reads=["kt", ("qt", par)], writes=[("pb", 0)])
                for h in range(4):
                    tp(pbv[1][0:nt, h * 128:(h + 1) * 128], kt[:, h, 0:nt], identb, reads=["kt", "init_pool"], writes=[("pb", 1)])
                cm = cmaskb[0:nt, 0:nt].unsqueeze(1).broadcast_to([nt, 4, nt])
                P.op("dve", lambda e: e.tensor_tensor(
                    out=am2[par][0:nt, :, 0:nt], in0=pb[0][0:nt, :].rearrange("p (h t) -> p h t", h=4)[:, :, 0:nt], in1=cm, op=ALU.mult),
                    reads=[("pb", 0), "init_pool"], writes=[("am", par)])
                copy_op("act", ktTs2[par][0:nt, :], pbv[1][0:nt, 0:512], reads=[("pb", 1)], writes=[("ktTs", par)])
                yield

            def seq(ci):
                c0, nt = chunks[ci]
                par = ci % 2
                bO = [2, 3]
                for h in range(4):
                    bank = bO[h // 2]
                    reg = pb[bank][0:nt, (h % 2) * 256:(h % 2 + 1) * 256]
                    mm(reg, am2[par][0:nt, h, 0:nt], va[0:nt, ci, h * 256:(h + 1) * 256], True, False,
                       reads=[("am", par), ("va", ci, h)], writes=[("pb", bank)])
                    mm(reg, qt2[par][:, h, 0:nt], Sbf[l][:, h, :], False, True, reads=[("qt", par), ("Sbf", l)], writes=[("pb", bank)])
                yield
                for hh in range(2):
                    for h in (hh * 2, hh * 2 + 1):
                        mm(pb[4][:, (h % 2) * 256:(h % 2 + 1) * 256], ktTs2[par][0:nt, h * 128:(h + 1) * 128], va[0:nt, ci, h * 256:(h + 1) * 256], True, True,
                           reads=[("ktTs", par), ("va", ci, h)], writes=[("pb", 4)])
                    Sv = S[l][:, hh * 2:(hh + 1) * 2, :].rearrange("p h v -> p (h v)")
                    P.op("dve", lambda e, Sv=Sv: e.tensor_tensor(out=Sv, in0=pb[4][:, :], in1=Sv, op=ALU.add),
                         reads=[("pb", 4), ("S", l)], writes=[("S", l)])
                eblb = ebl[:, par, :].unsqueeze(2).broadcast_to([128, 4, 256])
                for h in range(4):
                    P.op("act", lambda e, h=h: e.activation(out=Sbf[l][:, h, :], in_=S[l][:, h, :], func=AF.Copy, scale=ebl[:, par, h:h + 1]),
                         reads=[("S", l), ("ebl", par)], writes=[("Sbf", l)])
                P.op("dve", lambda e: e.tensor_tensor(out=S[l][:], in0=S[l][:], in1=eblb, op=ALU.mult),
                     reads=[("ebl", par), ("S", l)], writes=[("S", l)])
                yield
                for h in range(4):
                    bank = bO[h // 2]
                    P.op("act", lambda e, bank=bank, h=h: e.activation(
                        out=junk[0:nt, 0:256], in_=pb[bank][0:nt, (h % 2) * 256:(h % 2 + 1) * 256], func=AF.Square, accum_out=ssm[0:nt, h:h + 1]),
                        reads=[("pb", bank)], writes=["ssm", ("sq", 2)])
                P.op("act", lambda e: e.activation(out=ssm[0:nt, 4:8], in_=ssm[0:nt, 0:4], func=AF.Ln, scale=1.0 / 256, bias=EPS),
                     reads=["ssm"], writes=["ssr"])
                P.op("act", lambda e: e.activation(out=ssm[0:nt, 4:8], in_=ssm[0:nt, 4:8], func=AF.Exp, scale=-0.5),
                     reads=["ssr"], writes=["ssr"])
                for h in range(4):
                    bank = bO[h // 2]
                    P.op("dve", lambda e, bank=bank, h=h: e.scalar_tensor_tensor(
                        out=on[0:nt, h * 256:(h + 1) * 256], in0=pb[bank][0:nt, (h % 2) * 256:(h % 2 + 1) * 256],
                        scalar=ssm[0:nt, 4 + h:5 + h], in1=gA[0:nt, l, :], op0=ALU.mult, op1=ALU.mult),
                        reads=[("pb", bank), "ssr", "init_sp"], writes=["on"])
                yield
                for blk in range(8):
                    tp(pbv[4][:, blk * 128:blk * 128 + nt], on[0:nt, blk * 128:(blk + 1) * 128], identb[0:nt, 0:nt],
                       reads=["on", "init_pool"], writes=[("pb", 4)])
                P.op("dve", lambda e: e.tensor_tensor(
                    out=Ua[:, :, c0:c0 + nt], in0=pbv[4][:].rearrange("p (b t) -> p b t", b=8)[:, :, 0:nt], in1=Ua[:, :, c0:c0 + nt], op=ALU.mult),
                    reads=[("pb", 4)] + [("Ua", kc) for kc in range(8)], writes=[("Ua", kc) for kc in range(8)])
                yield

            n = len(chunks)
            for _ in prep(0):
                yield
            for ci in range(n):
                sg_ = seq(ci)
                pg_ = prep(ci + 1) if ci + 1 < n else None
                for ch in GLA_PAT:
                    if ch == "S":
                        next(sg_)
                        yield
                    elif pg_ is not None:
                        next(pg_)
                        yield

        def qk_flush(keep=0):
            q = st.setdefault("qk_q", [])
            while len(q) > keep:
                q.pop(0)()

        def qk_group(c_off, g, is_q):
            def ev(bank, cbi):
                hp = g * 2 + cbi
                st["qk_n"] = st.get("qk_n", 0) + 1
                i = st["qk_n"] % 3
                P.op("act", lambda e: e.activation(out=sq[i][:, 0:ntok], in_=pb[bank][:, 0:ntok], func=AF.Square),
                     reads=[("pb", bank)], writes=[("sq", i)])
                qk_flush(QK_DEPTH - 1)

                def finish():
                    b2 = nb()
                    mm(pb[b2][:, 0:ntok], bonesb, sq[i][:, 0:ntok], True, True, reads=[("sq", i), "init_pool"], writes=[("pb", b2)])
                    ri = i % 2
                    rstd_from_psum(b2, ri, ntok, 64)
                    if is_q:
                        for par in range(2):
                            ps_ = slice(par * 64, (par + 1) * 64)
                            P.op("dve", lambda e, ps_=ps_, par=par: e.scalar_tensor_tensor(
                                out=Qz[ps_, hp, par, 0:ntok], in0=pb[bank][ps_, 0:ntok], scalar=gq[ps_, l:l + 1],
                                in1=rstd[ri][ps_, 0:ntok], op0=ALU.mult, op1=ALU.mult),
                                reads=[("pb", bank), ("rstd", ri), "gq"], writes=[("Qz", hp)])
                    else:
                        P.op("dve", lambda e: e.scalar_tensor_tensor(
                            out=Kcur[:, hp, 0:ntok], in0=pb[bank][:, 0:ntok], scalar=gvec[:, l * NG + 17:l * NG + 18],
                            in1=rstd[ri][:, 0:ntok], op0=ALU.mult, op1=ALU.mult),
                            reads=[("pb", bank), ("rstd", ri), "init_sp"], writes=[("Kcur", hp)])
                st["qk_q"].append(finish)
            formB(w_in_l, c_off + g * 256, 8, 256, hT_rhs, hT_res, ev)

        def vb_group(g):
            def ev(bank, tb, nt_b):
                copy_op(evac_eng(), Vcur[0:nt_b, tb, :].rearrange("p (h e) -> p h e", e=66)[:, g * 4:(g + 1) * 4, 0:64],
                        pb[bank][0:nt_b, 0:256].rearrange("p (h e) -> p h e", e=64),
                        reads=[("pb", bank)], writes=[("Vcur", tb, g)])
            formA(w_in_l, OFF_VB + g * 256, 256, ntok, ev)

        def mga_group(g):
            def ev(bank, cbi):
                kc = g * 2 + cbi
                P.op("act", lambda e: e.activation(out=M[:, kc, 0:ntok], in_=pb[bank][:, 0:ntok], func=AF.Sigmoid),
                     reads=[("pb", bank)], writes=[("M", kc)])
            formB(w_in_l, OFF_MGA + g * 256, 8, 256, hT_rhs, hT_res, ev)

        def mgb_group(g):
            def ev(bank, cbi):
                kc = g * 2 + cbi
                P.op("act", lambda e: e.activation(out=M2[:, kc * TT:kc * TT + ntok], in_=pb[bank][:, 0:ntok], func=AF.Sigmoid),
                     reads=[("pb", bank)], writes=[("va", kc // 2, (kc % 2) * 2), ("va", kc // 2, (kc % 2) * 2 + 1)])
            formB(w_in_l, OFF_MGB + g * 256, 8, 256, hT_rhs, hT_res, ev)

        Ua_rhs = lambda kc: Ua[:, kc, 0:ntok]
        Ua_res = lambda kc: ("Ua", kc)
        Ub_rhs = lambda kc: Ub[:, kc, 0:ntok]
        Ub_res = lambda kc: ("Ub", kc)

        def bra_group(g):
            def ev(bank, cbi):
                kc = g * 2 + cbi
                P.op("dve", lambda e: e.tensor_tensor(out=M[:, kc, 0:ntok], in0=pb[bank][:, 0:ntok], in1=M[:, kc, 0:ntok], op=ALU.mult),
                     reads=[("pb", bank), ("M", kc)], writes=[("M", kc)])
            formB(wba[l], g * 256, 8, 256, Ua_rhs, Ua_res, ev)

        P.cur_tag = "inprojB"
        for g in range(4):
            qk_group(OFF_QB, g, True)
        for g in range(4):
            qk_group(OFF_KB, g, False)
        qk_flush()
        fill1 = []
        for g in range(4):
            def evv(bank, tb, nt_b, g=g):
                copy_op(evac_eng(), Vcur[0:nt_b, tb, :].rearrange("p (h e) -> p h e", e=66)[:, g * 4:(g + 1) * 4, 0:64],
                        pb[bank][0:nt_b, 0:256].rearrange("p (h e) -> p h e", e=64),
                        reads=[("pb", bank)], writes=[("Vcur", tb, g)])
            fill1.extend(formA_thunks(w_in_l, OFF_VB + g * 256, 256, ntok, evv))
        interleave(gla_gen(), fill1, 7 * nblk, "gla")
        qk_flush()

        P.cur_tag = "kvout"
        if is_sample or last_tile:
            dst = (sgs if is_sample else sgp)[l].rearrange("h d v -> d h v")
            key = ("Sout", l, is_sample)
            P.op("sp", lambda e, dst=dst: e.dma_start(out=dst, in_=S[l][:]), reads=[("S", l)], dma_sem=key)
            out_sems.append(key)
            if is_sample:
                P.op("dve", lambda e: e.memset(S[l][:], 0.0), writes=[("S", l)])
                P.op("dve", lambda e: e.memset(Sbf[l][:], 0.0), writes=[("Sbf", l)])

        Kcur_all = [("Kcur", hp) for hp in range(8)]
        Vcur_all = [("Vcur", tb, g) for tb in range(4) for g in range(4)]
        if (not is_sample) and (not last_tile):
            P.op("sp", lambda e: e.dma_start(out=kspill[l], in_=Kcur[:].rearrange("p a b -> p (a b)")),
                 reads=Kcur_all, writes=[("kspill", l)], dma_sem=("kst", l))
            P.op("sp", lambda e: e.dma_start(out=vspill[l], in_=Vcur[:].rearrange("p a b -> p (a b)")),
                 reads=Vcur_all, writes=[("vspill", l)], dma_sem=("vst", l))
        if is_sample or last_tile:
            kdst = kbs if is_sample else kbp
            vdst = vbs if is_sample else vbp
            r0 = 448 if is_sample else 0
            for tb, (c0, nt) in enumerate(chunks):
                bank = nb()
                for hp in range(8):
                    tp(pbv[bank][0:nt, hp * 128:(hp + 1) * 128], Kcur[:, hp, c0:c0 + nt], identb, reads=[("Kcur", hp), "init_pool"], writes=[("pb", bank)])
                copy_op("act", xin[0][0:nt, :], pbv[bank][0:nt, :], reads=[("pb", bank)], writes=[("xin", 0)])
                key = ("kout", l, is_sample)
                P.op("sp", lambda e, c0=c0, nt=nt: e.dma_start(out=kdst[l, r0 + c0:r0 + c0 + nt, :], in_=xin[0][0:nt, :]),
                     reads=[("xin", 0)], dma_sem=key)
                copy_op("dve", xin[1][0:nt, :].rearrange("p (h e) -> p h e", e=64),
                        Vcur[0:nt, tb, :].rearrange("p (h e) -> p h e", e=66)[:, :, 0:64],
                        reads=[("Vcur", tb, g) for g in range(4)], writes=[("xin", 1)])
                key2 = ("vout", l, is_sample)
                P.op("sp", lambda e, c0=c0, nt=nt: e.dma_start(out=vdst[l, r0 + c0:r0 + c0 + nt, :], in_=xin[1][0:nt, :]),
                     reads=[("xin", 1)], dma_sem=key2)
            out_sems.extend([("kout", l, is_sample), ("vout", l, is_sample)])

        def band_gen():
            for pi, (q0, nq) in enumerate(chunks):
                blocks = []
                for C in range(4, -1, -1):
                    b = pi - C
                    if b >= 0:
                        nk = min(128, ntok - b * 128)
                        blocks.append((C, Kcur, "cur", b, nk))
                    elif have_prev:
                        blocks.append((C, Kprev, "prev", 4 + b, 128))

                def qk_exp(hg):
                    for (C, Kb, which, b, nk) in blocks:
                        bank = nb()
                        Kres = (lambda hp: ("Kcur", hp)) if which == "cur" else (lambda hp: "Kprev")
                        if C <= 1:
                            mm(pb[bank][0:nk, 0:4 * nq], Jb[:, 0:nk], Gp[:, l * 2 + C, hg * 4:(hg + 1) * 4, 0:nq], True, False,
                               reads=[("Gp", l), "init_pool"], writes=[("pb", bank)])
                        for h2 in range(2):
                            hp = hg * 2 + h2
                            mm(pb[bank][0:nk, h2 * 2 * nq:(h2 + 1) * 2 * nq], Kb[:, hp, b * 128:b * 128 + nk], Qz[:, hp, :, q0:q0 + nq],
                               C > 1, h2 == 1, reads=[Kres(hp), ("Qz", hp)], writes=[("pb", bank)])
                        src3 = pb[bank][:, 0:4 * nq].rearrange("p (h q) -> p h q", h=4)
                        P.op("act", lambda e, src3=src3, C=C, hg=hg, nk=nk: e.activation(
                            out=PT[hg % 2][C][0:nk, :, 0:nq], in_=src3[0:nk, :, 0:nq], func=AF.Exp),
                            reads=[("pb", bank)], writes=[("PT", hg % 2, C)])
                        if nq == 128 and C == 4:
                            P.op("dve", lambda e, C=C, hg=hg: e.memset(PT[hg % 2][C][0:64, :, 64:128], 0.0), writes=[("PT", hg % 2, C)])
                        elif nq == 128 and C == 0:
                            P.op("dve", lambda e, C=C, hg=hg: e.memset(PT[hg % 2][C][64:128, :, 0:64], 0.0), writes=[("PT", hg % 2, C)])
                        yield

                def pv_norm(hg):
                    ob_bank = nb()
                    for hh in range(4):
                        h = hg * 4 + hh
                        for bi, (C, Kb, which, b, nk) in enumerate(blocks):
                            Vb = Vcur if which == "cur" else Vprev
                            vres = [("Vcur", b, h // 4)] if which == "cur" else ["Vprev"]
                            mm(pb[ob_bank][0:nq, hh * 66:(hh + 1) * 66], PT[hg % 2][C][0:nk, hh, 0:nq], Vb[0:nk, b, h * 66:(h + 1) * 66],
                               bi == 0, bi == len(blocks) - 1, reads=[("PT", hg % 2, C)] + vres, writes=[("pb", ob_bank)])
                    ob3 = pb[ob_bank][0:nq, 0:264].rearrange("p (h e) -> p h e", e=66)
                    P.op("dve", lambda e, ob3=ob3, hg=hg: e.reciprocal(out=rec[0:nq, hg * 4:(hg + 1) * 4].unsqueeze(2), in_=ob3[:, :, 64:65]),
                         reads=[("pb", ob_bank)], writes=["rec"])
                    recb = rec[0:nq, hg * 4:(hg + 1) * 4].unsqueeze(2).broadcast_to([nq, 4, 64])
                    P.op("dve", lambda e, ob3=ob3, hg=hg, recb=recb: e.tensor_tensor(
                        out=on[0:nq, hg * 256:(hg + 1) * 256].rearrange("p (h e) -> p h e", e=64), in0=ob3[:, :, 0:64], in1=recb, op=ALU.mult),
                        reads=[("pb", ob_bank), "rec"], writes=["on"])

                for hg in range(5):
                    if hg < 4:
                        yield from qk_exp(hg)
                    if hg >= 1:
                        pv_norm(hg - 1)
                        yield
                bI = nb()
                for blk in range(8):
                    tp(pbv[bI][:, blk * 128:blk * 128 + nq], on[0:nq, blk * 128:(blk + 1) * 128], identb[0:nq, 0:nq],
                       reads=["on", "init_pool"], writes=[("pb", bI)])
                P.op("dve", lambda e, bI=bI, q0=q0, nq=nq: e.tensor_tensor(
                    out=Ub[:, :, q0:q0 + nq], in0=pbv[bI][:].rearrange("p (b t) -> p b t", b=8)[:, :, 0:nq], in1=Ub[:, :, q0:q0 + nq], op=ALU.mult),
                    reads=[("pb", bI)] + [("Ub", kc) for kc in range(8)], writes=[("Ub", kc) for kc in range(8)])
                yield

        fill2 = []
        for g in range(4):
            def evg(bank, cbi, g=g):
                kc = g * 2 + cbi
                i = kc % 2
                P.op("act", lambda e: e.activation(out=gt[i][:, 0:ntok], in_=pb[bank][:, 0:ntok], func=AF.Tanh, scale=0.5),
                     reads=[("pb", bank)], writes=[("sq", i)])
                P.op("dve", lambda e: e.scalar_tensor_tensor(out=Ub[:, kc, 0:ntok], in0=gt[i][:, 0:ntok], scalar=1.0, in1=pb[bank][:, 0:ntok],
                                                             op0=ALU.add, op1=ALU.mult),
                     reads=[("sq", i), ("pb", bank)], writes=[("Ub", kc)])
            fill2.extend(formB_thunks(w_in_l, OFF_GB + g * 256, 8, 256, hT_rhs, hT_res, evg))
        n_front = len(fill2)
        for g in range(4):
            def evm(bank, cbi, g=g):
                kc = g * 2 + cbi
                P.op("act", lambda e: e.activation(out=M[:, kc, 0:ntok], in_=pb[bank][:, 0:ntok], func=AF.Tanh, scale=0.5),
                     reads=[("pb", bank)], writes=[("M", kc)])
            fill2.extend(formB_thunks(w_in_l, OFF_MGA + g * 256, 8, 256, hT_rhs, hT_res, evm))
        for g in range(4):
            def evb(bank, cbi, g=g):
                kc = g * 2 + cbi
                P.op("act", lambda e: e.activation(out=M2[:, kc * TT:kc * TT + ntok], in_=pb[bank][:, 0:ntok], func=AF.Tanh, scale=0.5),
                     reads=[("pb", bank)], writes=[("va", kc // 2, (kc % 2) * 2), ("va", kc // 2, (kc % 2) * 2 + 1)])
            fill2.extend(formB_thunks(w_in_l, OFF_MGB + g * 256, 8, 256, hT_rhs, hT_res, evb))
        for g in range(4):
            def eva(bank, cbi, g=g):
                kc = g * 2 + cbi
                P.op("dve", lambda e: e.scalar_tensor_tensor(out=M[:, kc, 0:ntok], in0=M[:, kc, 0:ntok], scalar=1.0, in1=pb[bank][:, 0:ntok],
                                                             op0=ALU.add, op1=ALU.mult),
                     reads=[("pb", bank), ("M", kc)], writes=[("M", kc)])
            fill2.extend(formB_thunks(wba[l], g * 256, 8, 256, Ua_rhs, Ua_res, eva))
        ny = 0
        for pi in range(nblk):
            nbk = sum(1 for C in range(5) if (pi - C >= 0) or have_prev)
            ny += 4 * (nbk + 1) + 1
        interleave(band_gen(), fill2, ny, "band", front=n_front)

        P.cur_tag = "branchB"
        for g in range(4):
            def ev(bank, cbi, g=g):
                kc = g * 2 + cbi
                P.op("dve", lambda e: e.scalar_tensor_tensor(out=ftmp[:, 0:ntok], in0=M2[:, kc * TT:kc * TT + ntok], scalar=1.0, in1=pb[bank][:, 0:ntok],
                                                             op0=ALU.add, op1=ALU.mult),
                     reads=[("pb", bank), ("va", kc // 2, (kc % 2) * 2), ("va", kc // 2, (kc % 2) * 2 + 1)], writes=["ftmp"])
                P.op("dve", lambda e: e.scalar_tensor_tensor(out=M[:, kc, 0:ntok], in0=ftmp[:, 0:ntok], scalar=0.5, in1=M[:, kc, 0:ntok],
                                                             op0=ALU.mult, op1=ALU.add),
                     reads=["ftmp", ("M", kc)], writes=[("M", kc)])
            formB(wbb[l], g * 256, 8, 256, Ub_rhs, Ub_res, ev)

        if nxt is not None and l == DEPTH - 1:
            nsrc, ntok0, nntok = nxt
            for tb in range((nntok + 127) // 128):
                nt2 = min(128, nntok - tb * 128)
                dstv, dres = xstage(tb)
                P.op("sp", lambda e, tb=tb, nt2=nt2, dstv=dstv: e.dma_start(out=dstv[0:nt2, :], in_=nsrc[ntok0 + tb * 128:ntok0 + tb * 128 + nt2, :]),
                     writes=dres, dma_sem=("xpre", tb))
        P.cur_tag = "pload"
        for tb, (c0, nt) in enumerate(chunks):
            psb, pres = pin[tb % 2], ("pin", tb % 2)
            if tb >= 2:
                if p_stage_xin:
                    psb, pres = xin[tb % 2], ("xin", tb % 2)
                else:
                    P.op("sp", lambda e, tb=tb, c0=c0, nt=nt: e.dma_start(out=pin[tb % 2][0:nt, :], in_=psrc[l, tok0 + c0:tok0 + c0 + nt, :]),
                         writes=[("pin", tb % 2)], dma_sem=("pin", tb % 2))
            bank = nb()
            for j in range(2):
                tp(pb[bank][:, j * 128:j * 128 + nt], psb[0:nt, j * 128:(j + 1) * 128], identf[0:nt, 0:nt],
                   reads=[pres, "init_sp"], writes=[("pb", bank)])
            copy_op("act", pT[:, 0:2, c0:c0 + nt], pb[bank][:, 0:256].rearrange("p (j t) -> p j t", j=2)[:, :, 0:nt],
                    reads=[("pb", bank)], writes=["pT"])
        P.cur_tag = "wout"
        M_rhs = lambda kc: M[:, kc, 0:ntok]
        M_res = lambda kc: ("M", kc)
        for g in range(4):
            def ev(bank, cbi, g=g):
                kc = g * 2 + cbi
                P.op("dve", lambda e: e.scalar_tensor_tensor(out=xT[:, kc, 0:ntok], in0=pb[bank][:, 0:ntok], scalar=0.5, in1=xT[:, kc, 0:ntok],
                                                             op0=ALU.mult, op1=ALU.add),
                     reads=[("pb", bank), ("xT", kc)], writes=[("xT", kc)])
            formB(wo[l], g * 256, 8, 256, M_rhs, M_res, ev)

        P.cur_tag = "ple"
        rmsnorm_to_hT(l, ntok, l * NG + 8)
        for g in range(4):
            def ev(bank, cbi, g=g):
                kc = g * 2 + cbi
                P.op("act", lambda e: e.activation(out=M[:, kc, 0:ntok], in_=pb[bank][:, 0:ntok], func=AF.Sigmoid),
                     reads=[("pb", bank)], writes=[("M", kc)])
            formB(wpg[l], g * 256, 8, 256, hT_rhs, hT_res, ev)
        pT_rhs = lambda kc: pT[:, kc, 0:ntok]
        pT_res = lambda kc: "pT"
        for g in range(4):
            def ev(bank, cbi, g=g):
                kc = g * 2 + cbi
                P.op("dve", lambda e: e.tensor_tensor(out=ftmp[:, 0:ntok], in0=pb[bank][:, 0:ntok], in1=M[:, kc, 0:ntok], op=ALU.mult),
                     reads=[("pb", bank), ("M", kc)], writes=["ftmp"])
                P.op("dve", lambda e: e.tensor_tensor(out=xT[:, kc, 0:ntok], in0=ftmp[:, 0:ntok], in1=xT[:, kc, 0:ntok], op=ALU.add),
                     reads=["ftmp", ("xT", kc)], writes=[("xT", kc)])
            formB(wp[l], g * 256, 2, 256, pT_rhs, pT_res, ev)

        P.cur_tag = "yout"
        if l == DEPTH - 1:
            ydst = ys if is_sample else yp
            for tb, (c0, nt) in enumerate(chunks):
                buf = tb % 2
                for g in range(2):
                    bank = nb()
                    for kk in range(4):
                        kc = g * 4 + kk
                        tp(pb[bank][0:nt, kk * 128:(kk + 1) * 128], xT[:, kc, c0:c0 + nt], identf, reads=[("xT", kc), "init_sp"], writes=[("pb", bank)])
                    copy_op("dve", xin[buf][0:nt, g * 512:(g + 1) * 512], pb[bank][0:nt, :], reads=[("pb", bank)], writes=[("xin", buf)])
                key = ("yout", buf, is_sample)
                P.op("sp", lambda e, c0=c0, nt=nt, buf=buf: e.dma_start(out=ydst[tok0 + c0:tok0 + c0 + nt, :], in_=xin[buf][0:nt, :]),
                     reads=[("xin", buf)], dma_sem=key)
                if key not in out_sems:
                    out_sems.append(key)

    tiles = [(TT, t * TT, xp, pp, False, t, t == NT - 1) for t in range(NT)]
    if do_sample:
        tiles.append((64, 0, xs, ps, True, 0, False))
    for ti, (ntok_, tok0_, xsrc_, psrc_, iss_, t_, last_) in enumerate(tiles):
        nxt = None
        if ti + 1 < len(tiles):
            n2 = tiles[ti + 1]
            nxt = (n2[2], n2[1], n2[0])
        for l in range(DEPTH):
            tile_layer(l, ntok_, tok0_, xsrc_, psrc_, iss_, t_, last_, prefetched=(ti > 0), nxt=nxt)
    if do_sample:
        out_sems.extend(["kroll", "vroll"])
    P.emit(final_wait_sems=out_sems)
    return nc


_CACHE = {}


def _consts():
    c = np.zeros((128, 6, 128), np.float32)
    c[:, 0, :] = np.eye(128)
    c[:, 1, :] = np.eye(128)[::-1]
    j = np.arange(128)[:, None]
    i = np.arange(128)[None, :]
    c[:, 2, :] = (j <= i)
    c[:, 3, :] = (j <= i) * (-1.0 / 16.0)
    blk = np.zeros((128, 128), np.float32)
    blk[0:64, 0:64] = 1
    blk[64:128, 64:128] = 1
    c[:, 4, :] = blk
    c[:, 5, :] = 1
    return c


def make_in_maps(inp, SEQ, DEPTH, n_cores=8):
    f = lambda a: np.ascontiguousarray(np.asarray(a, dtype=np.float32))
    gv = np.zeros((128, DEPTH * NG), np.float32)
    for l in range(DEPTH):
        gv[:, l * NG:l * NG + 8] = f(inp["norm_g"])[l].reshape(8, 128).T
        gv[:, l * NG + 8:l * NG + 16] = f(inp["ple_norm_g"])[l].reshape(8, 128).T
        gv[:, l * NG + 16] = np.tile(f(inp["q_norm_g"])[l], 2)
        gv[:, l * NG + 17] = np.tile(f(inp["k_norm_g"])[l], 2)
    gA = np.ascontiguousarray(np.broadcast_to(f(inp["gla_norm_g"])[None, :, :], (128, DEPTH, 256)))
    rb = f(inp["rel_bias"])
    tabext = np.ascontiguousarray(np.concatenate([rb, np.repeat(rb[..., -1:], 127, axis=-1)], axis=-1))
    cb = np.ascontiguousarray(np.broadcast_to(rb[None, :, :, 256], (128, DEPTH, 16)))
    wgu = np.ascontiguousarray(np.concatenate([f(inp["w_gate_up"]), f(inp["b_gate"])[:, None, :]], axis=1))
    common = {
        "w_in": f(inp["w_in"]), "wgu": wgu, "wba": f(inp["w_branch_a"]), "wbb": f(inp["w_branch_b"]),
        "wo": f(inp["w_out"]), "wpg": f(inp["w_ple_gate"]), "wp": f(inp["w_ple"]),
        "gv": gv, "gA": gA, "tabext": tabext, "cb": cb, "cst": _consts(),
    }
    xp = f(inp["x_prompt"]); pp = f(inp["p_prompt"]); xs = f(inp["x_sample"]); ps = f(inp["p_sample"])
    sg = f(inp["state_gla"]); ck = f(inp["cache_band_k"]); cv = f(inp["cache_band_v"])
    maps = []
    for c in range(n_cores):
        b = c % xp.shape[0]
        m = dict(common)
        m["xp"] = np.ascontiguousarray(xp[b])
        m["pp"] = np.ascontiguousarray(pp[:, b])
        m["xs"] = np.ascontiguousarray(xs[c])
        m["ps"] = np.ascontiguousarray(ps[:, c])
        m["sg"] = np.ascontiguousarray(sg[:, c])
        m["ck"] = np.ascontiguousarray(ck[:, c].reshape(DEPTH, 512, 1024))
        m["cv"] = np.ascontiguousarray(cv[:, c].reshape(DEPTH, 512, 1024))
        maps.append(m)
    return maps


def run(inp, SEQ, DEPTH):
    key = (SEQ, DEPTH)
    if key not in _CACHE:
        _CACHE[key] = build_program(SEQ, DEPTH)
    nc = _CACHE[key]
    maps = make_in_maps(inp, SEQ, DEPTH)
    res = run_bass_kernel_spmd(nc, maps, core_ids=list(range(8)))
    r = res.results
    B = np.asarray(inp["x_prompt"]).shape[0]
    y_prompt = np.stack([r[b]["yp"] for b in range(B)])
    y_sample = np.stack([r[c]["ys"] for c in range(8)])
    sgp = np.stack([r[b]["sgp"] for b in range(B)], axis=1)
    kbp = np.stack([r[b]["kbp"] for b in range(B)], axis=1).reshape(DEPTH, B, 512, 16, 64)
    vbp = np.stack([r[b]["vbp"] for b in range(B)], axis=1).reshape(DEPTH, B, 512, 16, 64)
    sgs = np.stack([r[c]["sgs"] for c in range(8)], axis=1)
    kbs = np.stack([r[c]["kbs"] for c in range(8)], axis=1).reshape(DEPTH, 8, 512, 16, 64)
    vbs = np.stack([r[c]["vbs"] for c in range(8)], axis=1).reshape(DEPTH, 8, 512, 16, 64)
    return (y_prompt, y_sample, sgp, kbp, vbp, sgs, kbs, vbs)


def kernel(**inputs):
    return run(inputs, 4096, 2)
```

```python
import numpy as np
import concourse.bass as bass
import concourse.mybir as mybir
from concourse.bass_utils import run_bass_kernel_spmd

F32 = mybir.dt.float32
BF16 = mybir.dt.bfloat16
AF = mybir.ActivationFunctionType
ALU = mybir.AluOpType

D = 1024
N_IN = 9232
OFF_QA, OFF_KA, OFF_VA, OFF_RA, OFF_GA = 0, 512, 1024, 2048, 2064
OFF_QB, OFF_KB, OFF_VB, OFF_GB, OFF_MGA, OFF_MGB = 3088, 4112, 5136, 6160, 7184, 8208
EPS = 1e-6
TT = 512
QK_DEPTH = 1
import os
GLA_PAT = 'SPSSPSP'
NG = 18

ENGS = ("pe", "act", "dve", "pool", "sp")


class Op:
    __slots__ = ("eng", "fn", "deps", "inc", "count", "dma_sem", "dma_val", "tag", "meta")

    def __init__(self, eng, fn):
        self.eng = eng
        self.fn = fn
        self.deps = []
        self.inc = False
        self.count = 0
        self.dma_sem = None
        self.dma_val = 0


class Prog:
    def __init__(self, nc):
        self.nc = nc
        self.ops = {e: [] for e in ENGS}
        self.last_w = {}
        self.readers = {}
        self.dma_cnt = {}
        self.cur_tag = ""

    def op(self, eng, fn, reads=(), writes=(), dma_sem=None):
        o = Op(eng, fn)
        o.tag = self.cur_tag
        o.meta = None
        is_dma = dma_sem is not None
        deps = {}
        for r in reads:
            w = self.last_w.get(r)
            if w is not None:
                deps[id(w)] = (w, "raw")
        for r in writes:
            w = self.last_w.get(r)
            if w is not None and id(w) not in deps:
                deps[id(w)] = (w, "waw")
            for rd in self.readers.get(r, ()):
                if id(rd) not in deps:
                    deps[id(rd)] = (rd, "war")
        for d, kind in deps.values():
            d_is_dma = d.dma_sem is not None
            if d.eng == eng and not d_is_dma and not is_dma:
                if eng == "pe" or kind != "raw":
                    continue
            if not d_is_dma:
                d.inc = True
            o.deps.append(d)
        for r in reads:
            self.readers.setdefault(r, []).append(o)
        for r in writes:
            self.last_w[r] = o
            self.readers[r] = []
        if is_dma:
            o.dma_sem = dma_sem
            c = self.dma_cnt.get(dma_sem, 0) + 16
            self.dma_cnt[dma_sem] = c
            o.dma_val = c
        self.ops[eng].append(o)
        return o

    def emit(self, final_wait_sems=()):
        nc = self.nc
        esem = {e: nc.alloc_semaphore("es_" + e) for e in ENGS}
        dsem = {}
        for i, k in enumerate(self.dma_cnt):
            dsem[k] = nc.alloc_semaphore("ds%d" % i)
        for e in ENGS:
            c = 0
            for o in self.ops[e]:
                if o.dma_sem is None and o.inc:
                    c += 1
                    o.count = c
        ops = self.ops
        dma_cnt = self.dma_cnt

        def run(e, eng):
            known = {}
            for o in ops[e]:
                need = {}
                for d in o.deps:
                    if d.dma_sem is not None:
                        key = ("d", d.dma_sem)
                        val = d.dma_val
                    else:
                        key = ("e", d.eng)
                        val = d.count
                    if val > need.get(key, 0):
                        need[key] = val
                for key, val in need.items():
                    if known.get(key, 0) >= val:
                        continue
                    known[key] = val
                    sem = dsem[key[1]] if key[0] == "d" else esem[key[1]]
                    eng.wait_ge(sem, val)
                ins = o.fn(eng)
                if o.dma_sem is not None:
                    ins.then_inc(dsem[o.dma_sem], 16)
                elif o.inc:
                    ins.then_inc(esem[e], 1)
            if e == "sp":
                for k in final_wait_sems:
                    eng.wait_ge(dsem[k], dma_cnt[k])

        with nc.Block() as block:
            @block.tensor
            def _(eng):
                run("pe", eng)

            @block.scalar
            def _(eng):
                run("act", eng)

            @block.vector
            def _(eng):
                run("dve", eng)

            @block.gpsimd
            def _(eng):
                run("pool", eng)

            @block.sync
            def _(eng):
                run("sp", eng)


def build_program(SEQ, DEPTH, do_sample=True):
    nc = bass.Bass("TRN2", target_bir_lowering=False)
    P = Prog(nc)
    NT = SEQ // TT

    def din(name, shape):
        return nc.dram_tensor(name, list(shape), F32, kind="ExternalInput").ap()

    def dout(name, shape):
        return nc.dram_tensor(name, list(shape), F32, kind="ExternalOutput").ap()

    xp = din("xp", [SEQ, D]); pp = din("pp", [DEPTH, SEQ, 256])
    xs = din("xs", [64, D]); ps = din("ps", [DEPTH, 64, 256])
    sg = din("sg", [DEPTH, 4, 128, 256])
    ck = din("ck", [DEPTH, 512, D]); cv = din("cv", [DEPTH, 512, D])
    w_in = din("w_in", [DEPTH, D, N_IN])
    wgu_d = din("wgu", [DEPTH, 17, 512])
    wba = din("wba", [DEPTH, D, D]); wbb = din("wbb", [DEPTH, D, D])
    wo = din("wo", [DEPTH, D, D]); wpg = din("wpg", [DEPTH, D, D])
    wp = din("wp", [DEPTH, 256, D])
    gv_d = din("gv", [128, DEPTH * NG])
    gA_d = din("gA", [128, DEPTH, 256])
    tabext = nc.dram_tensor("tabext", [DEPTH, 16, 384], F32, kind="ExternalInput")
    cb_d = din("cb", [128, DEPTH, 16])
    cst_d = din("cst", [128, 6, 128])

    yp = dout("yp", [SEQ, D]); ys = dout("ys", [64, D])
    sgp = dout("sgp", [DEPTH, 4, 128, 256])
    kbp = dout("kbp", [DEPTH, 512, D]); vbp = dout("vbp", [DEPTH, 512, D])
    sgs = dout("sgs", [DEPTH, 4, 128, 256])
    kbs = dout("kbs", [DEPTH, 512, D]); vbs = dout("vbs", [DEPTH, 512, D])

    kspill = [nc.dram_tensor("kspill%d" % l, [128, 8 * 512], BF16, kind="Internal").ap() for l in range(DEPTH)]
    vspill = [nc.dram_tensor("vspill%d" % l, [128, 4 * 1056], BF16, kind="Internal").ap() for l in range(DEPTH)]

    def sb(name, shape, dt):
        return nc.alloc_sbuf_tensor("s_" + name, list(shape), dt)

    xT = sb("xT", [128, 8, TT], F32)
    hT = sb("hT", [128, 8, TT], BF16)
    R1 = sb("R1", [128, 8, TT], BF16)
    Qz = sb("Qz", [128, 8, 2, TT], BF16)
    va = sb("va", [128, 4, 1024], BF16)
    M2 = va[:].rearrange("p a b -> p (a b)")
    Ua = sb("Ua", [128, 8, TT], BF16)
    Ub = sb("Ub", [128, 8, TT], BF16)
    M = sb("M", [128, 8, TT], BF16)
    Kcur = sb("Kcur", [128, 8, TT], BF16)
    Kprev = sb("Kprev", [128, 8, TT], BF16)
    Vcur = sb("Vcur", [128, 4, 1056], BF16)
    Vprev = sb("Vprev", [128, 4, 1056], BF16)
    S = [sb("S%d" % l, [128, 4, 256], F32) for l in range(DEPTH)]
    Sbf = [sb("Sbf%d" % l, [128, 4, 256], BF16) for l in range(DEPTH)]
    NW = 3
    W = [sb("W%d" % i, [128, 8, 256], BF16) for i in range(NW)]
    Gp = sb("Gp", [128, DEPTH * 2, 16, 128], BF16)
    PT = [[sb("PT%d_%d" % (i, k), [128, 4, 128], BF16) for i in range(5)] for k in range(2)]
    e1 = sb("e1", [128, 512], F32)
    eb = sb("eb", [128, 4, 128], F32)
    enb = sb("enb", [128, 4, 128], F32)
    qt = sb("qt", [128, 4, 128], BF16)
    kt = sb("kt", [128, 4, 128], BF16)
    am = sb("am", [128, 4, 128], BF16)
    ktTs = sb("ktTs", [128, 512], BF16)
    on = sb("on", [128, 1024], BF16)
    ssm = sb("ssm", [128, 8], F32)
    rec = sb("rec", [128, 16], F32)
    sq = [sb("sq%d" % i, [128, TT], BF16) for i in range(3)]
    rstd = [sb("rstd%d" % i, [128, TT], F32) for i in range(2)]
    gt = [sq[0], sq[1]]
    junk = sq[2]
    qt2 = [qt, sb("qtB", [128, 4, 128], BF16)]
    am2 = [am, sb("amB", [128, 4, 128], BF16)]
    ktTs2 = [ktTs, sb("ktTsB", [128, 512], BF16)]
    ebl = sb("ebl", [128, 2, 4], F32)
    ftmp = sb("ftmp", [128, TT], F32)
    xin = [sb("xin%d" % i, [128, 1024], F32) for i in range(2)]
    pin = [sb("pin%d" % i, [128, 256], F32) for i in range(2)]
    pT = sb("pT", [128, 2, TT], BF16)
    cst = sb("cst", [128, 2, 128], F32)
    cstb = sb("cstb", [128, 6, 128], BF16)
    gvec = sb("gvec", [128, DEPTH * NG], F32)
    gq = sb("gq", [128, DEPTH], F32)
    gA = sb("gA", [128, DEPTH, 256], F32)
    cbt = sb("cbt", [128, DEPTH, 16], F32)
    wgu = sb("wgu", [32, DEPTH, 512], BF16)
    raT = sb("raT", [32, TT], BF16)
    pb = [nc.alloc_psum_tensor("pb%d" % i, [128, 512], F32) for i in range(8)]
    pbv = [b.bitcast(BF16) for b in pb]
    UaF = Ua[:].rearrange("p a b -> p (a b)").bitcast(F32)
    UbF = Ub[:].rearrange("p a b -> p (a b)").bitcast(F32)
    def xstage(tb):
        v = UaF if tb < 2 else UbF
        return v[:, (tb % 2) * 1024:(tb % 2 + 1) * 1024], [(("Ua" if tb < 2 else "Ub"), (tb % 2) * 4 + k) for k in range(4)]

    identf = cst[:, 0, :]
    tri = cst[:, 1, :]
    identb = cstb[:, 0, :]
    Jb = cstb[:, 1, :]
    cmaskb = cstb[:, 2, :]
    bonesb = cstb[:, 4, :]
    onesb = cstb[:, 5, :]

    st = {"bank": 0, "ws": 0, "ev": 0, "fb": 0, "mb": 0, "split": False, "kind": "mix"}
    out_sems = []

    def nb(kind=None):
        if st["split"]:
            if (kind or st["kind"]) == "fill":
                b = 5 + st["fb"] % 3
                st["fb"] += 1
            else:
                b = st["mb"] % 5
                st["mb"] += 1
            return b
        b = st["bank"]
        st["bank"] = (b + 1) % 8
        return b

    def interleave(gen, fillers, n_yields, name="", front=0):
        st["split"] = True
        P.cur_tag = "mix:" + name
        nf = len(fillers)
        done = 0
        y = 0
        for _ in gen:
            y += 1
            if y <= front:
                want = min(nf, y)
            else:
                want = min(nf, max(front, front + ((y - front) * (nf - front) + (n_yields - front) - 1) // max(1, n_yields - front)))
            while done < want:
                st["kind"] = "fill"
                P.cur_tag = "fill:" + name
                fillers[done]()
                P.cur_tag = "mix:" + name
                st["kind"] = "mix"
                done += 1
        while done < nf:
            st["kind"] = "fill"
            P.cur_tag = "fill:" + name
            fillers[done]()
            st["kind"] = "mix"
            done += 1
        st["split"] = False

    def evac_eng():
        st["ev"] += 1
        return "act" if st["ev"] % 2 == 0 else "dve"

    def copy_op(eng, out, in_, reads, writes, scale=None):
        if eng == "act":
            if scale is None:
                P.op("act", lambda e: e.activation(out=out, in_=in_, func=AF.Copy), reads, writes)
            else:
                P.op("act", lambda e: e.activation(out=out, in_=in_, func=AF.Copy, scale=scale), reads, writes)
        else:
            if scale is None:
                P.op("dve", lambda e: e.tensor_copy(out=out, in_=in_), reads, writes)
            else:
                P.op("dve", lambda e: e.tensor_scalar(out=out, in0=in_, scalar1=scale, scalar2=None, op0=ALU.mult), reads, writes)

    def mm(out, lhsT, rhs, start, stop, reads, writes):
        o = P.op("pe", lambda e: e.matmul(out, lhsT=lhsT, rhs=rhs, start=start, stop=stop, skip_group_check=True), reads, writes)
        o.meta = int(np.prod(rhs.shape[1:]))

    def tp(out, in_, ident, reads, writes):
        o = P.op("pe", lambda e: e.transpose(out, in_, ident), reads, writes)
        o.meta = int(np.prod(ident.shape[1:]))

    wcache = {}

    def wload(w2d, r0, nk, c0, ncols):
        s = st["ws"] % NW
        st["ws"] += 1
        dst = W[s][:, 0:nk, 0:ncols]
        key = (w2d.name, int(w2d.offset), r0, nk, c0, ncols)
        if key not in wcache:
            src = w2d[r0:r0 + nk * 128, c0:c0 + ncols].rearrange("(k p) c -> p k c", p=128)
            P.op("pool", lambda e: e.dma_start(out=dst, in_=src), writes=[("W", s)], dma_sem=("w", s))
            sc = nc.dram_tensor("wsc%d" % len(wcache), [128, nk * ncols], BF16, kind="Internal").ap()
            wcache[key] = sc
            P.op("sp", lambda e: e.dma_start(out=sc.rearrange("p (k c) -> p k c", k=nk), in_=dst),
                 reads=[("W", s)], writes=[("wsc", key)], dma_sem=("wst", s))
        else:
            sc = wcache[key]
            P.op("pool", lambda e: e.dma_start(out=dst, in_=sc.rearrange("p (k c) -> p k c", k=nk)),
                 reads=[("wsc", key)], writes=[("W", s)], dma_sem=("w", s))
        return s

    def rstd_from_psum(bank, i, ntok, n):
        P.op("act", lambda e: e.activation(out=rstd[i][:, 0:ntok], in_=pb[bank][:, 0:ntok], func=AF.Ln, scale=1.0 / n, bias=EPS),
             reads=[("pb", bank)], writes=[("rstd", i)])
        P.op("act", lambda e: e.activation(out=rstd[i][:, 0:ntok], in_=rstd[i][:, 0:ntok], func=AF.Exp, scale=-0.5),
             reads=[("rstd", i)], writes=[("rstd", i)])

    P.op("sp", lambda e: e.dma_start(out=cst[:, 0, :], in_=cst_d[:, 0, :]), writes=["init_sp"], dma_sem="init_sp")
    P.op("sp", lambda e: e.dma_start(out=cst[:, 1, :], in_=cst_d[:, 3, :]), writes=["init_sp"], dma_sem="init_sp")
    P.op("sp", lambda e: e.dma_start(out=gvec[:], in_=gv_d), writes=["init_sp"], dma_sem="init_sp")
    P.op("sp", lambda e: e.dma_start(out=gA[:], in_=gA_d), writes=["init_sp"], dma_sem="init_sp")
    P.op("sp", lambda e: e.dma_start(out=cbt[:], in_=cb_d), writes=["init_sp"], dma_sem="init_sp")
    P.op("pool", lambda e: e.dma_start(out=cstb[:], in_=cst_d), writes=["init_pool"], dma_sem="init_pool")
    P.op("pool", lambda e: e.dma_start(out=wgu[0:17, :, :], in_=wgu_d.rearrange("l k c -> k l c")), writes=["init_pool"], dma_sem="init_pool")
    P.op("dve", lambda e: e.memset(raT[:], 1.0), writes=["raT"])
    P.op("dve", lambda e: e.memset(Vcur[:], 1.0), writes=[("Vcur", tb, g) for tb in range(4) for g in range(4)])
    P.op("dve", lambda e: e.memset(Vprev[:], 1.0), writes=["Vprev"])
    P.op("dve", lambda e: e.memset(Qz[:], 0.0), writes=[("Qz", hp) for hp in range(8)])
    for k in range(2):
        for i in range(5):
            P.op("dve", lambda e, i=i, k=k: e.memset(PT[k][i][:], 0.0), writes=[("PT", k, i)])
    for l in range(DEPTH):
        P.op("dve", lambda e, l=l: e.memset(S[l][:], 0.0), writes=[("S", l)])
        P.op("dve", lambda e, l=l: e.memset(Sbf[l][:], 0.0), writes=[("Sbf", l)])
        P.op("dve", lambda e, l=l: e.tensor_scalar(out=gq[:, l:l + 1], in0=gvec[:, l * NG + 16:l * NG + 17], scalar1=0.125, scalar2=None, op0=ALU.mult),
             reads=["init_sp"], writes=["gq"])
        for C in range(2):
            stg = xin[C][:].rearrange("p (h i) -> p h i", h=8)
            for half in range(2):
                src = bass.AP(tabext, l * 16 * 384 + half * 8 * 384 + 1 + 128 * C, [[1, 128], [384, 8], [1, 128]])
                P.op("sp", lambda e, stg=stg, src=src: e.dma_start(out=stg, in_=src), writes=[("xin", C)], dma_sem=("ginit", C))
                cbb = cbt[:, l, half * 8:(half + 1) * 8].unsqueeze(2).broadcast_to([128, 8, 128])
                P.op("dve", lambda e, stg=stg, cbb=cbb, l=l, C=C, half=half: e.tensor_tensor(
                    out=Gp[:, l * 2 + C, half * 8:(half + 1) * 8, :], in0=stg, in1=cbb, op=ALU.subtract),
                    reads=[("xin", C), "init_sp"], writes=[("Gp", l)])

    def rmsnorm_to_hT(l, ntok, gcol0):
        bank = nb()
        for kc in range(8):
            if kc % 2 == 0:
                P.op("act", lambda e, kc=kc: e.activation(out=sq[kc % 2][:, 0:ntok], in_=xT[:, kc, 0:ntok], func=AF.Square),
                     reads=[("xT", kc)], writes=[("sq", kc % 2)])
            else:
                P.op("dve", lambda e, kc=kc: e.tensor_tensor(out=sq[kc % 2][:, 0:ntok], in0=xT[:, kc, 0:ntok], in1=xT[:, kc, 0:ntok], op=ALU.mult),
                     reads=[("xT", kc)], writes=[("sq", kc % 2)])
            mm(pb[bank][:, 0:ntok], onesb, sq[kc % 2][:, 0:ntok], kc == 0, kc == 7,
               reads=[("sq", kc % 2), "init_pool"], writes=[("pb", bank)])
        rstd_from_psum(bank, 0, ntok, D)
        for kc in range(8):
            P.op("dve", lambda e, kc=kc: e.scalar_tensor_tensor(
                out=hT[:, kc, 0:ntok], in0=xT[:, kc, 0:ntok], scalar=gvec[:, gcol0 + kc:gcol0 + kc + 1],
                in1=rstd[0][:, 0:ntok], op0=ALU.mult, op1=ALU.mult),
                reads=[("xT", kc), ("rstd", 0), "init_sp"], writes=[("hT", kc)])

    def formB(w2d, c0, nk, ncols, rhs_fn, rhs_res, evac):
        s = wload(w2d, 0, nk, c0, ncols)
        for cbi in range((ncols + 127) // 128):
            m = min(128, ncols - cbi * 128)
            bank = nb()
            for kc in range(nk):
                rhs = rhs_fn(kc)
                mm(pb[bank][0:m, 0:rhs.shape[-1]], W[s][:, kc, cbi * 128:cbi * 128 + m], rhs, kc == 0, kc == nk - 1,
                   reads=[("W", s), rhs_res(kc)], writes=[("pb", bank)])
            evac(bank, cbi)

    def formA(w2d, c0, ncols, ntok, evac):
        s = wload(w2d, 0, 8, c0, ncols)
        for tb in range((ntok + 127) // 128):
            nt_b = min(128, ntok - tb * 128)
            bank = nb()
            for kc in range(8):
                mm(pb[bank][0:nt_b, 0:ncols], hT[:, kc, tb * 128:tb * 128 + nt_b], W[s][:, kc, 0:ncols], kc == 0, kc == 7,
                   reads=[("W", s), ("hT", kc)], writes=[("pb", bank)])
            evac(bank, tb, nt_b)

    def formB_thunks(w2d, c0, nk, ncols, rhs_fn, rhs_res, evac):
        box = {}
        th = []
        ncb = (ncols + 127) // 128
        for cbi in range(ncb):
            def t(cbi=cbi):
                if cbi == 0:
                    box["s"] = wload(w2d, 0, nk, c0, ncols)
                s_ = box["s"]
                m = min(128, ncols - cbi * 128)
                bank = nb()
                for kc in range(nk):
                    rhs = rhs_fn(kc)
                    mm(pb[bank][0:m, 0:rhs.shape[-1]], W[s_][:, kc, cbi * 128:cbi * 128 + m], rhs, kc == 0, kc == nk - 1,
                       reads=[("W", s_), rhs_res(kc)], writes=[("pb", bank)])
                evac(bank, cbi)
            th.append(t)
        return th

    def formA_thunks(w2d, c0, ncols, ntok, evac):
        box = {}
        th = []
        for tb in range((ntok + 127) // 128):
            def t(tb=tb):
                if tb == 0:
                    box["s"] = wload(w2d, 0, 8, c0, ncols)
                s_ = box["s"]
                nt_b = min(128, ntok - tb * 128)
                bank = nb()
                for kc in range(8):
                    mm(pb[bank][0:nt_b, 0:ncols], hT[:, kc, tb * 128:tb * 128 + nt_b], W[s_][:, kc, 0:ncols], kc == 0, kc == 7,
                       reads=[("W", s_), ("hT", kc)], writes=[("pb", bank)])
                evac(bank, tb, nt_b)
            th.append(t)
        return th

    def tile_layer(l, ntok, tok0, xsrc, psrc, is_sample, t, last_tile, prefetched=False, nxt=None):
        nblk = (ntok + 127) // 128
        chunks = [(c * 128, min(128, ntok - c * 128)) for c in range(nblk)]
        w_in_l = w_in[l]
        hT_res = lambda kc: ("hT", kc)
        hT_rhs = lambda kc: hT[:, kc, 0:ntok]

        P.cur_tag = "pload"
        p_stage_xin = (not is_sample) and (not last_tile) and (prefetched or l > 0)
        for tb, (c0, nt) in enumerate(chunks):
            if tb < 2:
                P.op("sp", lambda e, tb=tb, c0=c0, nt=nt: e.dma_start(out=pin[tb % 2][0:nt, :], in_=psrc[l, tok0 + c0:tok0 + c0 + nt, :]),
                     writes=[("pin", tb % 2)], dma_sem=("pin", tb % 2))
            elif p_stage_xin:
                P.op("sp", lambda e, tb=tb, c0=c0, nt=nt: e.dma_start(out=xin[tb % 2][0:nt, 0:256], in_=psrc[l, tok0 + c0:tok0 + c0 + nt, :]),
                     writes=[("xin", tb % 2)], dma_sem=("pinx", tb % 2))
        P.cur_tag = "cache"
        have_prev = False
        if is_sample:
            have_prev = True
            P.op("sp", lambda e: e.dma_start(out=S[l][:], in_=sg[l].rearrange("h d v -> d h v")), writes=[("S", l)], dma_sem=("Sin", l))
            copy_op("act", Sbf[l][:], S[l][:], reads=[("S", l)], writes=[("Sbf", l)])
            P.op("pool", lambda e: e.dma_start(out=va[:], in_=ck[l].rearrange("(b p) c -> p b c", p=128)),
                 writes=[("va", tb, g) for tb in range(4) for g in range(4)], dma_sem="ckld")
            for tb in range(4):
                bank = nb()
                for hp in range(8):
                    tp(pbv[bank][:, hp * 128:(hp + 1) * 128], va[:, tb, hp * 128:(hp + 1) * 128], identb,
                       reads=[("va", tb, 0), "init_pool"], writes=[("pb", bank)])
                copy_op(evac_eng(), Kprev[:, :, tb * 128:(tb + 1) * 128], pbv[bank][:].rearrange("p (h t) -> p h t", h=8),
                        reads=[("pb", bank)], writes=["Kprev"])
            Uv = Ub[:].rearrange("p a b -> p (a b)").rearrange("p (t c) -> p t c", c=1024)
            P.op("pool", lambda e: e.dma_start(out=Uv, in_=cv[l].rearrange("(b p) c -> p b c", p=128)),
                 writes=[("Ub", kc) for kc in range(8)], dma_sem="cvld")
            for tb in range(4):
                copy_op("dve", Vprev[:, tb, :].rearrange("p (h e) -> p h e", e=66)[:, :, 0:64],
                        Uv[:, tb, :].rearrange("p (h e) -> p h e", e=64),
                        reads=[("Ub", kc) for kc in range(8)], writes=["Vprev"])
            P.op("sp", lambda e: e.dma_start(out=kbs[l, 0:448, :], in_=ck[l, 64:512, :]), dma_sem="kroll")
            P.op("sp", lambda e: e.dma_start(out=vbs[l, 0:448, :], in_=cv[l, 64:512, :]), dma_sem="vroll")
        elif t > 0:
            have_prev = True
            P.op("sp", lambda e: e.dma_start(out=Kprev[:].rearrange("p a b -> p (a b)"), in_=kspill[l]),
                 reads=[("kspill", l)], writes=["Kprev"], dma_sem="kprev")
            P.op("sp", lambda e: e.dma_start(out=Vprev[:].rearrange("p a b -> p (a b)"), in_=vspill[l]),
                 reads=[("vspill", l)], writes=["Vprev"], dma_sem="vprev")

        P.cur_tag = "xload"
        if l == 0:
            for tb, (c0, nt) in enumerate(chunks):
                if prefetched:
                    xsrc_sb, xres = xstage(tb)
                else:
                    P.op("pool", lambda e, tb=tb, c0=c0, nt=nt: e.dma_start(out=xin[tb % 2][0:nt, :], in_=xsrc[tok0 + c0:tok0 + c0 + nt, :]),
                         writes=[("xin", tb % 2)], dma_sem=("xin", tb % 2))
                    xsrc_sb, xres = xin[tb % 2], [("xin", tb % 2)]
                for g in range(2):
                    bank = nb()
                    for kk in range(4):
                        kc = g * 4 + kk
                        tp(pb[bank][:, kk * 128:kk * 128 + nt], xsrc_sb[0:nt, kc * 128:(kc + 1) * 128], identf[0:nt, 0:nt],
                           reads=xres + ["init_sp"], writes=[("pb", bank)])
                    copy_op("dve", xT[:, g * 4:(g + 1) * 4, c0:c0 + nt],
                            pb[bank][:].rearrange("p (k t) -> p k t", k=4)[:, :, 0:nt],
                            reads=[("pb", bank)], writes=[("xT", g * 4 + kk) for kk in range(4)])

        P.cur_tag = "norm1"
        rmsnorm_to_hT(l, ntok, l * NG)

        P.cur_tag = "inprojA"
        for g in range(2):
            def ev(bank, cbi, g=g):
                h = g * 2 + cbi
                copy_op("act", R1[:, h, 0:ntok], pb[bank][:, 0:ntok], reads=[("pb", bank)], writes=[("R1", h)], scale=128.0 ** -0.5)
            formB(w_in_l, OFF_QA + g * 256, 8, 256, hT_rhs, hT_res, ev)
        for g in range(2):
            def ev(bank, cbi, g=g):
                h = g * 2 + cbi
                copy_op("dve", R1[:, 4 + h, 0:ntok], pb[bank][:, 0:ntok], reads=[("pb", bank)], writes=[("R1", 4 + h)])
            formB(w_in_l, OFF_KA + g * 256, 8, 256, hT_rhs, hT_res, ev)

        def ev_ra(bank, cbi):
            copy_op("dve", raT[0:16, 0:ntok], pb[bank][0:16, 0:ntok], reads=[("pb", bank)], writes=["raT"])
        formB(w_in_l, OFF_RA, 8, 16, hT_rhs, hT_res, ev_ra)
        for g in range(4):
            def ev(bank, tb, nt_b, g=g):
                copy_op(evac_eng(), va[0:nt_b, tb, g * 256:(g + 1) * 256], pb[bank][0:nt_b, 0:256],
                        reads=[("pb", bank)], writes=[("va", tb, g)])
            formA(w_in_l, OFF_VA + g * 256, 256, ntok, ev)

        def gate_silu_group(c_off, g, Ux, Uname):
            def ev(bank, cbi):
                kc = g * 2 + cbi
                P.op("act", lambda e: e.activation(out=Ux[:, kc, 0:ntok], in_=pb[bank][:, 0:ntok], func=AF.Silu),
                     reads=[("pb", bank)], writes=[(Uname, kc)])
            formB(w_in_l, c_off + g * 256, 8, 256, hT_rhs, hT_res, ev)

        for g in range(4):
            gate_silu_group(OFF_GA, g, Ua, "Ua")

        def gla_gen():
            def prep(ci):
                c0, nt = chunks[ci]
                par = ci % 2
                mm(pb[0][0:nt, 0:512], raT[0:17, c0:c0 + nt], wgu[0:17, l, :], True, True,
                   reads=["raT", "init_pool"], writes=[("pb", 0)])
                P.op("act", lambda e: e.activation(out=e1[0:nt, :], in_=pb[0][0:nt, :], func=AF.Exp, scale=-1.0),
                     reads=[("pb", 0)], writes=["e1"])
                P.op("act", lambda e: e.activation(out=e1[0:nt, :], in_=e1[0:nt, :], func=AF.Ln, bias=1.0),
                     reads=["e1"], writes=["e1"])
                yield
                for h in range(4):
                    mm(pb[1][:, h * 128:h * 128 + nt], e1[0:nt, h * 128:(h + 1) * 128], tri[0:nt, 0:nt], True, True,
                       reads=["e1", "init_sp"], writes=[("pb", 1)])
                bB3 = pb[1][:].rearrange("p (h t) -> p h t", h=4)[:, :, 0:nt]
                P.op("act", lambda e: e.activation(out=eb[:, :, 0:nt], in_=bB3, func=AF.Exp), reads=[("pb", 1)], writes=["eb"])
                P.op("act", lambda e: e.activation(out=enb[:, :, 0:nt], in_=bB3, func=AF.Exp, scale=-1.0), reads=[("pb", 1)], writes=["enb"])
                P.op("dve", lambda e: e.tensor_copy(out=ebl[:, par, :].unsqueeze(2), in_=eb[:, :, nt - 1:nt]),
                     reads=["eb"], writes=[("ebl", par)])
                P.op("dve", lambda e: e.tensor_tensor(out=qt2[par][:, :, 0:nt], in0=R1[:, 0:4, c0:c0 + nt], in1=eb[:, :, 0:nt], op=ALU.mult),
                     reads=["eb"] + [("R1", h) for h in range(4)], writes=[("qt", par)])
                P.op("dve", lambda e: e.tensor_tensor(out=kt[:, :, 0:nt], in0=R1[:, 4:8, c0:c0 + nt], in1=enb[:, :, 0:nt], op=ALU.mult),
                     reads=["enb"] + [("R1", 4 + h) for h in range(4)], writes=["kt"])
                yield
                for h in range(4):
                    mm(pb[0][0:nt, h * 128:h * 128 + nt], kt[:, h, 0:nt], qt2[par][:, h, 0:nt], True, True,
                       reads=["kt", ("qt", par)], writes=[("pb", 0)])
                for h in range(4):
                    tp(pbv[1][0:nt, h * 128:(h + 1) * 128], kt[:, h, 0:nt], identb, reads=["kt", "init_pool"], writes=[("pb", 1)])
                cm = cmaskb[0:nt, 0:nt].unsqueeze(1).broadcast_to([nt, 4, nt])
                P.op("dve", lambda e: e.tensor_tensor(
                    out=am2[par][0:nt, :, 0:nt], in0=pb[0][0:nt, :].rearrange("p (h t) -> p h t", h=4)[:, :, 0:nt], in1=cm, op=ALU.mult),
                    reads=[("pb", 0), "init_pool"], writes=[("am", par)])
                copy_op("act", ktTs2[par][0:nt, :], pbv[1][0:nt, 0:512], reads=[("pb", 1)], writes=[("ktTs", par)])
                yield

            def seq(ci):
                c0, nt = chunks[ci]
                par = ci % 2
                bO = [2, 3]
                for h in range(4):
                    bank = bO[h // 2]
                    reg = pb[bank][0:nt, (h % 2) * 256:(h % 2 + 1) * 256]
                    mm(reg, am2[par][0:nt, h, 0:nt], va[0:nt, ci, h * 256:(h + 1) * 256], True, False,
                       reads=[("am", par), ("va", ci, h)], writes=[("pb", bank)])
                    mm(reg, qt2[par][:, h, 0:nt], Sbf[l][:, h, :], False, True, reads=[("qt", par), ("Sbf", l)], writes=[("pb", bank)])
                yield
                eblb = ebl[:, par, :].unsqueeze(2).broadcast_to([128, 4, 256])
                for hh in range(2):
                    bS = 4 if hh == 0 else 1
                    for h in (hh * 2, hh * 2 + 1):
                        mm(pb[bS][:, (h % 2) * 256:(h % 2 + 1) * 256], ktTs2[par][0:nt, h * 128:(h + 1) * 128], va[0:nt, ci, h * 256:(h + 1) * 256], True, True,
                           reads=[("ktTs", par), ("va", ci, h)], writes=[("pb", bS)])
                    Sv = S[l][:, hh * 2:(hh + 1) * 2, :].rearrange("p h v -> p (h v)")
                    P.op("dve", lambda e, Sv=Sv, bS=bS: e.tensor_tensor(out=Sv, in0=pb[bS][:, :], in1=Sv, op=ALU.add),
                         reads=[("pb", bS), ("S", l)], writes=[("S", l), ("S", l, hh)])
                    for h in (hh * 2, hh * 2 + 1):
                        P.op("act", lambda e, h=h: e.activation(out=Sbf[l][:, h, :], in_=S[l][:, h, :], func=AF.Copy, scale=ebl[:, par, h:h + 1]),
                             reads=[("S", l, hh), ("ebl", par)], writes=[("Sbf", l)])
                P.op("dve", lambda e: e.tensor_tensor(out=S[l][:], in0=S[l][:], in1=eblb, op=ALU.mult),
                     reads=[("ebl", par), ("S", l)], writes=[("S", l), ("S", l, 0), ("S", l, 1)])
                yield
                for h in range(4):
                    bank = bO[h // 2]
                    P.op("act", lambda e, bank=bank, h=h: e.activation(
                        out=junk[0:nt, 0:256], in_=pb[bank][0:nt, (h % 2) * 256:(h % 2 + 1) * 256], func=AF.Square, accum_out=ssm[0:nt, h:h + 1]),
                        reads=[("pb", bank)], writes=["ssm", ("sq", 2)])
                P.op("act", lambda e: e.activation(out=ssm[0:nt, 4:8], in_=ssm[0:nt, 0:4], func=AF.Ln, scale=1.0 / 256, bias=EPS),
                     reads=["ssm"], writes=["ssr"])
                P.op("act", lambda e: e.activation(out=ssm[0:nt, 4:8], in_=ssm[0:nt, 4:8], func=AF.Exp, scale=-0.5),
                     reads=["ssr"], writes=["ssr"])
                for h in range(4):
                    bank = bO[h // 2]
                    P.op("dve", lambda e, bank=bank, h=h: e.scalar_tensor_tensor(
                        out=on[0:nt, h * 256:(h + 1) * 256], in0=pb[bank][0:nt, (h % 2) * 256:(h % 2 + 1) * 256],
                        scalar=ssm[0:nt, 4 + h:5 + h], in1=gA[0:nt, l, :], op0=ALU.mult, op1=ALU.mult),
                        reads=[("pb", bank), "ssr", "init_sp"], writes=["on"])
                yield
                for blk in range(8):
                    tp(pbv[4][:, blk * 128:blk * 128 + nt], on[0:nt, blk * 128:(blk + 1) * 128], identb[0:nt, 0:nt],
                       reads=["on", "init_pool"], writes=[("pb", 4)])
                P.op("dve", lambda e: e.tensor_tensor(
                    out=Ua[:, :, c0:c0 + nt], in0=pbv[4][:].rearrange("p (b t) -> p b t", b=8)[:, :, 0:nt], in1=Ua[:, :, c0:c0 + nt], op=ALU.mult),
                    reads=[("pb", 4)] + [("Ua", kc) for kc in range(8)], writes=[("Ua", kc) for kc in range(8)])
                yield

            n = len(chunks)
            for _ in prep(0):
                yield
            for ci in range(n):
                sg_ = seq(ci)
                pg_ = prep(ci + 1) if ci + 1 < n else None
                for ch in GLA_PAT:
                    if ch == "S":
                        next(sg_)
                        yield
                    elif pg_ is not None:
                        next(pg_)
                        yield

        def qk_flush(keep=0):
            q = st.setdefault("qk_q", [])
            while len(q) > keep:
                q.pop(0)()

        def qk_group(c_off, g, is_q):
            def ev(bank, cbi):
                hp = g * 2 + cbi
                st["qk_n"] = st.get("qk_n", 0) + 1
                i = st["qk_n"] % 3
                P.op("act", lambda e: e.activation(out=sq[i][:, 0:ntok], in_=pb[bank][:, 0:ntok], func=AF.Square),
                     reads=[("pb", bank)], writes=[("sq", i)])
                qk_flush(QK_DEPTH - 1)

                def finish():
                    b2 = nb()
                    mm(pb[b2][:, 0:ntok], bonesb, sq[i][:, 0:ntok], True, True, reads=[("sq", i), "init_pool"], writes=[("pb", b2)])
                    ri = i % 2
                    rstd_from_psum(b2, ri, ntok, 64)
                    if is_q:
                        for par in range(2):
                            ps_ = slice(par * 64, (par + 1) * 64)
                            P.op("dve", lambda e, ps_=ps_, par=par: e.scalar_tensor_tensor(
                                out=Qz[ps_, hp, par, 0:ntok], in0=pb[bank][ps_, 0:ntok], scalar=gq[ps_, l:l + 1],
                                in1=rstd[ri][ps_, 0:ntok], op0=ALU.mult, op1=ALU.mult),
                                reads=[("pb", bank), ("rstd", ri), "gq"], writes=[("Qz", hp)])
                    else:
                        P.op("dve", lambda e: e.scalar_tensor_tensor(
                            out=Kcur[:, hp, 0:ntok], in0=pb[bank][:, 0:ntok], scalar=gvec[:, l * NG + 17:l * NG + 18],
                            in1=rstd[ri][:, 0:ntok], op0=ALU.mult, op1=ALU.mult),
                            reads=[("pb", bank), ("rstd", ri), "init_sp"], writes=[("Kcur", hp)])
                st["qk_q"].append(finish)
            formB(w_in_l, c_off + g * 256, 8, 256, hT_rhs, hT_res, ev)

        def vb_group(g):
            def ev(bank, tb, nt_b):
                copy_op(evac_eng(), Vcur[0:nt_b, tb, :].rearrange("p (h e) -> p h e", e=66)[:, g * 4:(g + 1) * 4, 0:64],
                        pb[bank][0:nt_b, 0:256].rearrange("p (h e) -> p h e", e=64),
                        reads=[("pb", bank)], writes=[("Vcur", tb, g)])
            formA(w_in_l, OFF_VB + g * 256, 256, ntok, ev)

        def mga_group(g):
            def ev(bank, cbi):
                kc = g * 2 + cbi
                P.op("act", lambda e: e.activation(out=M[:, kc, 0:ntok], in_=pb[bank][:, 0:ntok], func=AF.Sigmoid),
                     reads=[("pb", bank)], writes=[("M", kc)])
            formB(w_in_l, OFF_MGA + g * 256, 8, 256, hT_rhs, hT_res, ev)

        def mgb_group(g):
            def ev(bank, cbi):
                kc = g * 2 + cbi
                P.op("act", lambda e: e.activation(out=M2[:, kc * TT:kc * TT + ntok], in_=pb[bank][:, 0:ntok], func=AF.Sigmoid),
                     reads=[("pb", bank)], writes=[("va", kc // 2, (kc % 2) * 2), ("va", kc // 2, (kc % 2) * 2 + 1)])
            formB(w_in_l, OFF_MGB + g * 256, 8, 256, hT_rhs, hT_res, ev)

        Ua_rhs = lambda kc: Ua[:, kc, 0:ntok]
        Ua_res = lambda kc: ("Ua", kc)
        Ub_rhs = lambda kc: Ub[:, kc, 0:ntok]
        Ub_res = lambda kc: ("Ub", kc)

        def bra_group(g):
            def ev(bank, cbi):
                kc = g * 2 + cbi
                P.op("dve", lambda e: e.tensor_tensor(out=M[:, kc, 0:ntok], in0=pb[bank][:, 0:ntok], in1=M[:, kc, 0:ntok], op=ALU.mult),
                     reads=[("pb", bank), ("M", kc)], writes=[("M", kc)])
            formB(wba[l], g * 256, 8, 256, Ua_rhs, Ua_res, ev)

        P.cur_tag = "inprojB"
        for g in range(4):
            qk_group(OFF_QB, g, True)
        for g in range(4):
            qk_group(OFF_KB, g, False)
        qk_flush()
        fill1 = []
        for g in range(4):
            def evv(bank, tb, nt_b, g=g):
                copy_op(evac_eng(), Vcur[0:nt_b, tb, :].rearrange("p (h e) -> p h e", e=66)[:, g * 4:(g + 1) * 4, 0:64],
                        pb[bank][0:nt_b, 0:256].rearrange("p (h e) -> p h e", e=64),
                        reads=[("pb", bank)], writes=[("Vcur", tb, g)])
            fill1.extend(formA_thunks(w_in_l, OFF_VB + g * 256, 256, ntok, evv))
        interleave(gla_gen(), fill1, 7 * nblk, "gla")
        qk_flush()

        P.cur_tag = "kvout"
        if is_sample or last_tile:
            dst = (sgs if is_sample else sgp)[l].rearrange("h d v -> d h v")
            key = ("Sout", l, is_sample)
            P.op("sp", lambda e, dst=dst: e.dma_start(out=dst, in_=S[l][:]), reads=[("S", l)], dma_sem=key)
            out_sems.append(key)
            if is_sample:
                P.op("dve", lambda e: e.memset(S[l][:], 0.0), writes=[("S", l)])
                P.op("dve", lambda e: e.memset(Sbf[l][:], 0.0), writes=[("Sbf", l)])

        Kcur_all = [("Kcur", hp) for hp in range(8)]
        Vcur_all = [("Vcur", tb, g) for tb in range(4) for g in range(4)]
        if (not is_sample) and (not last_tile):
            P.op("sp", lambda e: e.dma_start(out=kspill[l], in_=Kcur[:].rearrange("p a b -> p (a b)")),
                 reads=Kcur_all, writes=[("kspill", l)], dma_sem=("kst", l))
            P.op("sp", lambda e: e.dma_start(out=vspill[l], in_=Vcur[:].rearrange("p a b -> p (a b)")),
                 reads=Vcur_all, writes=[("vspill", l)], dma_sem=("vst", l))
        if is_sample or last_tile:
            kdst = kbs if is_sample else kbp
            vdst = vbs if is_sample else vbp
            r0 = 448 if is_sample else 0
            for tb, (c0, nt) in enumerate(chunks):
                bank = nb()
                for hp in range(8):
                    tp(pbv[bank][0:nt, hp * 128:(hp + 1) * 128], Kcur[:, hp, c0:c0 + nt], identb, reads=[("Kcur", hp), "init_pool"], writes=[("pb", bank)])
                copy_op("act", xin[0][0:nt, :], pbv[bank][0:nt, :], reads=[("pb", bank)], writes=[("xin", 0)])
                key = ("kout", l, is_sample)
                P.op("sp", lambda e, c0=c0, nt=nt: e.dma_start(out=kdst[l, r0 + c0:r0 + c0 + nt, :], in_=xin[0][0:nt, :]),
                     reads=[("xin", 0)], dma_sem=key)
                copy_op("dve", xin[1][0:nt, :].rearrange("p (h e) -> p h e", e=64),
                        Vcur[0:nt, tb, :].rearrange("p (h e) -> p h e", e=66)[:, :, 0:64],
                        reads=[("Vcur", tb, g) for g in range(4)], writes=[("xin", 1)])
                key2 = ("vout", l, is_sample)
                P.op("sp", lambda e, c0=c0, nt=nt: e.dma_start(out=vdst[l, r0 + c0:r0 + c0 + nt, :], in_=xin[1][0:nt, :]),
                     reads=[("xin", 1)], dma_sem=key2)
            out_sems.extend([("kout", l, is_sample), ("vout", l, is_sample)])

        def band_gen():
            for pi, (q0, nq) in enumerate(chunks):
                blocks = []
                for C in range(4, -1, -1):
                    b = pi - C
                    if b >= 0:
                        nk = min(128, ntok - b * 128)
                        blocks.append((C, Kcur, "cur", b, nk))
                    elif have_prev:
                        blocks.append((C, Kprev, "prev", 4 + b, 128))

                def qk_exp(hg):
                    for (C, Kb, which, b, nk) in blocks:
                        bank = nb()
                        Kres = (lambda hp: ("Kcur", hp)) if which == "cur" else (lambda hp: "Kprev")
                        if C <= 1:
                            mm(pb[bank][0:nk, 0:4 * nq], Jb[:, 0:nk], Gp[:, l * 2 + C, hg * 4:(hg + 1) * 4, 0:nq], True, False,
                               reads=[("Gp", l), "init_pool"], writes=[("pb", bank)])
                        for h2 in range(2):
                            hp = hg * 2 + h2
                            mm(pb[bank][0:nk, h2 * 2 * nq:(h2 + 1) * 2 * nq], Kb[:, hp, b * 128:b * 128 + nk], Qz[:, hp, :, q0:q0 + nq],
                               C > 1, h2 == 1, reads=[Kres(hp), ("Qz", hp)], writes=[("pb", bank)])
                        src3 = pb[bank][:, 0:4 * nq].rearrange("p (h q) -> p h q", h=4)
                        P.op("act", lambda e, src3=src3, C=C, hg=hg, nk=nk: e.activation(
                            out=PT[hg % 2][C][0:nk, :, 0:nq], in_=src3[0:nk, :, 0:nq], func=AF.Exp),
                            reads=[("pb", bank)], writes=[("PT", hg % 2, C)])
                        if nq == 128 and C == 4:
                            P.op("dve", lambda e, C=C, hg=hg: e.memset(PT[hg % 2][C][0:64, :, 64:128], 0.0), writes=[("PT", hg % 2, C)])
                        elif nq == 128 and C == 0:
                            P.op("dve", lambda e, C=C, hg=hg: e.memset(PT[hg % 2][C][64:128, :, 0:64], 0.0), writes=[("PT", hg % 2, C)])
                        yield

                def pv_norm(hg):
                    ob_bank = nb()
                    for hh in range(4):
                        h = hg * 4 + hh
                        for bi, (C, Kb, which, b, nk) in enumerate(blocks):
                            Vb = Vcur if which == "cur" else Vprev
                            vres = [("Vcur", b, h // 4)] if which == "cur" else ["Vprev"]
                            mm(pb[ob_bank][0:nq, hh * 66:(hh + 1) * 66], PT[hg % 2][C][0:nk, hh, 0:nq], Vb[0:nk, b, h * 66:(h + 1) * 66],
                               bi == 0, bi == len(blocks) - 1, reads=[("PT", hg % 2, C)] + vres, writes=[("pb", ob_bank)])
                    ob3 = pb[ob_bank][0:nq, 0:264].rearrange("p (h e) -> p h e", e=66)
                    P.op("dve", lambda e, ob3=ob3, hg=hg: e.reciprocal(out=rec[0:nq, hg * 4:(hg + 1) * 4].unsqueeze(2), in_=ob3[:, :, 64:65]),
                         reads=[("pb", ob_bank)], writes=["rec"])
                    recb = rec[0:nq, hg * 4:(hg + 1) * 4].unsqueeze(2).broadcast_to([nq, 4, 64])
                    P.op("dve", lambda e, ob3=ob3, hg=hg, recb=recb: e.tensor_tensor(
                        out=on[0:nq, hg * 256:(hg + 1) * 256].rearrange("p (h e) -> p h e", e=64), in0=ob3[:, :, 0:64], in1=recb, op=ALU.mult),
                        reads=[("pb", ob_bank), "rec"], writes=["on"])

                for hg in range(5):
                    if hg < 4:
                        yield from qk_exp(hg)
                    if hg >= 1:
                        pv_norm(hg - 1)
                        yield
                bI = nb()
                for blk in range(8):
                    tp(pbv[bI][:, blk * 128:blk * 128 + nq], on[0:nq, blk * 128:(blk + 1) * 128], identb[0:nq, 0:nq],
                       reads=["on", "init_pool"], writes=[("pb", bI)])
                P.op("dve", lambda e, bI=bI, q0=q0, nq=nq: e.tensor_tensor(
                    out=Ub[:, :, q0:q0 + nq], in0=pbv[bI][:].rearrange("p (b t) -> p b t", b=8)[:, :, 0:nq], in1=Ub[:, :, q0:q0 + nq], op=ALU.mult),
                    reads=[("pb", bI)] + [("Ub", kc) for kc in range(8)], writes=[("Ub", kc) for kc in range(8)])
                yield

        fill2 = []
        for g in range(4):
            def evg(bank, cbi, g=g):
                kc = g * 2 + cbi
                i = kc % 2
                P.op("act", lambda e: e.activation(out=gt[i][:, 0:ntok], in_=pb[bank][:, 0:ntok], func=AF.Tanh, scale=0.5),
                     reads=[("pb", bank)], writes=[("sq", i)])
                P.op("dve", lambda e: e.scalar_tensor_tensor(out=Ub[:, kc, 0:ntok], in0=gt[i][:, 0:ntok], scalar=1.0, in1=pb[bank][:, 0:ntok],
                                                             op0=ALU.add, op1=ALU.mult),
                     reads=[("sq", i), ("pb", bank)], writes=[("Ub", kc)])
            fill2.extend(formB_thunks(w_in_l, OFF_GB + g * 256, 8, 256, hT_rhs, hT_res, evg))
        n_front = len(fill2)
        for g in range(4):
            def evm(bank, cbi, g=g):
                kc = g * 2 + cbi
                P.op("act", lambda e: e.activation(out=M[:, kc, 0:ntok], in_=pb[bank][:, 0:ntok], func=AF.Tanh, scale=0.5),
                     reads=[("pb", bank)], writes=[("M", kc)])
            fill2.extend(formB_thunks(w_in_l, OFF_MGA + g * 256, 8, 256, hT_rhs, hT_res, evm))
        for g in range(4):
            def evb(bank, cbi, g=g):
                kc = g * 2 + cbi
                P.op("act", lambda e: e.activation(out=M2[:, kc * TT:kc * TT + ntok], in_=pb[bank][:, 0:ntok], func=AF.Tanh, scale=0.5),
                     reads=[("pb", bank)], writes=[("va", kc // 2, (kc % 2) * 2), ("va", kc // 2, (kc % 2) * 2 + 1)])
            fill2.extend(formB_thunks(w_in_l, OFF_MGB + g * 256, 8, 256, hT_rhs, hT_res, evb))
        for g in range(4):
            def eva(bank, cbi, g=g):
                kc = g * 2 + cbi
                P.op("dve", lambda e: e.scalar_tensor_tensor(out=M[:, kc, 0:ntok], in0=M[:, kc, 0:ntok], scalar=1.0, in1=pb[bank][:, 0:ntok],
                                                             op0=ALU.add, op1=ALU.mult),
                     reads=[("pb", bank), ("M", kc)], writes=[("M", kc)])
            fill2.extend(formB_thunks(wba[l], g * 256, 8, 256, Ua_rhs, Ua_res, eva))
        ny = 0
        for pi in range(nblk):
            nbk = sum(1 for C in range(5) if (pi - C >= 0) or have_prev)
            ny += 4 * (nbk + 1) + 1
        interleave(band_gen(), fill2, ny, "band", front=n_front)

        P.cur_tag = "branchB"
        for g in range(4):
            def ev(bank, cbi, g=g):
                kc = g * 2 + cbi
                P.op("dve", lambda e: e.scalar_tensor_tensor(out=ftmp[:, 0:ntok], in0=M2[:, kc * TT:kc * TT + ntok], scalar=1.0, in1=pb[bank][:, 0:ntok],
                                                             op0=ALU.add, op1=ALU.mult),
                     reads=[("pb", bank), ("va", kc // 2, (kc % 2) * 2), ("va", kc // 2, (kc % 2) * 2 + 1)], writes=["ftmp"])
                P.op("dve", lambda e: e.scalar_tensor_tensor(out=M[:, kc, 0:ntok], in0=ftmp[:, 0:ntok], scalar=0.5, in1=M[:, kc, 0:ntok],
                                                             op0=ALU.mult, op1=ALU.add),
                     reads=["ftmp", ("M", kc)], writes=[("M", kc)])
            formB(wbb[l], g * 256, 8, 256, Ub_rhs, Ub_res, ev)

        if nxt is not None and l == DEPTH - 1:
            nsrc, ntok0, nntok = nxt
            for tb in range((nntok + 127) // 128):
                nt2 = min(128, nntok - tb * 128)
                dstv, dres = xstage(tb)
                P.op("sp", lambda e, tb=tb, nt2=nt2, dstv=dstv: e.dma_start(out=dstv[0:nt2, :], in_=nsrc[ntok0 + tb * 128:ntok0 + tb * 128 + nt2, :]),
                     writes=dres, dma_sem=("xpre", tb))
        P.cur_tag = "pload"
        for tb, (c0, nt) in enumerate(chunks):
            psb, pres = pin[tb % 2], ("pin", tb % 2)
            if tb >= 2:
                if p_stage_xin:
                    psb, pres = xin[tb % 2], ("xin", tb % 2)
                else:
                    P.op("sp", lambda e, tb=tb, c0=c0, nt=nt: e.dma_start(out=pin[tb % 2][0:nt, :], in_=psrc[l, tok0 + c0:tok0 + c0 + nt, :]),
                         writes=[("pin", tb % 2)], dma_sem=("pin", tb % 2))
            bank = nb()
            for j in range(2):
                tp(pb[bank][:, j * 128:j * 128 + nt], psb[0:nt, j * 128:(j + 1) * 128], identf[0:nt, 0:nt],
                   reads=[pres, "init_sp"], writes=[("pb", bank)])
            copy_op("act", pT[:, 0:2, c0:c0 + nt], pb[bank][:, 0:256].rearrange("p (j t) -> p j t", j=2)[:, :, 0:nt],
                    reads=[("pb", bank)], writes=["pT"])
        P.cur_tag = "wout"
        M_rhs = lambda kc: M[:, kc, 0:ntok]
        M_res = lambda kc: ("M", kc)
        for g in range(4):
            def ev(bank, cbi, g=g):
                kc = g * 2 + cbi
                P.op("dve", lambda e: e.scalar_tensor_tensor(out=xT[:, kc, 0:ntok], in0=pb[bank][:, 0:ntok], scalar=0.5, in1=xT[:, kc, 0:ntok],
                                                             op0=ALU.mult, op1=ALU.add),
                     reads=[("pb", bank), ("xT", kc)], writes=[("xT", kc)])
            formB(wo[l], g * 256, 8, 256, M_rhs, M_res, ev)

        P.cur_tag = "ple"
        rmsnorm_to_hT(l, ntok, l * NG + 8)
        for g in range(4):
            def ev(bank, cbi, g=g):
                kc = g * 2 + cbi
                P.op("act", lambda e: e.activation(out=M[:, kc, 0:ntok], in_=pb[bank][:, 0:ntok], func=AF.Sigmoid),
                     reads=[("pb", bank)], writes=[("M", kc)])
            formB(wpg[l], g * 256, 8, 256, hT_rhs, hT_res, ev)
        pT_rhs = lambda kc: pT[:, kc, 0:ntok]
        pT_res = lambda kc: "pT"
        for g in range(4):
            def ev(bank, cbi, g=g):
                kc = g * 2 + cbi
                P.op("dve", lambda e: e.tensor_tensor(out=ftmp[:, 0:ntok], in0=pb[bank][:, 0:ntok], in1=M[:, kc, 0:ntok], op=ALU.mult),
                     reads=[("pb", bank), ("M", kc)], writes=["ftmp"])
                P.op("dve", lambda e: e.tensor_tensor(out=xT[:, kc, 0:ntok], in0=ftmp[:, 0:ntok], in1=xT[:, kc, 0:ntok], op=ALU.add),
                     reads=["ftmp", ("xT", kc)], writes=[("xT", kc)])
            formB(wp[l], g * 256, 2, 256, pT_rhs, pT_res, ev)

        P.cur_tag = "yout"
        if l == DEPTH - 1:
            ydst = ys if is_sample else yp
            for tb, (c0, nt) in enumerate(chunks):
                buf = tb % 2
                for g in range(2):
                    bank = nb()
                    for kk in range(4):
                        kc = g * 4 + kk
                        tp(pb[bank][0:nt, kk * 128:(kk + 1) * 128], xT[:, kc, c0:c0 + nt], identf, reads=[("xT", kc), "init_sp"], writes=[("pb", bank)])
                    copy_op("dve", xin[buf][0:nt, g * 512:(g + 1) * 512], pb[bank][0:nt, :], reads=[("pb", bank)], writes=[("xin", buf)])
                key = ("yout", buf, is_sample)
                P.op("sp", lambda e, c0=c0, nt=nt, buf=buf: e.dma_start(out=ydst[tok0 + c0:tok0 + c0 + nt, :], in_=xin[buf][0:nt, :]),
                     reads=[("xin", buf)], dma_sem=key)
                if key not in out_sems:
                    out_sems.append(key)

    tiles = [(TT, t * TT, xp, pp, False, t, t == NT - 1) for t in range(NT)]
    if do_sample:
        tiles.append((64, 0, xs, ps, True, 0, False))
    for ti, (ntok_, tok0_, xsrc_, psrc_, iss_, t_, last_) in enumerate(tiles):
        nxt = None
        if ti + 1 < len(tiles):
            n2 = tiles[ti + 1]
            nxt = (n2[2], n2[1], n2[0])
        for l in range(DEPTH):
            tile_layer(l, ntok_, tok0_, xsrc_, psrc_, iss_, t_, last_, prefetched=(ti > 0), nxt=nxt)
    if do_sample:
        out_sems.extend(["kroll", "vroll"])
    P.emit(final_wait_sems=out_sems)
    return nc


_CACHE = {}


def _consts():
    c = np.zeros((128, 6, 128), np.float32)
    c[:, 0, :] = np.eye(128)
    c[:, 1, :] = np.eye(128)[::-1]
    j = np.arange(128)[:, None]
    i = np.arange(128)[None, :]
    c[:, 2, :] = (j <= i)
    c[:, 3, :] = (j <= i) * (-1.0 / 16.0)
    blk = np.zeros((128, 128), np.float32)
    blk[0:64, 0:64] = 1
    blk[64:128, 64:128] = 1
    c[:, 4, :] = blk
    c[:, 5, :] = 1
    return c


def make_in_maps(inp, SEQ, DEPTH, n_cores=8):
    f = lambda a: np.ascontiguousarray(np.asarray(a, dtype=np.float32))
    gv = np.zeros((128, DEPTH * NG), np.float32)
    for l in range(DEPTH):
        gv[:, l * NG:l * NG + 8] = f(inp["norm_g"])[l].reshape(8, 128).T
        gv[:, l * NG + 8:l * NG + 16] = f(inp["ple_norm_g"])[l].reshape(8, 128).T
        gv[:, l * NG + 16] = np.tile(f(inp["q_norm_g"])[l], 2)
        gv[:, l * NG + 17] = np.tile(f(inp["k_norm_g"])[l], 2)
    gA = np.ascontiguousarray(np.broadcast_to(f(inp["gla_norm_g"])[None, :, :], (128, DEPTH, 256)))
    rb = f(inp["rel_bias"])
    tabext = np.ascontiguousarray(np.concatenate([rb, np.repeat(rb[..., -1:], 127, axis=-1)], axis=-1))
    cb = np.ascontiguousarray(np.broadcast_to(rb[None, :, :, 256], (128, DEPTH, 16)))
    wgu = np.ascontiguousarray(np.concatenate([f(inp["w_gate_up"]), f(inp["b_gate"])[:, None, :]], axis=1))
    common = {
        "w_in": f(inp["w_in"]), "wgu": wgu, "wba": f(inp["w_branch_a"]), "wbb": f(inp["w_branch_b"]),
        "wo": f(inp["w_out"]), "wpg": f(inp["w_ple_gate"]), "wp": f(inp["w_ple"]),
        "gv": gv, "gA": gA, "tabext": tabext, "cb": cb, "cst": _consts(),
    }
    xp = f(inp["x_prompt"]); pp = f(inp["p_prompt"]); xs = f(inp["x_sample"]); ps = f(inp["p_sample"])
    sg = f(inp["state_gla"]); ck = f(inp["cache_band_k"]); cv = f(inp["cache_band_v"])
    maps = []
    for c in range(n_cores):
        b = c % xp.shape[0]
        m = dict(common)
        m["xp"] = np.ascontiguousarray(xp[b])
        m["pp"] = np.ascontiguousarray(pp[:, b])
        m["xs"] = np.ascontiguousarray(xs[c])
        m["ps"] = np.ascontiguousarray(ps[:, c])
        m["sg"] = np.ascontiguousarray(sg[:, c])
        m["ck"] = np.ascontiguousarray(ck[:, c].reshape(DEPTH, 512, 1024))
        m["cv"] = np.ascontiguousarray(cv[:, c].reshape(DEPTH, 512, 1024))
        maps.append(m)
    return maps


def run(inp, SEQ, DEPTH):
    key = (SEQ, DEPTH)
    if key not in _CACHE:
        _CACHE[key] = build_program(SEQ, DEPTH)
    nc = _CACHE[key]
    maps = make_in_maps(inp, SEQ, DEPTH)
    res = run_bass_kernel_spmd(nc, maps, core_ids=list(range(8)))
    r = res.results
    B = np.asarray(inp["x_prompt"]).shape[0]
    y_prompt = np.stack([r[b]["yp"] for b in range(B)])
    y_sample = np.stack([r[c]["ys"] for c in range(8)])
    sgp = np.stack([r[b]["sgp"] for b in range(B)], axis=1)
    kbp = np.stack([r[b]["kbp"] for b in range(B)], axis=1).reshape(DEPTH, B, 512, 16, 64)
    vbp = np.stack([r[b]["vbp"] for b in range(B)], axis=1).reshape(DEPTH, B, 512, 16, 64)
    sgs = np.stack([r[c]["sgs"] for c in range(8)], axis=1)
    kbs = np.stack([r[c]["kbs"] for c in range(8)], axis=1).reshape(DEPTH, 8, 512, 16, 64)
    vbs = np.stack([r[c]["vbs"] for c in range(8)], axis=1).reshape(DEPTH, 8, 512, 16, 64)
    return (y_prompt, y_sample, sgp, kbp, vbp, sgs, kbs, vbs)


def kernel(**inputs):
    return run(inputs, 4096, 2)
```

```python
import numpy as np
import concourse.bass as bass
import concourse.mybir as mybir
from concourse.bass_utils import run_bass_kernel_spmd

F32 = mybir.dt.float32
BF16 = mybir.dt.bfloat16
AF = mybir.ActivationFunctionType
ALU = mybir.AluOpType

D = 1024
N_IN = 9232
OFF_QA, OFF_KA, OFF_VA, OFF_RA, OFF_GA = 0, 512, 1024, 2048, 2064
OFF_QB, OFF_KB, OFF_VB, OFF_GB, OFF_MGA, OFF_MGB = 3088, 4112, 5136, 6160, 7184, 8208
EPS = 1e-6
TT = 512
QK_DEPTH = 1
import os
GLA_PAT = 'SPSSPSP'
NG = 18

ENGS = ("pe", "act", "dve", "pool", "sp")


class Op:
    __slots__ = ("eng", "fn", "deps", "inc", "count", "dma_sem", "dma_val", "tag", "meta")

    def __init__(self, eng, fn):
        self.eng = eng
        self.fn = fn
        self.deps = []
        self.inc = False
        self.count = 0
        self.dma_sem = None
        self.dma_val = 0


class Prog:
    def __init__(self, nc):
        self.nc = nc
        self.ops = {e: [] for e in ENGS}
        self.last_w = {}
        self.readers = {}
        self.dma_cnt = {}
        self.cur_tag = ""

    def op(self, eng, fn, reads=(), writes=(), dma_sem=None):
        o = Op(eng, fn)
        o.tag = self.cur_tag
        o.meta = None
        is_dma = dma_sem is not None
        deps = {}
        for r in reads:
            w = self.last_w.get(r)
            if w is not None:
                deps[id(w)] = (w, "raw")
        for r in writes:
            w = self.last_w.get(r)
            if w is not None and id(w) not in deps:
                deps[id(w)] = (w, "waw")
            for rd in self.readers.get(r, ()):
                if id(rd) not in deps:
                    deps[id(rd)] = (rd, "war")
        for d, kind in deps.values():
            d_is_dma = d.dma_sem is not None
            if d.eng == eng and not d_is_dma and not is_dma:
                if eng == "pe" or kind != "raw":
                    continue
            if not d_is_dma:
                d.inc = True
            o.deps.append(d)
        for r in reads:
            self.readers.setdefault(r, []).append(o)
        for r in writes:
            self.last_w[r] = o
            self.readers[r] = []
        if is_dma:
            o.dma_sem = dma_sem
            c = self.dma_cnt.get(dma_sem, 0) + 16
            self.dma_cnt[dma_sem] = c
            o.dma_val = c
        self.ops[eng].append(o)
        return o

    def emit(self, final_wait_sems=()):
        nc = self.nc
        esem = {e: nc.alloc_semaphore("es_" + e) for e in ENGS}
        dsem = {}
        for i, k in enumerate(self.dma_cnt):
            dsem[k] = nc.alloc_semaphore("ds%d" % i)
        for e in ENGS:
            c = 0
            for o in self.ops[e]:
                if o.dma_sem is None and o.inc:
                    c += 1
                    o.count = c
        ops = self.ops
        dma_cnt = self.dma_cnt

        def run(e, eng):
            known = {}
            for o in ops[e]:
                need = {}
                for d in o.deps:
                    if d.dma_sem is not None:
                        key = ("d", d.dma_sem)
                        val = d.dma_val
                    else:
                        key = ("e", d.eng)
                        val = d.count
                    if val > need.get(key, 0):
                        need[key] = val
                for key, val in need.items():
                    if known.get(key, 0) >= val:
                        continue
                    known[key] = val
                    sem = dsem[key[1]] if key[0] == "d" else esem[key[1]]
                    eng.wait_ge(sem, val)
                ins = o.fn(eng)
                if o.dma_sem is not None:
                    ins.then_inc(dsem[o.dma_sem], 16)
                elif o.inc:
                    ins.then_inc(esem[e], 1)
            if e == "sp":
                for k in final_wait_sems:
                    eng.wait_ge(dsem[k], dma_cnt[k])

        with nc.Block() as block:
            @block.tensor
            def _(eng):
                run("pe", eng)

            @block.scalar
            def _(eng):
                run("act", eng)

            @block.vector
            def _(eng):
                run("dve", eng)

            @block.gpsimd
            def _(eng):
                run("pool", eng)

            @block.sync
            def _(eng):
                run("sp", eng)


def build_program(SEQ, DEPTH, do_sample=True):
    nc = bass.Bass("TRN2", target_bir_lowering=False)
    P = Prog(nc)
    NT = SEQ // TT

    def din(name, shape):
        return nc.dram_tensor(name, list(shape), F32, kind="ExternalInput").ap()

    def dout(name, shape):
        return nc.dram_tensor(name, list(shape), F32, kind="ExternalOutput").ap()

    xp = din("xp", [SEQ, D]); pp = din("pp", [DEPTH, SEQ, 256])
    xs = din("xs", [64, D]); ps = din("ps", [DEPTH, 64, 256])
    sg = din("sg", [DEPTH, 4, 128, 256])
    ck = din("ck", [DEPTH, 512, D]); cv = din("cv", [DEPTH, 512, D])
    w_in = din("w_in", [DEPTH, D, N_IN])
    wgu_d = din("wgu", [DEPTH, 17, 512])
    wba = din("wba", [DEPTH, D, D]); wbb = din("wbb", [DEPTH, D, D])
    wo = din("wo", [DEPTH, D, D]); wpg = din("wpg", [DEPTH, D, D])
    wp = din("wp", [DEPTH, 256, D])
    gv_d = din("gv", [128, DEPTH * NG])
    gA_d = din("gA", [128, DEPTH, 256])
    tabext = nc.dram_tensor("tabext", [DEPTH, 16, 384], F32, kind="ExternalInput")
    cb_d = din("cb", [128, DEPTH, 16])
    cst_d = din("cst", [128, 6, 128])

    yp = dout("yp", [SEQ, D]); ys = dout("ys", [64, D])
    sgp = dout("sgp", [DEPTH, 4, 128, 256])
    kbp = dout("kbp", [DEPTH, 512, D]); vbp = dout("vbp", [DEPTH, 512, D])
    sgs = dout("sgs", [DEPTH, 4, 128, 256])
    kbs = dout("kbs", [DEPTH, 512, D]); vbs = dout("vbs", [DEPTH, 512, D])

    kspill = [nc.dram_tensor("kspill%d" % l, [128, 8 * 512], BF16, kind="Internal").ap() for l in range(DEPTH)]
    vspill = [nc.dram_tensor("vspill%d" % l, [128, 4 * 1056], BF16, kind="Internal").ap() for l in range(DEPTH)]

    def sb(name, shape, dt):
        return nc.alloc_sbuf_tensor("s_" + name, list(shape), dt)

    xT = sb("xT", [128, 8, TT], F32)
    hT = sb("hT", [128, 8, TT], BF16)
    R1 = sb("R1", [128, 8, TT], BF16)
    Qz = sb("Qz", [128, 8, 2, TT], BF16)
    va = sb("va", [128, 4, 1024], BF16)
    M2 = va[:].rearrange("p a b -> p (a b)")
    Ua = sb("Ua", [128, 8, TT], BF16)
    Ub = sb("Ub", [128, 8, TT], BF16)
    M = sb("M", [128, 8, TT], BF16)
    Kcur = sb("Kcur", [128, 8, TT], BF16)
    Kprev = sb("Kprev", [128, 8, TT], BF16)
    Vcur = sb("Vcur", [128, 4, 1056], BF16)
    Vprev = sb("Vprev", [128, 4, 1056], BF16)
    S = [sb("S%d" % l, [128, 4, 256], F32) for l in range(DEPTH)]
    Sbf = [sb("Sbf%d" % l, [128, 4, 256], BF16) for l in range(DEPTH)]
    NW = 3
    W = [sb("W%d" % i, [128, 8, 256], BF16) for i in range(NW)]
    Gp = sb("Gp", [128, DEPTH * 2, 16, 128], BF16)
    PT = [[sb("PT%d_%d" % (i, k), [128, 4, 128], BF16) for i in range(5)] for k in range(2)]
    e1 = sb("e1", [128, 512], F32)
    eb = sb("eb", [128, 4, 128], F32)
    enb = sb("enb", [128, 4, 128], F32)
    qt = sb("qt", [128, 4, 128], BF16)
    kt = sb("kt", [128, 4, 128], BF16)
    am = sb("am", [128, 4, 128], BF16)
    ktTs = sb("ktTs", [128, 512], BF16)
    on = sb("on", [128, 1024], BF16)
    ssm = sb("ssm", [128, 8], F32)
    rec = sb("rec", [128, 16], F32)
    sq = [sb("sq%d" % i, [128, TT], BF16) for i in range(3)]
    rstd = [sb("rstd%d" % i, [128, TT], F32) for i in range(2)]
    gt = [sq[0], sq[1]]
    junk = sq[2]
    qt2 = [qt, sb("qtB", [128, 4, 128], BF16)]
    am2 = [am, sb("amB", [128, 4, 128], BF16)]
    ktTs2 = [ktTs, sb("ktTsB", [128, 512], BF16)]
    ebl = sb("ebl", [128, 2, 4], F32)
    ftmp = sb("ftmp", [128, TT], F32)
    xin = [sb("xin%d" % i, [128, 1024], F32) for i in range(2)]
    pin = [sb("pin%d" % i, [128, 256], F32) for i in range(2)]
    pT = sb("pT", [128, 2, TT], BF16)
    cst = sb("cst", [128, 2, 128], F32)
    cstb = sb("cstb", [128, 6, 128], BF16)
    gvec = sb("gvec", [128, DEPTH * NG], F32)
    gq = sb("gq", [128, DEPTH], F32)
    gA = sb("gA", [128, DEPTH, 256], F32)
    cbt = sb("cbt", [128, DEPTH, 16], F32)
    wgu = sb("wgu", [32, DEPTH, 512], BF16)
    raT = sb("raT", [32, TT], BF16)
    pb = [nc.alloc_psum_tensor("pb%d" % i, [128, 512], F32) for i in range(8)]
    pbv = [b.bitcast(BF16) for b in pb]
    UaF = Ua[:].rearrange("p a b -> p (a b)").bitcast(F32)
    UbF = Ub[:].rearrange("p a b -> p (a b)").bitcast(F32)
    def xstage(tb):
        v = UaF if tb < 2 else UbF
        return v[:, (tb % 2) * 1024:(tb % 2 + 1) * 1024], [(("Ua" if tb < 2 else "Ub"), (tb % 2) * 4 + k) for k in range(4)]

    identf = cst[:, 0, :]
    tri = cst[:, 1, :]
    identb = cstb[:, 0, :]
    Jb = cstb[:, 1, :]
    cmaskb = cstb[:, 2, :]
    bonesb = cstb[:, 4, :]
    onesb = cstb[:, 5, :]

    st = {"bank": 0, "ws": 0, "ev": 0, "fb": 0, "mb": 0, "split": False, "kind": "mix"}
    out_sems = []

    def nb(kind=None):
        if st["split"]:
            if (kind or st["kind"]) == "fill":
                b = 5 + st["fb"] % 3
                st["fb"] += 1
            else:
                b = st["mb"] % 5
                st["mb"] += 1
            return b
        b = st["bank"]
        st["bank"] = (b + 1) % 8
        return b

    def interleave(gen, fillers, n_yields, name="", front=0):
        st["split"] = True
        P.cur_tag = "mix:" + name
        nf = len(fillers)
        done = 0
        y = 0
        for _ in gen:
            y += 1
            if y <= front:
                want = min(nf, y)
            else:
                want = min(nf, max(front, front + ((y - front) * (nf - front) + (n_yields - front) - 1) // max(1, n_yields - front)))
            while done < want:
                st["kind"] = "fill"
                P.cur_tag = "fill:" + name
                fillers[done]()
                P.cur_tag = "mix:" + name
                st["kind"] = "mix"
                done += 1
        while done < nf:
            st["kind"] = "fill"
            P.cur_tag = "fill:" + name
            fillers[done]()
            st["kind"] = "mix"
            done += 1
        st["split"] = False

    def evac_eng():
        st["ev"] += 1
        return "act" if st["ev"] % 2 == 0 else "dve"

    def copy_op(eng, out, in_, reads, writes, scale=None):
        if eng == "act":
            if scale is None:
                P.op("act", lambda e: e.activation(out=out, in_=in_, func=AF.Copy), reads, writes)
            else:
                P.op("act", lambda e: e.activation(out=out, in_=in_, func=AF.Copy, scale=scale), reads, writes)
        else:
            if scale is None:
                P.op("dve", lambda e: e.tensor_copy(out=out, in_=in_), reads, writes)
            else:
                P.op("dve", lambda e: e.tensor_scalar(out=out, in0=in_, scalar1=scale, scalar2=None, op0=ALU.mult), reads, writes)

    def mm(out, lhsT, rhs, start, stop, reads, writes):
        o = P.op("pe", lambda e: e.matmul(out, lhsT=lhsT, rhs=rhs, start=start, stop=stop, skip_group_check=True), reads, writes)
        o.meta = int(np.prod(rhs.shape[1:]))

    def tp(out, in_, ident, reads, writes):
        o = P.op("pe", lambda e: e.transpose(out, in_, ident), reads, writes)
        o.meta = int(np.prod(ident.shape[1:]))

    wcache = {}

    def wload(w2d, r0, nk, c0, ncols):
        s = st["ws"] % NW
        st["ws"] += 1
        dst = W[s][:, 0:nk, 0:ncols]
        key = (w2d.name, int(w2d.offset), r0, nk, c0, ncols)
        if key not in wcache:
            src = w2d[r0:r0 + nk * 128, c0:c0 + ncols].rearrange("(k p) c -> p k c", p=128)
            P.op("pool", lambda e: e.dma_start(out=dst, in_=src), writes=[("W", s)], dma_sem=("w", s))
            sc = nc.dram_tensor("wsc%d" % len(wcache), [128, nk * ncols], BF16, kind="Internal").ap()
            wcache[key] = sc
            P.op("sp", lambda e: e.dma_start(out=sc.rearrange("p (k c) -> p k c", k=nk), in_=dst),
                 reads=[("W", s)], writes=[("wsc", key)], dma_sem=("wst", s))
        else:
            sc = wcache[key]
            P.op("pool", lambda e: e.dma_start(out=dst, in_=sc.rearrange("p (k c) -> p k c", k=nk)),
                 reads=[("wsc", key)], writes=[("W", s)], dma_sem=("w", s))
        return s

    def rstd_from_psum(bank, i, ntok, n):
        P.op("act", lambda e: e.activation(out=rstd[i][:, 0:ntok], in_=pb[bank][:, 0:ntok], func=AF.Ln, scale=1.0 / n, bias=EPS),
             reads=[("pb", bank)], writes=[("rstd", i)])
        P.op("act", lambda e: e.activation(out=rstd[i][:, 0:ntok], in_=rstd[i][:, 0:ntok], func=AF.Exp, scale=-0.5),
             reads=[("rstd", i)], writes=[("rstd", i)])

    P.op("sp", lambda e: e.dma_start(out=cst[:, 0, :], in_=cst_d[:, 0, :]), writes=["init_sp"], dma_sem="init_sp")
    P.op("sp", lambda e: e.dma_start(out=cst[:, 1, :], in_=cst_d[:, 3, :]), writes=["init_sp"], dma_sem="init_sp")
    P.op("sp", lambda e: e.dma_start(out=gvec[:], in_=gv_d), writes=["init_sp"], dma_sem="init_sp")
    P.op("sp", lambda e: e.dma_start(out=gA[:], in_=gA_d), writes=["init_sp"], dma_sem="init_sp")
    P.op("sp", lambda e: e.dma_start(out=cbt[:], in_=cb_d), writes=["init_sp"], dma_sem="init_sp")
    P.op("pool", lambda e: e.dma_start(out=cstb[:], in_=cst_d), writes=["init_pool"], dma_sem="init_pool")
    P.op("pool", lambda e: e.dma_start(out=wgu[0:17, :, :], in_=wgu_d.rearrange("l k c -> k l c")), writes=["init_pool"], dma_sem="init_pool")
    P.op("dve", lambda e: e.memset(raT[:], 1.0), writes=["raT"])
    P.op("dve", lambda e: e.memset(Vcur[:], 1.0), writes=[("Vcur", tb, g) for tb in range(4) for g in range(4)])
    P.op("dve", lambda e: e.memset(Vprev[:], 1.0), writes=["Vprev"])
    P.op("dve", lambda e: e.memset(Qz[:], 0.0), writes=[("Qz", hp) for hp in range(8)])
    for k in range(2):
        for i in range(5):
            P.op("dve", lambda e, i=i, k=k: e.memset(PT[k][i][:], 0.0), writes=[("PT", k, i)])
    for l in range(DEPTH):
        P.op("dve", lambda e, l=l: e.memset(S[l][:], 0.0), writes=[("S", l)])
        P.op("dve", lambda e, l=l: e.memset(Sbf[l][:], 0.0), writes=[("Sbf", l, hq) for hq in range(4)])
        P.op("dve", lambda e, l=l: e.tensor_scalar(out=gq[:, l:l + 1], in0=gvec[:, l * NG + 16:l * NG + 17], scalar1=0.125, scalar2=None, op0=ALU.mult),
             reads=["init_sp"], writes=["gq"])
        for C in range(2):
            stg = xin[C][:].rearrange("p (h i) -> p h i", h=8)
            for half in range(2):
                src = bass.AP(tabext, l * 16 * 384 + half * 8 * 384 + 1 + 128 * C, [[1, 128], [384, 8], [1, 128]])
                P.op("sp", lambda e, stg=stg, src=src: e.dma_start(out=stg, in_=src), writes=[("xin", C)], dma_sem=("ginit", C))
                cbb = cbt[:, l, half * 8:(half + 1) * 8].unsqueeze(2).broadcast_to([128, 8, 128])
                P.op("dve", lambda e, stg=stg, cbb=cbb, l=l, C=C, half=half: e.tensor_tensor(
                    out=Gp[:, l * 2 + C, half * 8:(half + 1) * 8, :], in0=stg, in1=cbb, op=ALU.subtract),
                    reads=[("xin", C), "init_sp"], writes=[("Gp", l)])

    def rmsnorm_to_hT(l, ntok, gcol0):
        bank = nb()
        for kc in range(8):
            if kc % 2 == 0:
                P.op("act", lambda e, kc=kc: e.activation(out=sq[kc % 2][:, 0:ntok], in_=xT[:, kc, 0:ntok], func=AF.Square),
                     reads=[("xT", kc)], writes=[("sq", kc % 2)])
            else:
                P.op("dve", lambda e, kc=kc: e.tensor_tensor(out=sq[kc % 2][:, 0:ntok], in0=xT[:, kc, 0:ntok], in1=xT[:, kc, 0:ntok], op=ALU.mult),
                     reads=[("xT", kc)], writes=[("sq", kc % 2)])
            mm(pb[bank][:, 0:ntok], onesb, sq[kc % 2][:, 0:ntok], kc == 0, kc == 7,
               reads=[("sq", kc % 2), "init_pool"], writes=[("pb", bank)])
        rstd_from_psum(bank, 0, ntok, D)
        for kc in range(8):
            P.op("dve", lambda e, kc=kc: e.scalar_tensor_tensor(
                out=hT[:, kc, 0:ntok], in0=xT[:, kc, 0:ntok], scalar=gvec[:, gcol0 + kc:gcol0 + kc + 1],
                in1=rstd[0][:, 0:ntok], op0=ALU.mult, op1=ALU.mult),
                reads=[("xT", kc), ("rstd", 0), "init_sp"], writes=[("hT", kc)])

    def formB(w2d, c0, nk, ncols, rhs_fn, rhs_res, evac):
        s = wload(w2d, 0, nk, c0, ncols)
        for cbi in range((ncols + 127) // 128):
            m = min(128, ncols - cbi * 128)
            bank = nb()
            for kc in range(nk):
                rhs = rhs_fn(kc)
                mm(pb[bank][0:m, 0:rhs.shape[-1]], W[s][:, kc, cbi * 128:cbi * 128 + m], rhs, kc == 0, kc == nk - 1,
                   reads=[("W", s), rhs_res(kc)], writes=[("pb", bank)])
            evac(bank, cbi)

    def formA(w2d, c0, ncols, ntok, evac):
        s = wload(w2d, 0, 8, c0, ncols)
        for tb in range((ntok + 127) // 128):
            nt_b = min(128, ntok - tb * 128)
            bank = nb()
            for kc in range(8):
                mm(pb[bank][0:nt_b, 0:ncols], hT[:, kc, tb * 128:tb * 128 + nt_b], W[s][:, kc, 0:ncols], kc == 0, kc == 7,
                   reads=[("W", s), ("hT", kc)], writes=[("pb", bank)])
            evac(bank, tb, nt_b)

    def formB_thunks(w2d, c0, nk, ncols, rhs_fn, rhs_res, evac):
        box = {}
        th = []
        ncb = (ncols + 127) // 128
        for cbi in range(ncb):
            def t(cbi=cbi):
                if cbi == 0:
                    box["s"] = wload(w2d, 0, nk, c0, ncols)
                s_ = box["s"]
                m = min(128, ncols - cbi * 128)
                bank = nb()
                for kc in range(nk):
                    rhs = rhs_fn(kc)
                    mm(pb[bank][0:m, 0:rhs.shape[-1]], W[s_][:, kc, cbi * 128:cbi * 128 + m], rhs, kc == 0, kc == nk - 1,
                       reads=[("W", s_), rhs_res(kc)], writes=[("pb", bank)])
                evac(bank, cbi)
            th.append(t)
        return th

    def formA_thunks(w2d, c0, ncols, ntok, evac):
        box = {}
        th = []
        for tb in range((ntok + 127) // 128):
            def t(tb=tb):
                if tb == 0:
                    box["s"] = wload(w2d, 0, 8, c0, ncols)
                s_ = box["s"]
                nt_b = min(128, ntok - tb * 128)
                bank = nb()
                for kc in range(8):
                    mm(pb[bank][0:nt_b, 0:ncols], hT[:, kc, tb * 128:tb * 128 + nt_b], W[s_][:, kc, 0:ncols], kc == 0, kc == 7,
                       reads=[("W", s_), ("hT", kc)], writes=[("pb", bank)])
                evac(bank, tb, nt_b)
            th.append(t)
        return th

    def tile_layer(l, ntok, tok0, xsrc, psrc, is_sample, t, last_tile, prefetched=False, nxt=None):
        nblk = (ntok + 127) // 128
        chunks = [(c * 128, min(128, ntok - c * 128)) for c in range(nblk)]
        w_in_l = w_in[l]
        hT_res = lambda kc: ("hT", kc)
        hT_rhs = lambda kc: hT[:, kc, 0:ntok]

        P.cur_tag = "pload"
        p_stage_xin = (not is_sample) and (not last_tile) and (prefetched or l > 0)
        for tb, (c0, nt) in enumerate(chunks):
            if tb < 2:
                P.op("sp", lambda e, tb=tb, c0=c0, nt=nt: e.dma_start(out=pin[tb % 2][0:nt, :], in_=psrc[l, tok0 + c0:tok0 + c0 + nt, :]),
                     writes=[("pin", tb % 2)], dma_sem=("pin", tb % 2))
            elif p_stage_xin:
                P.op("sp", lambda e, tb=tb, c0=c0, nt=nt: e.dma_start(out=xin[tb % 2][0:nt, 0:256], in_=psrc[l, tok0 + c0:tok0 + c0 + nt, :]),
                     writes=[("xin", tb % 2)], dma_sem=("pinx", tb % 2))
        P.cur_tag = "cache"
        have_prev = False
        if is_sample:
            have_prev = True
            P.op("sp", lambda e: e.dma_start(out=S[l][:], in_=sg[l].rearrange("h d v -> d h v")), writes=[("S", l)], dma_sem=("Sin", l))
            copy_op("act", Sbf[l][:], S[l][:], reads=[("S", l)], writes=[("Sbf", l, hq) for hq in range(4)])
            P.op("pool", lambda e: e.dma_start(out=va[:], in_=ck[l].rearrange("(b p) c -> p b c", p=128)),
                 writes=[("va", tb, g) for tb in range(4) for g in range(4)], dma_sem="ckld")
            for tb in range(4):
                bank = nb()
                for hp in range(8):
                    tp(pbv[bank][:, hp * 128:(hp + 1) * 128], va[:, tb, hp * 128:(hp + 1) * 128], identb,
                       reads=[("va", tb, 0), "init_pool"], writes=[("pb", bank)])
                copy_op(evac_eng(), Kprev[:, :, tb * 128:(tb + 1) * 128], pbv[bank][:].rearrange("p (h t) -> p h t", h=8),
                        reads=[("pb", bank)], writes=["Kprev"])
            Uv = Ub[:].rearrange("p a b -> p (a b)").rearrange("p (t c) -> p t c", c=1024)
            P.op("pool", lambda e: e.dma_start(out=Uv, in_=cv[l].rearrange("(b p) c -> p b c", p=128)),
                 writes=[("Ub", kc) for kc in range(8)], dma_sem="cvld")
            for tb in range(4):
                copy_op("dve", Vprev[:, tb, :].rearrange("p (h e) -> p h e", e=66)[:, :, 0:64],
                        Uv[:, tb, :].rearrange("p (h e) -> p h e", e=64),
                        reads=[("Ub", kc) for kc in range(8)], writes=["Vprev"])
            P.op("sp", lambda e: e.dma_start(out=kbs[l, 0:448, :], in_=ck[l, 64:512, :]), dma_sem="kroll")
            P.op("sp", lambda e: e.dma_start(out=vbs[l, 0:448, :], in_=cv[l, 64:512, :]), dma_sem="vroll")
        elif t > 0:
            have_prev = True
            P.op("sp", lambda e: e.dma_start(out=Kprev[:].rearrange("p a b -> p (a b)"), in_=kspill[l]),
                 reads=[("kspill", l)], writes=["Kprev"], dma_sem="kprev")
            P.op("sp", lambda e: e.dma_start(out=Vprev[:].rearrange("p a b -> p (a b)"), in_=vspill[l]),
                 reads=[("vspill", l)], writes=["Vprev"], dma_sem="vprev")

        P.cur_tag = "xload"
        if l == 0:
            for tb, (c0, nt) in enumerate(chunks):
                if prefetched:
                    xsrc_sb, xres = xstage(tb)
                else:
                    P.op("pool", lambda e, tb=tb, c0=c0, nt=nt: e.dma_start(out=xin[tb % 2][0:nt, :], in_=xsrc[tok0 + c0:tok0 + c0 + nt, :]),
                         writes=[("xin", tb % 2)], dma_sem=("xin", tb % 2))
                    xsrc_sb, xres = xin[tb % 2], [("xin", tb % 2)]
                for g in range(2):
                    bank = nb()
                    for kk in range(4):
                        kc = g * 4 + kk
                        tp(pb[bank][:, kk * 128:kk * 128 + nt], xsrc_sb[0:nt, kc * 128:(kc + 1) * 128], identf[0:nt, 0:nt],
                           reads=xres + ["init_sp"], writes=[("pb", bank)])
                    copy_op("dve", xT[:, g * 4:(g + 1) * 4, c0:c0 + nt],
                            pb[bank][:].rearrange("p (k t) -> p k t", k=4)[:, :, 0:nt],
                            reads=[("pb", bank)], writes=[("xT", g * 4 + kk) for kk in range(4)])

        P.cur_tag = "norm1"
        rmsnorm_to_hT(l, ntok, l * NG)

        P.cur_tag = "inprojA"
        for g in range(2):
            def ev(bank, cbi, g=g):
                h = g * 2 + cbi
                copy_op("act", R1[:, h, 0:ntok], pb[bank][:, 0:ntok], reads=[("pb", bank)], writes=[("R1", h)], scale=128.0 ** -0.5)
            formB(w_in_l, OFF_QA + g * 256, 8, 256, hT_rhs, hT_res, ev)
        for g in range(2):
            def ev(bank, cbi, g=g):
                h = g * 2 + cbi
                copy_op("dve", R1[:, 4 + h, 0:ntok], pb[bank][:, 0:ntok], reads=[("pb", bank)], writes=[("R1", 4 + h)])
            formB(w_in_l, OFF_KA + g * 256, 8, 256, hT_rhs, hT_res, ev)

        def ev_ra(bank, cbi):
            copy_op("dve", raT[0:16, 0:ntok], pb[bank][0:16, 0:ntok], reads=[("pb", bank)], writes=["raT"])
        formB(w_in_l, OFF_RA, 8, 16, hT_rhs, hT_res, ev_ra)
        for g in range(4):
            def ev(bank, tb, nt_b, g=g):
                copy_op(evac_eng(), va[0:nt_b, tb, g * 256:(g + 1) * 256], pb[bank][0:nt_b, 0:256],
                        reads=[("pb", bank)], writes=[("va", tb, g)])
            formA(w_in_l, OFF_VA + g * 256, 256, ntok, ev)

        def gate_silu_group(c_off, g, Ux, Uname):
            def ev(bank, cbi):
                kc = g * 2 + cbi
                P.op("act", lambda e: e.activation(out=Ux[:, kc, 0:ntok], in_=pb[bank][:, 0:ntok], func=AF.Silu),
                     reads=[("pb", bank)], writes=[(Uname, kc)])
            formB(w_in_l, c_off + g * 256, 8, 256, hT_rhs, hT_res, ev)

        for g in range(4):
            gate_silu_group(OFF_GA, g, Ua, "Ua")

        def gla_gen():
            def prep(ci):
                c0, nt = chunks[ci]
                par = ci % 2
                mm(pb[0][0:nt, 0:512], raT[0:17, c0:c0 + nt], wgu[0:17, l, :], True, True,
                   reads=["raT", "init_pool"], writes=[("pb", 0)])
                P.op("act", lambda e: e.activation(out=e1[0:nt, :], in_=pb[0][0:nt, :], func=AF.Exp, scale=-1.0),
                     reads=[("pb", 0)], writes=["e1"])
                P.op("act", lambda e: e.activation(out=e1[0:nt, :], in_=e1[0:nt, :], func=AF.Ln, bias=1.0),
                     reads=["e1"], writes=["e1"])
                yield
                for h in range(4):
                    mm(pb[1][:, h * 128:h * 128 + nt], e1[0:nt, h * 128:(h + 1) * 128], tri[0:nt, 0:nt], True, True,
                       reads=["e1", "init_sp"], writes=[("pb", 1)])
                bB3 = pb[1][:].rearrange("p (h t) -> p h t", h=4)[:, :, 0:nt]
                P.op("act", lambda e: e.activation(out=eb[:, :, 0:nt], in_=bB3, func=AF.Exp), reads=[("pb", 1)], writes=["eb"])
                P.op("act", lambda e: e.activation(out=enb[:, :, 0:nt], in_=bB3, func=AF.Exp, scale=-1.0), reads=[("pb", 1)], writes=["enb"])
                P.op("dve", lambda e: e.tensor_copy(out=ebl[:, par, :].unsqueeze(2), in_=eb[:, :, nt - 1:nt]),
                     reads=["eb"], writes=[("ebl", par)])
                P.op("dve", lambda e: e.tensor_tensor(out=qt2[par][:, :, 0:nt], in0=R1[:, 0:4, c0:c0 + nt], in1=eb[:, :, 0:nt], op=ALU.mult),
                     reads=["eb"] + [("R1", h) for h in range(4)], writes=[("qt", par)])
                P.op("dve", lambda e: e.tensor_tensor(out=kt[:, :, 0:nt], in0=R1[:, 4:8, c0:c0 + nt], in1=enb[:, :, 0:nt], op=ALU.mult),
                     reads=["enb"] + [("R1", 4 + h) for h in range(4)], writes=["kt"])
                yield
                for h in range(4):
                    mm(pb[0][0:nt, h * 128:h * 128 + nt], kt[:, h, 0:nt], qt2[par][:, h, 0:nt], True, True,
                       reads=["kt", ("qt", par)], writes=[("pb", 0)])
                for h in range(4):
                    tp(pbv[1][0:nt, h * 128:(h + 1) * 128], kt[:, h, 0:nt], identb, reads=["kt", "init_pool"], writes=[("pb", 1)])
                cm = cmaskb[0:nt, 0:nt].unsqueeze(1).broadcast_to([nt, 4, nt])
                P.op("dve", lambda e: e.tensor_tensor(
                    out=am2[par][0:nt, :, 0:nt], in0=pb[0][0:nt, :].rearrange("p (h t) -> p h t", h=4)[:, :, 0:nt], in1=cm, op=ALU.mult),
                    reads=[("pb", 0), "init_pool"], writes=[("am", par)])
                copy_op("act", ktTs2[par][0:nt, :], pbv[1][0:nt, 0:512], reads=[("pb", 1)], writes=[("ktTs", par)])
                yield

            def seq(ci):
                c0, nt = chunks[ci]
                par = ci % 2
                bO = [2, 3]
                for h in range(4):
                    bank = bO[h // 2]
                    reg = pb[bank][0:nt, (h % 2) * 256:(h % 2 + 1) * 256]
                    mm(reg, am2[par][0:nt, h, 0:nt], va[0:nt, ci, h * 256:(h + 1) * 256], True, False,
                       reads=[("am", par), ("va", ci, h)], writes=[("pb", bank)])
                    mm(reg, qt2[par][:, h, 0:nt], Sbf[l][:, h, :], False, True, reads=[("qt", par), ("Sbf", l, h)], writes=[("pb", bank)])
                yield
                eblb = ebl[:, par, :].unsqueeze(2).broadcast_to([128, 4, 256])
                for hh in range(2):
                    bS = 4 if hh == 0 else 1
                    for h in (hh * 2, hh * 2 + 1):
                        mm(pb[bS][:, (h % 2) * 256:(h % 2 + 1) * 256], ktTs2[par][0:nt, h * 128:(h + 1) * 128], va[0:nt, ci, h * 256:(h + 1) * 256], True, True,
                           reads=[("ktTs", par), ("va", ci, h)], writes=[("pb", bS)])
                    Sv = S[l][:, hh * 2:(hh + 1) * 2, :].rearrange("p h v -> p (h v)")
                    P.op("dve", lambda e, Sv=Sv, bS=bS: e.tensor_tensor(out=Sv, in0=pb[bS][:, :], in1=Sv, op=ALU.add),
                         reads=[("pb", bS), ("S", l)], writes=[("S", l), ("S", l, hh)])
                    for h in (hh * 2, hh * 2 + 1):
                        P.op("act", lambda e, h=h: e.activation(out=Sbf[l][:, h, :], in_=S[l][:, h, :], func=AF.Copy, scale=ebl[:, par, h:h + 1]),
                             reads=[("S", l, hh), ("ebl", par)], writes=[("Sbf", l, h)])
                P.op("dve", lambda e: e.tensor_tensor(out=S[l][:], in0=S[l][:], in1=eblb, op=ALU.mult),
                     reads=[("ebl", par), ("S", l)], writes=[("S", l), ("S", l, 0), ("S", l, 1)])
                yield
                for h in range(4):
                    bank = bO[h // 2]
                    P.op("act", lambda e, bank=bank, h=h: e.activation(
                        out=junk[0:nt, 0:256], in_=pb[bank][0:nt, (h % 2) * 256:(h % 2 + 1) * 256], func=AF.Square, accum_out=ssm[0:nt, h:h + 1]),
                        reads=[("pb", bank)], writes=["ssm", ("sq", 2)])
                P.op("act", lambda e: e.activation(out=ssm[0:nt, 4:8], in_=ssm[0:nt, 0:4], func=AF.Ln, scale=1.0 / 256, bias=EPS),
                     reads=["ssm"], writes=["ssr"])
                P.op("act", lambda e: e.activation(out=ssm[0:nt, 4:8], in_=ssm[0:nt, 4:8], func=AF.Exp, scale=-0.5),
                     reads=["ssr"], writes=["ssr"])
                for h in range(4):
                    bank = bO[h // 2]
                    P.op("dve", lambda e, bank=bank, h=h: e.scalar_tensor_tensor(
                        out=on[0:nt, h * 256:(h + 1) * 256], in0=pb[bank][0:nt, (h % 2) * 256:(h % 2 + 1) * 256],
                        scalar=ssm[0:nt, 4 + h:5 + h], in1=gA[0:nt, l, :], op0=ALU.mult, op1=ALU.mult),
                        reads=[("pb", bank), "ssr", "init_sp"], writes=["on"])
                yield
                for blk in range(8):
                    tp(pbv[4][:, blk * 128:blk * 128 + nt], on[0:nt, blk * 128:(blk + 1) * 128], identb[0:nt, 0:nt],
                       reads=["on", "init_pool"], writes=[("pb", 4)])
                P.op("dve", lambda e: e.tensor_tensor(
                    out=Ua[:, :, c0:c0 + nt], in0=pbv[4][:].rearrange("p (b t) -> p b t", b=8)[:, :, 0:nt], in1=Ua[:, :, c0:c0 + nt], op=ALU.mult),
                    reads=[("pb", 4)] + [("Ua", kc) for kc in range(8)], writes=[("Ua", kc) for kc in range(8)])
                yield

            n = len(chunks)
            for _ in prep(0):
                yield
            for ci in range(n):
                sg_ = seq(ci)
                pg_ = prep(ci + 1) if ci + 1 < n else None
                for ch in GLA_PAT:
                    if ch == "S":
                        next(sg_)
                        yield
                    elif pg_ is not None:
                        next(pg_)
                        yield

        def qk_flush(keep=0):
            q = st.setdefault("qk_q", [])
            while len(q) > keep:
                q.pop(0)()

        def qk_group(c_off, g, is_q):
            def ev(bank, cbi):
                hp = g * 2 + cbi
                st["qk_n"] = st.get("qk_n", 0) + 1
                i = st["qk_n"] % 3
                P.op("act", lambda e: e.activation(out=sq[i][:, 0:ntok], in_=pb[bank][:, 0:ntok], func=AF.Square),
                     reads=[("pb", bank)], writes=[("sq", i)])
                qk_flush(QK_DEPTH - 1)

                def finish():
                    b2 = nb()
                    mm(pb[b2][:, 0:ntok], bonesb, sq[i][:, 0:ntok], True, True, reads=[("sq", i), "init_pool"], writes=[("pb", b2)])
                    ri = i % 2
                    rstd_from_psum(b2, ri, ntok, 64)
                    if is_q:
                        for par in range(2):
                            ps_ = slice(par * 64, (par + 1) * 64)
                            P.op("dve", lambda e, ps_=ps_, par=par: e.scalar_tensor_tensor(
                                out=Qz[ps_, hp, par, 0:ntok], in0=pb[bank][ps_, 0:ntok], scalar=gq[ps_, l:l + 1],
                                in1=rstd[ri][ps_, 0:ntok], op0=ALU.mult, op1=ALU.mult),
                                reads=[("pb", bank), ("rstd", ri), "gq"], writes=[("Qz", hp)])
                    else:
                        P.op("dve", lambda e: e.scalar_tensor_tensor(
                            out=Kcur[:, hp, 0:ntok], in0=pb[bank][:, 0:ntok], scalar=gvec[:, l * NG + 17:l * NG + 18],
                            in1=rstd[ri][:, 0:ntok], op0=ALU.mult, op1=ALU.mult),
                            reads=[("pb", bank), ("rstd", ri), "init_sp"], writes=[("Kcur", hp)])
                st["qk_q"].append(finish)
            formB(w_in_l, c_off + g * 256, 8, 256, hT_rhs, hT_res, ev)

        def vb_group(g):
            def ev(bank, tb, nt_b):
                copy_op(evac_eng(), Vcur[0:nt_b, tb, :].rearrange("p (h e) -> p h e", e=66)[:, g * 4:(g + 1) * 4, 0:64],
                        pb[bank][0:nt_b, 0:256].rearrange("p (h e) -> p h e", e=64),
                        reads=[("pb", bank)], writes=[("Vcur", tb, g)])
            formA(w_in_l, OFF_VB + g * 256, 256, ntok, ev)

        def mga_group(g):
            def ev(bank, cbi):
                kc = g * 2 + cbi
                P.op("act", lambda e: e.activation(out=M[:, kc, 0:ntok], in_=pb[bank][:, 0:ntok], func=AF.Sigmoid),
                     reads=[("pb", bank)], writes=[("M", kc)])
            formB(w_in_l, OFF_MGA + g * 256, 8, 256, hT_rhs, hT_res, ev)

        def mgb_group(g):
            def ev(bank, cbi):
                kc = g * 2 + cbi
                P.op("act", lambda e: e.activation(out=M2[:, kc * TT:kc * TT + ntok], in_=pb[bank][:, 0:ntok], func=AF.Sigmoid),
                     reads=[("pb", bank)], writes=[("va", kc // 2, (kc % 2) * 2), ("va", kc // 2, (kc % 2) * 2 + 1)])
            formB(w_in_l, OFF_MGB + g * 256, 8, 256, hT_rhs, hT_res, ev)

        Ua_rhs = lambda kc: Ua[:, kc, 0:ntok]
        Ua_res = lambda kc: ("Ua", kc)
        Ub_rhs = lambda kc: Ub[:, kc, 0:ntok]
        Ub_res = lambda kc: ("Ub", kc)

        def bra_group(g):
            def ev(bank, cbi):
                kc = g * 2 + cbi
                P.op("dve", lambda e: e.tensor_tensor(out=M[:, kc, 0:ntok], in0=pb[bank][:, 0:ntok], in1=M[:, kc, 0:ntok], op=ALU.mult),
                     reads=[("pb", bank), ("M", kc)], writes=[("M", kc)])
            formB(wba[l], g * 256, 8, 256, Ua_rhs, Ua_res, ev)

        P.cur_tag = "inprojB"
        for g in range(4):
            qk_group(OFF_QB, g, True)
        for g in range(4):
            qk_group(OFF_KB, g, False)
        qk_flush()
        fill1 = []
        for g in range(4):
            def evv(bank, tb, nt_b, g=g):
                copy_op(evac_eng(), Vcur[0:nt_b, tb, :].rearrange("p (h e) -> p h e", e=66)[:, g * 4:(g + 1) * 4, 0:64],
                        pb[bank][0:nt_b, 0:256].rearrange("p (h e) -> p h e", e=64),
                        reads=[("pb", bank)], writes=[("Vcur", tb, g)])
            fill1.extend(formA_thunks(w_in_l, OFF_VB + g * 256, 256, ntok, evv))
        interleave(gla_gen(), fill1, 7 * nblk, "gla")
        qk_flush()

        P.cur_tag = "kvout"
        if is_sample or last_tile:
            dst = (sgs if is_sample else sgp)[l].rearrange("h d v -> d h v")
            key = ("Sout", l, is_sample)
            P.op("sp", lambda e, dst=dst: e.dma_start(out=dst, in_=S[l][:]), reads=[("S", l)], dma_sem=key)
            out_sems.append(key)
            if is_sample:
                P.op("dve", lambda e: e.memset(S[l][:], 0.0), writes=[("S", l)])
                P.op("dve", lambda e: e.memset(Sbf[l][:], 0.0), writes=[("Sbf", l, hq) for hq in range(4)])

        Kcur_all = [("Kcur", hp) for hp in range(8)]
        Vcur_all = [("Vcur", tb, g) for tb in range(4) for g in range(4)]
        if (not is_sample) and (not last_tile):
            P.op("sp", lambda e: e.dma_start(out=kspill[l], in_=Kcur[:].rearrange("p a b -> p (a b)")),
                 reads=Kcur_all, writes=[("kspill", l)], dma_sem=("kst", l))
            P.op("sp", lambda e: e.dma_start(out=vspill[l], in_=Vcur[:].rearrange("p a b -> p (a b)")),
                 reads=Vcur_all, writes=[("vspill", l)], dma_sem=("vst", l))
        if is_sample or last_tile:
            kdst = kbs if is_sample else kbp
            vdst = vbs if is_sample else vbp
            r0 = 448 if is_sample else 0
            for tb, (c0, nt) in enumerate(chunks):
                bank = nb()
                for hp in range(8):
                    tp(pbv[bank][0:nt, hp * 128:(hp + 1) * 128], Kcur[:, hp, c0:c0 + nt], identb, reads=[("Kcur", hp), "init_pool"], writes=[("pb", bank)])
                copy_op("act", xin[0][0:nt, :], pbv[bank][0:nt, :], reads=[("pb", bank)], writes=[("xin", 0)])
                key = ("kout", l, is_sample)
                P.op("sp", lambda e, c0=c0, nt=nt: e.dma_start(out=kdst[l, r0 + c0:r0 + c0 + nt, :], in_=xin[0][0:nt, :]),
                     reads=[("xin", 0)], dma_sem=key)
                copy_op("dve", xin[1][0:nt, :].rearrange("p (h e) -> p h e", e=64),
                        Vcur[0:nt, tb, :].rearrange("p (h e) -> p h e", e=66)[:, :, 0:64],
                        reads=[("Vcur", tb, g) for g in range(4)], writes=[("xin", 1)])
                key2 = ("vout", l, is_sample)
                P.op("sp", lambda e, c0=c0, nt=nt: e.dma_start(out=vdst[l, r0 + c0:r0 + c0 + nt, :], in_=xin[1][0:nt, :]),
                     reads=[("xin", 1)], dma_sem=key2)
            out_sems.extend([("kout", l, is_sample), ("vout", l, is_sample)])

        def band_gen():
            for pi, (q0, nq) in enumerate(chunks):
                blocks = []
                for C in range(4, -1, -1):
                    b = pi - C
                    if b >= 0:
                        nk = min(128, ntok - b * 128)
                        blocks.append((C, Kcur, "cur", b, nk))
                    elif have_prev:
                        blocks.append((C, Kprev, "prev", 4 + b, 128))

                def qk_exp(hg):
                    for (C, Kb, which, b, nk) in blocks:
                        bank = nb()
                        Kres = (lambda hp: ("Kcur", hp)) if which == "cur" else (lambda hp: "Kprev")
                        if C <= 1:
                            mm(pb[bank][0:nk, 0:4 * nq], Jb[:, 0:nk], Gp[:, l * 2 + C, hg * 4:(hg + 1) * 4, 0:nq], True, False,
                               reads=[("Gp", l), "init_pool"], writes=[("pb", bank)])
                        for h2 in range(2):
                            hp = hg * 2 + h2
                            mm(pb[bank][0:nk, h2 * 2 * nq:(h2 + 1) * 2 * nq], Kb[:, hp, b * 128:b * 128 + nk], Qz[:, hp, :, q0:q0 + nq],
                               C > 1, h2 == 1, reads=[Kres(hp), ("Qz", hp)], writes=[("pb", bank)])
                        src3 = pb[bank][:, 0:4 * nq].rearrange("p (h q) -> p h q", h=4)
                        P.op("act", lambda e, src3=src3, C=C, hg=hg, nk=nk: e.activation(
                            out=PT[hg % 2][C][0:nk, :, 0:nq], in_=src3[0:nk, :, 0:nq], func=AF.Exp),
                            reads=[("pb", bank)], writes=[("PT", hg % 2, C)])
                        if nq == 128 and C == 4:
                            P.op("dve", lambda e, C=C, hg=hg: e.memset(PT[hg % 2][C][0:64, :, 64:128], 0.0), writes=[("PT", hg % 2, C)])
                        elif nq == 128 and C == 0:
                            P.op("dve", lambda e, C=C, hg=hg: e.memset(PT[hg % 2][C][64:128, :, 0:64], 0.0), writes=[("PT", hg % 2, C)])
                        yield

                def pv_norm(hg):
                    ob_bank = nb()
                    for hh in range(4):
                        h = hg * 4 + hh
                        for bi, (C, Kb, which, b, nk) in enumerate(blocks):
                            Vb = Vcur if which == "cur" else Vprev
                            vres = [("Vcur", b, h // 4)] if which == "cur" else ["Vprev"]
                            mm(pb[ob_bank][0:nq, hh * 66:(hh + 1) * 66], PT[hg % 2][C][0:nk, hh, 0:nq], Vb[0:nk, b, h * 66:(h + 1) * 66],
                               bi == 0, bi == len(blocks) - 1, reads=[("PT", hg % 2, C)] + vres, writes=[("pb", ob_bank)])
                    ob3 = pb[ob_bank][0:nq, 0:264].rearrange("p (h e) -> p h e", e=66)
                    P.op("dve", lambda e, ob3=ob3, hg=hg: e.reciprocal(out=rec[0:nq, hg * 4:(hg + 1) * 4].unsqueeze(2), in_=ob3[:, :, 64:65]),
                         reads=[("pb", ob_bank)], writes=["rec"])
                    recb = rec[0:nq, hg * 4:(hg + 1) * 4].unsqueeze(2).broadcast_to([nq, 4, 64])
                    P.op("dve", lambda e, ob3=ob3, hg=hg, recb=recb: e.tensor_tensor(
                        out=on[0:nq, hg * 256:(hg + 1) * 256].rearrange("p (h e) -> p h e", e=64), in0=ob3[:, :, 0:64], in1=recb, op=ALU.mult),
                        reads=[("pb", ob_bank), "rec"], writes=["on"])

                for hg in range(5):
                    if hg < 4:
                        yield from qk_exp(hg)
                    if hg >= 1:
                        pv_norm(hg - 1)
                        yield
                bI = nb()
                for blk in range(8):
                    tp(pbv[bI][:, blk * 128:blk * 128 + nq], on[0:nq, blk * 128:(blk + 1) * 128], identb[0:nq, 0:nq],
                       reads=["on", "init_pool"], writes=[("pb", bI)])
                P.op("dve", lambda e, bI=bI, q0=q0, nq=nq: e.tensor_tensor(
                    out=Ub[:, :, q0:q0 + nq], in0=pbv[bI][:].rearrange("p (b t) -> p b t", b=8)[:, :, 0:nq], in1=Ub[:, :, q0:q0 + nq], op=ALU.mult),
                    reads=[("pb", bI)] + [("Ub", kc) for kc in range(8)], writes=[("Ub", kc) for kc in range(8)])
                yield

        fill2 = []
        for g in range(4):
            def evg(bank, cbi, g=g):
                kc = g * 2 + cbi
                i = kc % 2
                P.op("act", lambda e: e.activation(out=gt[i][:, 0:ntok], in_=pb[bank][:, 0:ntok], func=AF.Tanh, scale=0.5),
                     reads=[("pb", bank)], writes=[("sq", i)])
                P.op("dve", lambda e: e.scalar_tensor_tensor(out=Ub[:, kc, 0:ntok], in0=gt[i][:, 0:ntok], scalar=1.0, in1=pb[bank][:, 0:ntok],
                                                             op0=ALU.add, op1=ALU.mult),
                     reads=[("sq", i), ("pb", bank)], writes=[("Ub", kc)])
            fill2.extend(formB_thunks(w_in_l, OFF_GB + g * 256, 8, 256, hT_rhs, hT_res, evg))
        n_front = len(fill2)
        for g in range(4):
            def evm(bank, cbi, g=g):
                kc = g * 2 + cbi
                P.op("act", lambda e: e.activation(out=M[:, kc, 0:ntok], in_=pb[bank][:, 0:ntok], func=AF.Tanh, scale=0.5),
                     reads=[("pb", bank)], writes=[("M", kc)])
            fill2.extend(formB_thunks(w_in_l, OFF_MGA + g * 256, 8, 256, hT_rhs, hT_res, evm))
        for g in range(4):
            def evb(bank, cbi, g=g):
                kc = g * 2 + cbi
                P.op("act", lambda e: e.activation(out=M2[:, kc * TT:kc * TT + ntok], in_=pb[bank][:, 0:ntok], func=AF.Tanh, scale=0.5),
                     reads=[("pb", bank)], writes=[("va", kc // 2, (kc % 2) * 2), ("va", kc // 2, (kc % 2) * 2 + 1)])
            fill2.extend(formB_thunks(w_in_l, OFF_MGB + g * 256, 8, 256, hT_rhs, hT_res, evb))
        for g in range(4):
            def eva(bank, cbi, g=g):
                kc = g * 2 + cbi
                P.op("dve", lambda e: e.scalar_tensor_tensor(out=M[:, kc, 0:ntok], in0=M[:, kc, 0:ntok], scalar=1.0, in1=pb[bank][:, 0:ntok],
                                                             op0=ALU.add, op1=ALU.mult),
                     reads=[("pb", bank), ("M", kc)], writes=[("M", kc)])
            fill2.extend(formB_thunks(wba[l], g * 256, 8, 256, Ua_rhs, Ua_res, eva))
        ny = 0
        for pi in range(nblk):
            nbk = sum(1 for C in range(5) if (pi - C >= 0) or have_prev)
            ny += 4 * (nbk + 1) + 1
        interleave(band_gen(), fill2, ny, "band", front=n_front)

        P.cur_tag = "branchB"
        for g in range(4):
            def ev(bank, cbi, g=g):
                kc = g * 2 + cbi
                P.op("dve", lambda e: e.scalar_tensor_tensor(out=ftmp[:, 0:ntok], in0=M2[:, kc * TT:kc * TT + ntok], scalar=1.0, in1=pb[bank][:, 0:ntok],
                                                             op0=ALU.add, op1=ALU.mult),
                     reads=[("pb", bank), ("va", kc // 2, (kc % 2) * 2), ("va", kc // 2, (kc % 2) * 2 + 1)], writes=["ftmp"])
                P.op("dve", lambda e: e.scalar_tensor_tensor(out=M[:, kc, 0:ntok], in0=ftmp[:, 0:ntok], scalar=0.5, in1=M[:, kc, 0:ntok],
                                                             op0=ALU.mult, op1=ALU.add),
                     reads=["ftmp", ("M", kc)], writes=[("M", kc)])
            formB(wbb[l], g * 256, 8, 256, Ub_rhs, Ub_res, ev)

        if nxt is not None and l == DEPTH - 1:
            nsrc, ntok0, nntok = nxt
            for tb in range((nntok + 127) // 128):
                nt2 = min(128, nntok - tb * 128)
                dstv, dres = xstage(tb)
                P.op("sp", lambda e, tb=tb, nt2=nt2, dstv=dstv: e.dma_start(out=dstv[0:nt2, :], in_=nsrc[ntok0 + tb * 128:ntok0 + tb * 128 + nt2, :]),
                     writes=dres, dma_sem=("xpre", tb))
        P.cur_tag = "pload"
        for tb, (c0, nt) in enumerate(chunks):
            psb, pres = pin[tb % 2], ("pin", tb % 2)
            if tb >= 2:
                if p_stage_xin:
                    psb, pres = xin[tb % 2], ("xin", tb % 2)
                else:
                    P.op("sp", lambda e, tb=tb, c0=c0, nt=nt: e.dma_start(out=pin[tb % 2][0:nt, :], in_=psrc[l, tok0 + c0:tok0 + c0 + nt, :]),
                         writes=[("pin", tb % 2)], dma_sem=("pin", tb % 2))
            bank = nb()
            for j in range(2):
                tp(pb[bank][:, j * 128:j * 128 + nt], psb[0:nt, j * 128:(j + 1) * 128], identf[0:nt, 0:nt],
                   reads=[pres, "init_sp"], writes=[("pb", bank)])
            copy_op("act", pT[:, 0:2, c0:c0 + nt], pb[bank][:, 0:256].rearrange("p (j t) -> p j t", j=2)[:, :, 0:nt],
                    reads=[("pb", bank)], writes=["pT"])
        P.cur_tag = "wout"
        M_rhs = lambda kc: M[:, kc, 0:ntok]
        M_res = lambda kc: ("M", kc)
        for g in range(4):
            def ev(bank, cbi, g=g):
                kc = g * 2 + cbi
                P.op("dve", lambda e: e.scalar_tensor_tensor(out=xT[:, kc, 0:ntok], in0=pb[bank][:, 0:ntok], scalar=0.5, in1=xT[:, kc, 0:ntok],
                                                             op0=ALU.mult, op1=ALU.add),
                     reads=[("pb", bank), ("xT", kc)], writes=[("xT", kc)])
            formB(wo[l], g * 256, 8, 256, M_rhs, M_res, ev)

        P.cur_tag = "ple"
        rmsnorm_to_hT(l, ntok, l * NG + 8)
        for g in range(4):
            def ev(bank, cbi, g=g):
                kc = g * 2 + cbi
                P.op("act", lambda e: e.activation(out=M[:, kc, 0:ntok], in_=pb[bank][:, 0:ntok], func=AF.Sigmoid),
                     reads=[("pb", bank)], writes=[("M", kc)])
            formB(wpg[l], g * 256, 8, 256, hT_rhs, hT_res, ev)
        pT_rhs = lambda kc: pT[:, kc, 0:ntok]
        pT_res = lambda kc: "pT"
        for g in range(4):
            def ev(bank, cbi, g=g):
                kc = g * 2 + cbi
                P.op("dve", lambda e: e.tensor_tensor(out=ftmp[:, 0:ntok], in0=pb[bank][:, 0:ntok], in1=M[:, kc, 0:ntok], op=ALU.mult),
                     reads=[("pb", bank), ("M", kc)], writes=["ftmp"])
                P.op("dve", lambda e: e.tensor_tensor(out=xT[:, kc, 0:ntok], in0=ftmp[:, 0:ntok], in1=xT[:, kc, 0:ntok], op=ALU.add),
                     reads=["ftmp", ("xT", kc)], writes=[("xT", kc)])
            formB(wp[l], g * 256, 2, 256, pT_rhs, pT_res, ev)

        P.cur_tag = "yout"
        if l == DEPTH - 1:
            ydst = ys if is_sample else yp
            for tb, (c0, nt) in enumerate(chunks):
                buf = tb % 2
                for g in range(2):
                    bank = nb()
                    for kk in range(4):
                        kc = g * 4 + kk
                        tp(pb[bank][0:nt, kk * 128:(kk + 1) * 128], xT[:, kc, c0:c0 + nt], identf, reads=[("xT", kc), "init_sp"], writes=[("pb", bank)])
                    copy_op("dve", xin[buf][0:nt, g * 512:(g + 1) * 512], pb[bank][0:nt, :], reads=[("pb", bank)], writes=[("xin", buf)])
                key = ("yout", buf, is_sample)
                P.op("sp", lambda e, c0=c0, nt=nt, buf=buf: e.dma_start(out=ydst[tok0 + c0:tok0 + c0 + nt, :], in_=xin[buf][0:nt, :]),
                     reads=[("xin", buf)], dma_sem=key)
                if key not in out_sems:
                    out_sems.append(key)

    tiles = [(TT, t * TT, xp, pp, False, t, t == NT - 1) for t in range(NT)]
    if do_sample:
        tiles.append((64, 0, xs, ps, True, 0, False))
    for ti, (ntok_, tok0_, xsrc_, psrc_, iss_, t_, last_) in enumerate(tiles):
        nxt = None
        if ti + 1 < len(tiles):
            n2 = tiles[ti + 1]
            nxt = (n2[2], n2[1], n2[0])
        for l in range(DEPTH):
            tile_layer(l, ntok_, tok0_, xsrc_, psrc_, iss_, t_, last_, prefetched=(ti > 0), nxt=nxt)
    if do_sample:
        out_sems.extend(["kroll", "vroll"])
    P.emit(final_wait_sems=out_sems)
    return nc


_CACHE = {}


def _consts():
    c = np.zeros((128, 6, 128), np.float32)
    c[:, 0, :] = np.eye(128)
    c[:, 1, :] = np.eye(128)[::-1]
    j = np.arange(128)[:, None]
    i = np.arange(128)[None, :]
    c[:, 2, :] = (j <= i)
    c[:, 3, :] = (j <= i) * (-1.0 / 16.0)
    blk = np.zeros((128, 128), np.float32)
    blk[0:64, 0:64] = 1
    blk[64:128, 64:128] = 1
    c[:, 4, :] = blk
    c[:, 5, :] = 1
    return c


def make_in_maps(inp, SEQ, DEPTH, n_cores=8):
    f = lambda a: np.ascontiguousarray(np.asarray(a, dtype=np.float32))
    gv = np.zeros((128, DEPTH * NG), np.float32)
    for l in range(DEPTH):
        gv[:, l * NG:l * NG + 8] = f(inp["norm_g"])[l].reshape(8, 128).T
        gv[:, l * NG + 8:l * NG + 16] = f(inp["ple_norm_g"])[l].reshape(8, 128).T
        gv[:, l * NG + 16] = np.tile(f(inp["q_norm_g"])[l], 2)
        gv[:, l * NG + 17] = np.tile(f(inp["k_norm_g"])[l], 2)
    gA = np.ascontiguousarray(np.broadcast_to(f(inp["gla_norm_g"])[None, :, :], (128, DEPTH, 256)))
    rb = f(inp["rel_bias"])
    tabext = np.ascontiguousarray(np.concatenate([rb, np.repeat(rb[..., -1:], 127, axis=-1)], axis=-1))
    cb = np.ascontiguousarray(np.broadcast_to(rb[None, :, :, 256], (128, DEPTH, 16)))
    wgu = np.ascontiguousarray(np.concatenate([f(inp["w_gate_up"]), f(inp["b_gate"])[:, None, :]], axis=1))
    common = {
        "w_in": f(inp["w_in"]), "wgu": wgu, "wba": f(inp["w_branch_a"]), "wbb": f(inp["w_branch_b"]),
        "wo": f(inp["w_out"]), "wpg": f(inp["w_ple_gate"]), "wp": f(inp["w_ple"]),
        "gv": gv, "gA": gA, "tabext": tabext, "cb": cb, "cst": _consts(),
    }
    xp = f(inp["x_prompt"]); pp = f(inp["p_prompt"]); xs = f(inp["x_sample"]); ps = f(inp["p_sample"])
    sg = f(inp["state_gla"]); ck = f(inp["cache_band_k"]); cv = f(inp["cache_band_v"])
    maps = []
    for c in range(n_cores):
        b = c % xp.shape[0]
        m = dict(common)
        m["xp"] = np.ascontiguousarray(xp[b])
        m["pp"] = np.ascontiguousarray(pp[:, b])
        m["xs"] = np.ascontiguousarray(xs[c])
        m["ps"] = np.ascontiguousarray(ps[:, c])
        m["sg"] = np.ascontiguousarray(sg[:, c])
        m["ck"] = np.ascontiguousarray(ck[:, c].reshape(DEPTH, 512, 1024))
        m["cv"] = np.ascontiguousarray(cv[:, c].reshape(DEPTH, 512, 1024))
        maps.append(m)
    return maps


def run(inp, SEQ, DEPTH):
    key = (SEQ, DEPTH)
    if key not in _CACHE:
        _CACHE[key] = build_program(SEQ, DEPTH)
    nc = _CACHE[key]
    maps = make_in_maps(inp, SEQ, DEPTH)
    res = run_bass_kernel_spmd(nc, maps, core_ids=list(range(8)))
    r = res.results
    B = np.asarray(inp["x_prompt"]).shape[0]
    y_prompt = np.stack([r[b]["yp"] for b in range(B)])
    y_sample = np.stack([r[c]["ys"] for c in range(8)])
    sgp = np.stack([r[b]["sgp"] for b in range(B)], axis=1)
    kbp = np.stack([r[b]["kbp"] for b in range(B)], axis=1).reshape(DEPTH, B, 512, 16, 64)
    vbp = np.stack([r[b]["vbp"] for b in range(B)], axis=1).reshape(DEPTH, B, 512, 16, 64)
    sgs = np.stack([r[c]["sgs"] for c in range(8)], axis=1)
    kbs = np.stack([r[c]["kbs"] for c in range(8)], axis=1).reshape(DEPTH, 8, 512, 16, 64)
    vbs = np.stack([r[c]["vbs"] for c in range(8)], axis=1).reshape(DEPTH, 8, 512, 16, 64)
    return (y_prompt, y_sample, sgp, kbp, vbp, sgs, kbs, vbs)


def kernel(**inputs):
    return run(inputs, 4096, 2)
```

```python
import numpy as np
import concourse.bass as bass
import concourse.mybir as mybir
from concourse.bass_utils import run_bass_kernel_spmd

F32 = mybir.dt.float32
BF16 = mybir.dt.bfloat16
AF = mybir.ActivationFunctionType
ALU = mybir.AluOpType

D = 1024
N_IN = 9232
OFF_QA, OFF_KA, OFF_VA, OFF_RA, OFF_GA = 0, 512, 1024, 2048, 2064
OFF_QB, OFF_KB, OFF_VB, OFF_GB, OFF_MGA, OFF_MGB = 3088, 4112, 5136, 6160, 7184, 8208
EPS = 1e-6
TT = 512
QK_DEPTH = 1
import os
GLA_PAT = 'SPSSPSP'
NG = 18

ENGS = ("pe", "act", "dve", "pool", "sp")


class Op:
    __slots__ = ("eng", "fn", "deps", "inc", "count", "dma_sem", "dma_val", "tag", "meta")

    def __init__(self, eng, fn):
        self.eng = eng
        self.fn = fn
        self.deps = []
        self.inc = False
        self.count = 0
        self.dma_sem = None
        self.dma_val = 0


class Prog:
    def __init__(self, nc):
        self.nc = nc
        self.ops = {e: [] for e in ENGS}
        self.last_w = {}
        self.readers = {}
        self.dma_cnt = {}
        self.cur_tag = ""

    def op(self, eng, fn, reads=(), writes=(), dma_sem=None):
        o = Op(eng, fn)
        o.tag = self.cur_tag
        o.meta = None
        is_dma = dma_sem is not None
        deps = {}
        for r in reads:
            w = self.last_w.get(r)
            if w is not None:
                deps[id(w)] = (w, "raw")
        for r in writes:
            w = self.last_w.get(r)
            if w is not None and id(w) not in deps:
                deps[id(w)] = (w, "waw")
            for rd in self.readers.get(r, ()):
                if id(rd) not in deps:
                    deps[id(rd)] = (rd, "war")
        for d, kind in deps.values():
            d_is_dma = d.dma_sem is not None
            if d.eng == eng and not d_is_dma and not is_dma:
                if eng == "pe" or kind != "raw":
                    continue
            if not d_is_dma:
                d.inc = True
            o.deps.append(d)
        for r in reads:
            self.readers.setdefault(r, []).append(o)
        for r in writes:
            self.last_w[r] = o
            self.readers[r] = []
        if is_dma:
            o.dma_sem = dma_sem
            c = self.dma_cnt.get(dma_sem, 0) + 16
            self.dma_cnt[dma_sem] = c
            o.dma_val = c
        self.ops[eng].append(o)
        return o

    def emit(self, final_wait_sems=()):
        nc = self.nc
        esem = {e: nc.alloc_semaphore("es_" + e) for e in ENGS}
        dsem = {}
        for i, k in enumerate(self.dma_cnt):
            dsem[k] = nc.alloc_semaphore("ds%d" % i)
        for e in ENGS:
            c = 0
            for o in self.ops[e]:
                if o.dma_sem is None and o.inc:
                    c += 1
                    o.count = c
        ops = self.ops
        dma_cnt = self.dma_cnt

        def run(e, eng):
            known = {}
            for o in ops[e]:
                need = {}
                for d in o.deps:
                    if d.dma_sem is not None:
                        key = ("d", d.dma_sem)
                        val = d.dma_val
                    else:
                        key = ("e", d.eng)
                        val = d.count
                    if val > need.get(key, 0):
                        need[key] = val
                for key, val in need.items():
                    if known.get(key, 0) >= val:
                        continue
                    known[key] = val
                    sem = dsem[key[1]] if key[0] == "d" else esem[key[1]]
                    eng.wait_ge(sem, val)
                ins = o.fn(eng)
                if o.dma_sem is not None:
                    ins.then_inc(dsem[o.dma_sem], 16)
                elif o.inc:
                    ins.then_inc(esem[e], 1)
            if e == "sp":
                for k in final_wait_sems:
                    eng.wait_ge(dsem[k], dma_cnt[k])

        with nc.Block() as block:
            @block.tensor
            def _(eng):
                run("pe", eng)

            @block.scalar
            def _(eng):
                run("act", eng)

            @block.vector
            def _(eng):
                run("dve", eng)

            @block.gpsimd
            def _(eng):
                run("pool", eng)

            @block.sync
            def _(eng):
                run("sp", eng)


def build_program(SEQ, DEPTH, do_sample=True):
    nc = bass.Bass("TRN2", target_bir_lowering=False)
    P = Prog(nc)
    NT = SEQ // TT

    def din(name, shape):
        return nc.dram_tensor(name, list(shape), F32, kind="ExternalInput").ap()

    def dout(name, shape):
        return nc.dram_tensor(name, list(shape), F32, kind="ExternalOutput").ap()

    xp = din("xp", [SEQ, D]); pp = din("pp", [DEPTH, SEQ, 256])
    xs = din("xs", [64, D]); ps = din("ps", [DEPTH, 64, 256])
    sg = din("sg", [DEPTH, 4, 128, 256])
    ck = din("ck", [DEPTH, 512, D]); cv = din("cv", [DEPTH, 512, D])
    w_in = din("w_in", [DEPTH, D, N_IN])
    wgu_d = din("wgu", [DEPTH, 17, 512])
    wba = din("wba", [DEPTH, D, D]); wbb = din("wbb", [DEPTH, D, D])
    wo = din("wo", [DEPTH, D, D]); wpg = din("wpg", [DEPTH, D, D])
    wp = din("wp", [DEPTH, 256, D])
    gv_d = din("gv", [128, DEPTH * NG])
    gA_d = din("gA", [128, DEPTH, 256])
    tabext = nc.dram_tensor("tabext", [DEPTH, 16, 384], F32, kind="ExternalInput")
    cb_d = din("cb", [128, DEPTH, 16])
    cst_d = din("cst", [128, 6, 128])

    yp = dout("yp", [SEQ, D]); ys = dout("ys", [64, D])
    sgp = dout("sgp", [DEPTH, 4, 128, 256])
    kbp = dout("kbp", [DEPTH, 512, D]); vbp = dout("vbp", [DEPTH, 512, D])
    sgs = dout("sgs", [DEPTH, 4, 128, 256])
    kbs = dout("kbs", [DEPTH, 512, D]); vbs = dout("vbs", [DEPTH, 512, D])

    kspill = [nc.dram_tensor("kspill%d" % l, [128, 8 * 512], BF16, kind="Internal").ap() for l in range(DEPTH)]
    vspill = [nc.dram_tensor("vspill%d" % l, [128, 4 * 1056], BF16, kind="Internal").ap() for l in range(DEPTH)]

    def sb(name, shape, dt):
        return nc.alloc_sbuf_tensor("s_" + name, list(shape), dt)

    xT = sb("xT", [128, 8, TT], F32)
    hT = sb("hT", [128, 8, TT], BF16)
    R1 = sb("R1", [128, 8, TT], BF16)
    Qz = sb("Qz", [128, 8, 2, TT], BF16)
    va = sb("va", [128, 4, 1024], BF16)
    M2 = va[:].rearrange("p a b -> p (a b)")
    Ua = sb("Ua", [128, 8, TT], BF16)
    Ub = sb("Ub", [128, 8, TT], BF16)
    M = sb("M", [128, 8, TT], BF16)
    Kcur = sb("Kcur", [128, 8, TT], BF16)
    Kprev = sb("Kprev", [128, 8, TT], BF16)
    Vcur = sb("Vcur", [128, 4, 1056], BF16)
    Vprev = sb("Vprev", [128, 4, 1056], BF16)
    S = [sb("S%d" % l, [128, 4, 256], F32) for l in range(DEPTH)]
    Sbf = [sb("Sbf%d" % l, [128, 4, 256], BF16) for l in range(DEPTH)]
    NW = 3
    W = [sb("W%d" % i, [128, 8, 256], BF16) for i in range(NW)]
    Gp = sb("Gp", [128, DEPTH * 2, 16, 128], BF16)
    PT = [[sb("PT%d_%d" % (i, k), [128, 4, 128], BF16) for i in range(5)] for k in range(2)]
    e1 = sb("e1", [128, 512], F32)
    eb = sb("eb", [128, 4, 128], F32)
    enb = sb("enb", [128, 4, 128], F32)
    qt = sb("qt", [128, 4, 128], BF16)
    kt = sb("kt", [128, 4, 128], BF16)
    am = sb("am", [128, 4, 128], BF16)
    ktTs = sb("ktTs", [128, 512], BF16)
    on = sb("on", [128, 1024], BF16)
    ssm = sb("ssm", [128, 8], F32)
    rec = sb("rec", [128, 16], F32)
    sq = [sb("sq%d" % i, [128, TT], BF16) for i in range(3)]
    rstd = [sb("rstd%d" % i, [128, TT], F32) for i in range(2)]
    gt = [sq[0], sq[1]]
    junk = sq[2]
    qt2 = [qt, sb("qtB", [128, 4, 128], BF16)]
    am2 = [am, sb("amB", [128, 4, 128], BF16)]
    ktTs2 = [ktTs, sb("ktTsB", [128, 512], BF16)]
    ebl = sb("ebl", [128, 2, 4], F32)
    ftmp = sb("ftmp", [128, TT], F32)
    xin = [sb("xin%d" % i, [128, 1024], F32) for i in range(2)]
    pin = [sb("pin%d" % i, [128, 256], F32) for i in range(2)]
    pT = sb("pT", [128, 2, TT], BF16)
    cst = sb("cst", [128, 2, 128], F32)
    cstb = sb("cstb", [128, 6, 128], BF16)
    gvec = sb("gvec", [128, DEPTH * NG], F32)
    gq = sb("gq", [128, DEPTH], F32)
    gA = sb("gA", [128, DEPTH, 256], F32)
    cbt = sb("cbt", [128, DEPTH, 16], F32)
    wgu = sb("wgu", [32, DEPTH, 512], BF16)
    raT = sb("raT", [32, TT], BF16)
    pb = [nc.alloc_psum_tensor("pb%d" % i, [128, 512], F32) for i in range(8)]
    pbv = [b.bitcast(BF16) for b in pb]
    UaF = Ua[:].rearrange("p a b -> p (a b)").bitcast(F32)
    UbF = Ub[:].rearrange("p a b -> p (a b)").bitcast(F32)
    def xstage(tb):
        v = UaF if tb < 2 else UbF
        return v[:, (tb % 2) * 1024:(tb % 2 + 1) * 1024], [(("Ua" if tb < 2 else "Ub"), (tb % 2) * 4 + k) for k in range(4)]

    identf = cst[:, 0, :]
    tri = cst[:, 1, :]
    identb = cstb[:, 0, :]
    Jb = cstb[:, 1, :]
    cmaskb = cstb[:, 2, :]
    bonesb = cstb[:, 4, :]
    onesb = cstb[:, 5, :]

    st = {"bank": 0, "ws": 0, "ev": 0, "fb": 0, "mb": 0, "split": False, "kind": "mix"}
    out_sems = []

    def nb(kind=None):
        if st["split"]:
            if (kind or st["kind"]) == "fill":
                b = 5 + st["fb"] % 3
                st["fb"] += 1
            else:
                b = st["mb"] % 5
                st["mb"] += 1
            return b
        b = st["bank"]
        st["bank"] = (b + 1) % 8
        return b

    def interleave(gen, fillers, n_yields, name="", front=0):
        st["split"] = True
        P.cur_tag = "mix:" + name
        nf = len(fillers)
        done = 0
        y = 0
        for _ in gen:
            y += 1
            if y <= front:
                want = min(nf, y)
            else:
                want = min(nf, max(front, front + ((y - front) * (nf - front) + (n_yields - front) - 1) // max(1, n_yields - front)))
            while done < want:
                st["kind"] = "fill"
                P.cur_tag = "fill:" + name
                fillers[done]()
                P.cur_tag = "mix:" + name
                st["kind"] = "mix"
                done += 1
        while done < nf:
            st["kind"] = "fill"
            P.cur_tag = "fill:" + name
            fillers[done]()
            st["kind"] = "mix"
            done += 1
        st["split"] = False

    def evac_eng():
        st["ev"] += 1
        return "act" if st["ev"] % 2 == 0 else "dve"

    def copy_op(eng, out, in_, reads, writes, scale=None):
        if eng == "act":
            if scale is None:
                P.op("act", lambda e: e.activation(out=out, in_=in_, func=AF.Copy), reads, writes)
            else:
                P.op("act", lambda e: e.activation(out=out, in_=in_, func=AF.Copy, scale=scale), reads, writes)
        else:
            if scale is None:
                P.op("dve", lambda e: e.tensor_copy(out=out, in_=in_), reads, writes)
            else:
                P.op("dve", lambda e: e.tensor_scalar(out=out, in0=in_, scalar1=scale, scalar2=None, op0=ALU.mult), reads, writes)

    def mm(out, lhsT, rhs, start, stop, reads, writes):
        o = P.op("pe", lambda e: e.matmul(out, lhsT=lhsT, rhs=rhs, start=start, stop=stop, skip_group_check=True), reads, writes)
        o.meta = int(np.prod(rhs.shape[1:]))

    def tp(out, in_, ident, reads, writes):
        o = P.op("pe", lambda e: e.transpose(out, in_, ident), reads, writes)
        o.meta = int(np.prod(ident.shape[1:]))

    wcache = {}

    def wload(w2d, r0, nk, c0, ncols):
        s = st["ws"] % NW
        st["ws"] += 1
        dst = W[s][:, 0:nk, 0:ncols]
        key = (w2d.name, int(w2d.offset), r0, nk, c0, ncols)
        if key not in wcache:
            src = w2d[r0:r0 + nk * 128, c0:c0 + ncols].rearrange("(k p) c -> p k c", p=128)
            P.op("pool", lambda e: e.dma_start(out=dst, in_=src), writes=[("W", s)], dma_sem=("w", s))
            sc = nc.dram_tensor("wsc%d" % len(wcache), [128, nk * ncols], BF16, kind="Internal").ap()
            wcache[key] = sc
            P.op("sp", lambda e: e.dma_start(out=sc.rearrange("p (k c) -> p k c", k=nk), in_=dst),
                 reads=[("W", s)], writes=[("wsc", key)], dma_sem=("wst", s))
        else:
            sc = wcache[key]
            P.op("pool", lambda e: e.dma_start(out=dst, in_=sc.rearrange("p (k c) -> p k c", k=nk)),
                 reads=[("wsc", key)], writes=[("W", s)], dma_sem=("w", s))
        return s

    def rstd_from_psum(bank, i, ntok, n):
        P.op("act", lambda e: e.activation(out=rstd[i][:, 0:ntok], in_=pb[bank][:, 0:ntok], func=AF.Ln, scale=1.0 / n, bias=EPS),
             reads=[("pb", bank)], writes=[("rstd", i)])
        P.op("act", lambda e: e.activation(out=rstd[i][:, 0:ntok], in_=rstd[i][:, 0:ntok], func=AF.Exp, scale=-0.5),
             reads=[("rstd", i)], writes=[("rstd", i)])

    P.op("sp", lambda e: e.dma_start(out=cst[:, 0, :], in_=cst_d[:, 0, :]), writes=["init_sp"], dma_sem="init_sp")
    P.op("sp", lambda e: e.dma_start(out=cst[:, 1, :], in_=cst_d[:, 3, :]), writes=["init_sp"], dma_sem="init_sp")
    P.op("sp", lambda e: e.dma_start(out=gvec[:], in_=gv_d), writes=["init_sp"], dma_sem="init_sp")
    P.op("sp", lambda e: e.dma_start(out=gA[:], in_=gA_d), writes=["init_sp"], dma_sem="init_sp")
    P.op("sp", lambda e: e.dma_start(out=cbt[:], in_=cb_d), writes=["init_sp"], dma_sem="init_sp")
    P.op("pool", lambda e: e.dma_start(out=cstb[:], in_=cst_d), writes=["init_pool"], dma_sem="init_pool")
    P.op("pool", lambda e: e.dma_start(out=wgu[0:17, :, :], in_=wgu_d.rearrange("l k c -> k l c")), writes=["init_pool"], dma_sem="init_pool")
    P.op("dve", lambda e: e.memset(raT[:], 1.0), writes=["raT"])
    P.op("dve", lambda e: e.memset(Vcur[:], 1.0), writes=[("Vcur", tb, g) for tb in range(4) for g in range(4)])
    P.op("dve", lambda e: e.memset(Vprev[:], 1.0), writes=["Vprev"])
    P.op("dve", lambda e: e.memset(Qz[:], 0.0), writes=[("Qz", hp) for hp in range(8)])
    for k in range(2):
        for i in range(5):
            P.op("dve", lambda e, i=i, k=k: e.memset(PT[k][i][:], 0.0), writes=[("PT", k, i)])
    for l in range(DEPTH):
        P.op("dve", lambda e, l=l: e.memset(S[l][:], 0.0), writes=[("S", l)])
        P.op("dve", lambda e, l=l: e.memset(Sbf[l][:], 0.0), writes=[("Sbf", l, hq) for hq in range(4)])
        P.op("dve", lambda e, l=l: e.tensor_scalar(out=gq[:, l:l + 1], in0=gvec[:, l * NG + 16:l * NG + 17], scalar1=0.125, scalar2=None, op0=ALU.mult),
             reads=["init_sp"], writes=["gq"])
        for C in range(2):
            stg = xin[C][:].rearrange("p (h i) -> p h i", h=8)
            for half in range(2):
                src = bass.AP(tabext, l * 16 * 384 + half * 8 * 384 + 1 + 128 * C, [[1, 128], [384, 8], [1, 128]])
                P.op("sp", lambda e, stg=stg, src=src: e.dma_start(out=stg, in_=src), writes=[("xin", C)], dma_sem=("ginit", C))
                cbb = cbt[:, l, half * 8:(half + 1) * 8].unsqueeze(2).broadcast_to([128, 8, 128])
                P.op("dve", lambda e, stg=stg, cbb=cbb, l=l, C=C, half=half: e.tensor_tensor(
                    out=Gp[:, l * 2 + C, half * 8:(half + 1) * 8, :], in0=stg, in1=cbb, op=ALU.subtract),
                    reads=[("xin", C), "init_sp"], writes=[("Gp", l)])

    def rmsnorm_to_hT(l, ntok, gcol0):
        bank = nb()
        for kc in range(8):
            if kc % 2 == 0:
                P.op("act", lambda e, kc=kc: e.activation(out=sq[kc % 2][:, 0:ntok], in_=xT[:, kc, 0:ntok], func=AF.Square),
                     reads=[("xT", kc)], writes=[("sq", kc % 2)])
            else:
                P.op("dve", lambda e, kc=kc: e.tensor_tensor(out=sq[kc % 2][:, 0:ntok], in0=xT[:, kc, 0:ntok], in1=xT[:, kc, 0:ntok], op=ALU.mult),
                     reads=[("xT", kc)], writes=[("sq", kc % 2)])
            mm(pb[bank][:, 0:ntok], onesb, sq[kc % 2][:, 0:ntok], kc == 0, kc == 7,
               reads=[("sq", kc % 2), "init_pool"], writes=[("pb", bank)])
        rstd_from_psum(bank, 0, ntok, D)
        for kc in range(8):
            P.op("dve", lambda e, kc=kc: e.scalar_tensor_tensor(
                out=hT[:, kc, 0:ntok], in0=xT[:, kc, 0:ntok], scalar=gvec[:, gcol0 + kc:gcol0 + kc + 1],
                in1=rstd[0][:, 0:ntok], op0=ALU.mult, op1=ALU.mult),
                reads=[("xT", kc), ("rstd", 0), "init_sp"], writes=[("hT", kc)])

    def formB(w2d, c0, nk, ncols, rhs_fn, rhs_res, evac):
        s = wload(w2d, 0, nk, c0, ncols)
        for cbi in range((ncols + 127) // 128):
            m = min(128, ncols - cbi * 128)
            bank = nb()
            for kc in range(nk):
                rhs = rhs_fn(kc)
                mm(pb[bank][0:m, 0:rhs.shape[-1]], W[s][:, kc, cbi * 128:cbi * 128 + m], rhs, kc == 0, kc == nk - 1,
                   reads=[("W", s), rhs_res(kc)], writes=[("pb", bank)])
            evac(bank, cbi)

    def formA(w2d, c0, ncols, ntok, evac):
        s = wload(w2d, 0, 8, c0, ncols)
        for tb in range((ntok + 127) // 128):
            nt_b = min(128, ntok - tb * 128)
            bank = nb()
            for kc in range(8):
                mm(pb[bank][0:nt_b, 0:ncols], hT[:, kc, tb * 128:tb * 128 + nt_b], W[s][:, kc, 0:ncols], kc == 0, kc == 7,
                   reads=[("W", s), ("hT", kc)], writes=[("pb", bank)])
            evac(bank, tb, nt_b)

    def formB_thunks(w2d, c0, nk, ncols, rhs_fn, rhs_res, evac):
        box = {}
        th = []
        ncb = (ncols + 127) // 128
        for cbi in range(ncb):
            def t(cbi=cbi):
                if cbi == 0:
                    box["s"] = wload(w2d, 0, nk, c0, ncols)
                s_ = box["s"]
                m = min(128, ncols - cbi * 128)
                bank = nb()
                for kc in range(nk):
                    rhs = rhs_fn(kc)
                    mm(pb[bank][0:m, 0:rhs.shape[-1]], W[s_][:, kc, cbi * 128:cbi * 128 + m], rhs, kc == 0, kc == nk - 1,
                       reads=[("W", s_), rhs_res(kc)], writes=[("pb", bank)])
                evac(bank, cbi)
            th.append(t)
        return th

    def formA_thunks(w2d, c0, ncols, ntok, evac):
        box = {}
        th = []
        for tb in range((ntok + 127) // 128):
            def t(tb=tb):
                if tb == 0:
                    box["s"] = wload(w2d, 0, 8, c0, ncols)
                s_ = box["s"]
                nt_b = min(128, ntok - tb * 128)
                bank = nb()
                for kc in range(8):
                    mm(pb[bank][0:nt_b, 0:ncols], hT[:, kc, tb * 128:tb * 128 + nt_b], W[s_][:, kc, 0:ncols], kc == 0, kc == 7,
                       reads=[("W", s_), ("hT", kc)], writes=[("pb", bank)])
                evac(bank, tb, nt_b)
            th.append(t)
        return th

    def tile_layer(l, ntok, tok0, xsrc, psrc, is_sample, t, last_tile, prefetched=False, nxt=None):
        nblk = (ntok + 127) // 128
        chunks = [(c * 128, min(128, ntok - c * 128)) for c in range(nblk)]
        w_in_l = w_in[l]
        hT_res = lambda kc: ("hT", kc)
        hT_rhs = lambda kc: hT[:, kc, 0:ntok]

        P.cur_tag = "pload"
        p_stage_xin = (not is_sample) and (not last_tile) and (prefetched or l > 0)
        for tb, (c0, nt) in enumerate(chunks):
            if tb < 2:
                P.op("sp", lambda e, tb=tb, c0=c0, nt=nt: e.dma_start(out=pin[tb % 2][0:nt, :], in_=psrc[l, tok0 + c0:tok0 + c0 + nt, :]),
                     writes=[("pin", tb % 2)], dma_sem=("pin", tb % 2))
            elif p_stage_xin:
                P.op("sp", lambda e, tb=tb, c0=c0, nt=nt: e.dma_start(out=xin[tb % 2][0:nt, 0:256], in_=psrc[l, tok0 + c0:tok0 + c0 + nt, :]),
                     writes=[("xin", tb % 2)], dma_sem=("pinx", tb % 2))
        P.cur_tag = "cache"
        have_prev = False
        if is_sample:
            have_prev = True
            P.op("sp", lambda e: e.dma_start(out=S[l][:], in_=sg[l].rearrange("h d v -> d h v")), writes=[("S", l)], dma_sem=("Sin", l))
            copy_op("act", Sbf[l][:], S[l][:], reads=[("S", l)], writes=[("Sbf", l, hq) for hq in range(4)])
            P.op("pool", lambda e: e.dma_start(out=va[:], in_=ck[l].rearrange("(b p) c -> p b c", p=128)),
                 writes=[("va", tb, g) for tb in range(4) for g in range(4)], dma_sem="ckld")
            for tb in range(4):
                bank = nb()
                for hp in range(8):
                    tp(pbv[bank][:, hp * 128:(hp + 1) * 128], va[:, tb, hp * 128:(hp + 1) * 128], identb,
                       reads=[("va", tb, 0), "init_pool"], writes=[("pb", bank)])
                copy_op(evac_eng(), Kprev[:, :, tb * 128:(tb + 1) * 128], pbv[bank][:].rearrange("p (h t) -> p h t", h=8),
                        reads=[("pb", bank)], writes=["Kprev"])
            Uv = Ub[:].rearrange("p a b -> p (a b)").rearrange("p (t c) -> p t c", c=1024)
            P.op("pool", lambda e: e.dma_start(out=Uv, in_=cv[l].rearrange("(b p) c -> p b c", p=128)),
                 writes=[("Ub", kc) for kc in range(8)], dma_sem="cvld")
            for tb in range(4):
                copy_op("dve", Vprev[:, tb, :].rearrange("p (h e) -> p h e", e=66)[:, :, 0:64],
                        Uv[:, tb, :].rearrange("p (h e) -> p h e", e=64),
                        reads=[("Ub", kc) for kc in range(8)], writes=["Vprev"])
            P.op("sp", lambda e: e.dma_start(out=kbs[l, 0:448, :], in_=ck[l, 64:512, :]), dma_sem="kroll")
            P.op("sp", lambda e: e.dma_start(out=vbs[l, 0:448, :], in_=cv[l, 64:512, :]), dma_sem="vroll")
        elif t > 0:
            have_prev = True
            P.op("sp", lambda e: e.dma_start(out=Kprev[:].rearrange("p a b -> p (a b)"), in_=kspill[l]),
                 reads=[("kspill", l)], writes=["Kprev"], dma_sem="kprev")
            P.op("sp", lambda e: e.dma_start(out=Vprev[:].rearrange("p a b -> p (a b)"), in_=vspill[l]),
                 reads=[("vspill", l)], writes=["Vprev"], dma_sem="vprev")

        P.cur_tag = "xload"
        if l == 0:
            for tb, (c0, nt) in enumerate(chunks):
                if prefetched:
                    xsrc_sb, xres = xstage(tb)
                else:
                    P.op("pool", lambda e, tb=tb, c0=c0, nt=nt: e.dma_start(out=xin[tb % 2][0:nt, :], in_=xsrc[tok0 + c0:tok0 + c0 + nt, :]),
                         writes=[("xin", tb % 2)], dma_sem=("xin", tb % 2))
                    xsrc_sb, xres = xin[tb % 2], [("xin", tb % 2)]
                for g in range(2):
                    bank = nb()
                    for kk in range(4):
                        kc = g * 4 + kk
                        tp(pb[bank][:, kk * 128:kk * 128 + nt], xsrc_sb[0:nt, kc * 128:(kc + 1) * 128], identf[0:nt, 0:nt],
                           reads=xres + ["init_sp"], writes=[("pb", bank)])
                    copy_op("dve", xT[:, g * 4:(g + 1) * 4, c0:c0 + nt],
                            pb[bank][:].rearrange("p (k t) -> p k t", k=4)[:, :, 0:nt],
                            reads=[("pb", bank)], writes=[("xT", g * 4 + kk) for kk in range(4)])

        P.cur_tag = "norm1"
        rmsnorm_to_hT(l, ntok, l * NG)

        P.cur_tag = "inprojA"
        for g in range(2):
            def ev(bank, cbi, g=g):
                h = g * 2 + cbi
                copy_op("act", R1[:, h, 0:ntok], pb[bank][:, 0:ntok], reads=[("pb", bank)], writes=[("R1", h)], scale=128.0 ** -0.5)
            formB(w_in_l, OFF_QA + g * 256, 8, 256, hT_rhs, hT_res, ev)
        for g in range(2):
            def ev(bank, cbi, g=g):
                h = g * 2 + cbi
                copy_op("dve", R1[:, 4 + h, 0:ntok], pb[bank][:, 0:ntok], reads=[("pb", bank)], writes=[("R1", 4 + h)])
            formB(w_in_l, OFF_KA + g * 256, 8, 256, hT_rhs, hT_res, ev)

        def ev_ra(bank, cbi):
            copy_op("dve", raT[0:16, 0:ntok], pb[bank][0:16, 0:ntok], reads=[("pb", bank)], writes=["raT"])
        formB(w_in_l, OFF_RA, 8, 16, hT_rhs, hT_res, ev_ra)
        for g in range(4):
            def ev(bank, tb, nt_b, g=g):
                copy_op(evac_eng(), va[0:nt_b, tb, g * 256:(g + 1) * 256], pb[bank][0:nt_b, 0:256],
                        reads=[("pb", bank)], writes=[("va", tb, g)])
            formA(w_in_l, OFF_VA + g * 256, 256, ntok, ev)

        def gate_silu_group(c_off, g, Ux, Uname):
            def ev(bank, cbi):
                kc = g * 2 + cbi
                P.op("act", lambda e: e.activation(out=Ux[:, kc, 0:ntok], in_=pb[bank][:, 0:ntok], func=AF.Silu),
                     reads=[("pb", bank)], writes=[(Uname, kc)])
            formB(w_in_l, c_off + g * 256, 8, 256, hT_rhs, hT_res, ev)

        for g in range(4):
            gate_silu_group(OFF_GA, g, Ua, "Ua")

        def gla_gen():
            def prep(ci):
                c0, nt = chunks[ci]
                par = ci % 2
                mm(pb[0][0:nt, 0:512], raT[0:17, c0:c0 + nt], wgu[0:17, l, :], True, True,
                   reads=["raT", "init_pool"], writes=[("pb", 0)])
                P.op("act", lambda e: e.activation(out=e1[0:nt, :], in_=pb[0][0:nt, :], func=AF.Exp, scale=-1.0),
                     reads=[("pb", 0)], writes=["e1"])
                P.op("act", lambda e: e.activation(out=e1[0:nt, :], in_=e1[0:nt, :], func=AF.Ln, bias=1.0),
                     reads=["e1"], writes=["e1"])
                yield
                for h in range(4):
                    mm(pb[1][:, h * 128:h * 128 + nt], e1[0:nt, h * 128:(h + 1) * 128], tri[0:nt, 0:nt], True, True,
                       reads=["e1", "init_sp"], writes=[("pb", 1)])
                bB3 = pb[1][:].rearrange("p (h t) -> p h t", h=4)[:, :, 0:nt]
                P.op("act", lambda e: e.activation(out=eb[:, :, 0:nt], in_=bB3, func=AF.Exp), reads=[("pb", 1)], writes=["eb"])
                P.op("act", lambda e: e.activation(out=enb[:, :, 0:nt], in_=bB3, func=AF.Exp, scale=-1.0), reads=[("pb", 1)], writes=["enb"])
                P.op("dve", lambda e: e.tensor_copy(out=ebl[:, par, :].unsqueeze(2), in_=eb[:, :, nt - 1:nt]),
                     reads=["eb"], writes=[("ebl", par)])
                P.op("dve", lambda e: e.tensor_tensor(out=qt2[par][:, :, 0:nt], in0=R1[:, 0:4, c0:c0 + nt], in1=eb[:, :, 0:nt], op=ALU.mult),
                     reads=["eb"] + [("R1", h) for h in range(4)], writes=[("qt", par)])
                P.op("dve", lambda e: e.tensor_tensor(out=kt[:, :, 0:nt], in0=R1[:, 4:8, c0:c0 + nt], in1=enb[:, :, 0:nt], op=ALU.mult),
                     reads=["enb"] + [("R1", 4 + h) for h in range(4)], writes=["kt"])
                yield
                for h in range(4):
                    mm(pb[0][0:nt, h * 128:h * 128 + nt], kt[:, h, 0:nt], qt2[par][:, h, 0:nt], True, True,
                       reads=["kt", ("qt", par)], writes=[("pb", 0)])
                for h in range(4):
                    tp(pbv[1][0:nt, h * 128:(h + 1) * 128], kt[:, h, 0:nt], identb, reads=["kt", "init_pool"], writes=[("pb", 1)])
                cm = cmaskb[0:nt, 0:nt].unsqueeze(1).broadcast_to([nt, 4, nt])
                P.op("dve", lambda e: e.tensor_tensor(
                    out=am2[par][0:nt, :, 0:nt], in0=pb[0][0:nt, :].rearrange("p (h t) -> p h t", h=4)[:, :, 0:nt], in1=cm, op=ALU.mult),
                    reads=[("pb", 0), "init_pool"], writes=[("am", par)])
                copy_op("act", ktTs2[par][0:nt, :], pbv[1][0:nt, 0:512], reads=[("pb", 1)], writes=[("ktTs", par)])
                yield

            def seq(ci):
                c0, nt = chunks[ci]
                par = ci % 2
                bO = [2, 3]
                for h in range(4):
                    bank = bO[h // 2]
                    reg = pb[bank][0:nt, (h % 2) * 256:(h % 2 + 1) * 256]
                    mm(reg, am2[par][0:nt, h, 0:nt], va[0:nt, ci, h * 256:(h + 1) * 256], True, False,
                       reads=[("am", par), ("va", ci, h)], writes=[("pb", bank)])
                    mm(reg, qt2[par][:, h, 0:nt], Sbf[l][:, h, :], False, True, reads=[("qt", par), ("Sbf", l, h)], writes=[("pb", bank)])
                yield
                eblb = ebl[:, par, :].unsqueeze(2).broadcast_to([128, 4, 256])
                for hh in range(2):
                    bS = 4 if hh == 0 else 1
                    for h in (hh * 2, hh * 2 + 1):
                        mm(pb[bS][:, (h % 2) * 256:(h % 2 + 1) * 256], ktTs2[par][0:nt, h * 128:(h + 1) * 128], va[0:nt, ci, h * 256:(h + 1) * 256], True, True,
                           reads=[("ktTs", par), ("va", ci, h)], writes=[("pb", bS)])
                    Sv = S[l][:, hh * 2:(hh + 1) * 2, :].rearrange("p h v -> p (h v)")
                    P.op("dve", lambda e, Sv=Sv, bS=bS: e.tensor_tensor(out=Sv, in0=pb[bS][:, :], in1=Sv, op=ALU.add),
                         reads=[("pb", bS), ("S", l)], writes=[("S", l), ("S", l, hh)])
                    for h in (hh * 2, hh * 2 + 1):
                        P.op("act", lambda e, h=h: e.activation(out=Sbf[l][:, h, :], in_=S[l][:, h, :], func=AF.Copy, scale=ebl[:, par, h:h + 1]),
                             reads=[("S", l, hh), ("ebl", par)], writes=[("Sbf", l, h)])
                P.op("dve", lambda e: e.tensor_tensor(out=S[l][:], in0=S[l][:], in1=eblb, op=ALU.mult),
                     reads=[("ebl", par), ("S", l)], writes=[("S", l), ("S", l, 0), ("S", l, 1)])
                yield
                for h in range(4):
                    bank = bO[h // 2]
                    P.op("act", lambda e, bank=bank, h=h: e.activation(
                        out=junk[0:nt, 0:256], in_=pb[bank][0:nt, (h % 2) * 256:(h % 2 + 1) * 256], func=AF.Square, accum_out=ssm[0:nt, h:h + 1]),
                        reads=[("pb", bank)], writes=["ssm", ("sq", 2)])
                P.op("act", lambda e: e.activation(out=ssm[0:nt, 4:8], in_=ssm[0:nt, 0:4], func=AF.Ln, scale=1.0 / 256, bias=EPS),
                     reads=["ssm"], writes=["ssr"])
                P.op("act", lambda e: e.activation(out=ssm[0:nt, 4:8], in_=ssm[0:nt, 4:8], func=AF.Exp, scale=-0.5),
                     reads=["ssr"], writes=["ssr"])
                for h in range(4):
                    bank = bO[h // 2]
                    P.op("dve", lambda e, bank=bank, h=h: e.scalar_tensor_tensor(
                        out=on[0:nt, h * 256:(h + 1) * 256], in0=pb[bank][0:nt, (h % 2) * 256:(h % 2 + 1) * 256],
                        scalar=ssm[0:nt, 4 + h:5 + h], in1=gA[0:nt, l, :], op0=ALU.mult, op1=ALU.mult),
                        reads=[("pb", bank), "ssr", "init_sp"], writes=[("on", h)])
                yield
                for blk in range(8):
                    tp(pbv[4][:, blk * 128:blk * 128 + nt], on[0:nt, blk * 128:(blk + 1) * 128], identb[0:nt, 0:nt],
                       reads=[("on", blk // 2), "init_pool"], writes=[("pb", 4)])
                P.op("dve", lambda e: e.tensor_tensor(
                    out=Ua[:, :, c0:c0 + nt], in0=pbv[4][:].rearrange("p (b t) -> p b t", b=8)[:, :, 0:nt], in1=Ua[:, :, c0:c0 + nt], op=ALU.mult),
                    reads=[("pb", 4)] + [("Ua", kc) for kc in range(8)], writes=[("Ua", kc) for kc in range(8)])
                yield

            n = len(chunks)
            for _ in prep(0):
                yield
            for ci in range(n):
                sg_ = seq(ci)
                pg_ = prep(ci + 1) if ci + 1 < n else None
                for ch in GLA_PAT:
                    if ch == "S":
                        next(sg_)
                        yield
                    elif pg_ is not None:
                        next(pg_)
                        yield

        def qk_flush(keep=0):
            q = st.setdefault("qk_q", [])
            while len(q) > keep:
                q.pop(0)()

        def qk_group(c_off, g, is_q):
            def ev(bank, cbi):
                hp = g * 2 + cbi
                st["qk_n"] = st.get("qk_n", 0) + 1
                i = st["qk_n"] % 3
                P.op("act", lambda e: e.activation(out=sq[i][:, 0:ntok], in_=pb[bank][:, 0:ntok], func=AF.Square),
                     reads=[("pb", bank)], writes=[("sq", i)])
                qk_flush(QK_DEPTH - 1)

                def finish():
                    b2 = nb()
                    mm(pb[b2][:, 0:ntok], bonesb, sq[i][:, 0:ntok], True, True, reads=[("sq", i), "init_pool"], writes=[("pb", b2)])
                    ri = i % 2
                    rstd_from_psum(b2, ri, ntok, 64)
                    if is_q:
                        for par in range(2):
                            ps_ = slice(par * 64, (par + 1) * 64)
                            P.op("dve", lambda e, ps_=ps_, par=par: e.scalar_tensor_tensor(
                                out=Qz[ps_, hp, par, 0:ntok], in0=pb[bank][ps_, 0:ntok], scalar=gq[ps_, l:l + 1],
                                in1=rstd[ri][ps_, 0:ntok], op0=ALU.mult, op1=ALU.mult),
                                reads=[("pb", bank), ("rstd", ri), "gq"], writes=[("Qz", hp)])
                    else:
                        P.op("dve", lambda e: e.scalar_tensor_tensor(
                            out=Kcur[:, hp, 0:ntok], in0=pb[bank][:, 0:ntok], scalar=gvec[:, l * NG + 17:l * NG + 18],
                            in1=rstd[ri][:, 0:ntok], op0=ALU.mult, op1=ALU.mult),
                            reads=[("pb", bank), ("rstd", ri), "init_sp"], writes=[("Kcur", hp)])
                st["qk_q"].append(finish)
            formB(w_in_l, c_off + g * 256, 8, 256, hT_rhs, hT_res, ev)

        def vb_group(g):
            def ev(bank, tb, nt_b):
                copy_op(evac_eng(), Vcur[0:nt_b, tb, :].rearrange("p (h e) -> p h e", e=66)[:, g * 4:(g + 1) * 4, 0:64],
                        pb[bank][0:nt_b, 0:256].rearrange("p (h e) -> p h e", e=64),
                        reads=[("pb", bank)], writes=[("Vcur", tb, g)])
            formA(w_in_l, OFF_VB + g * 256, 256, ntok, ev)

        def mga_group(g):
            def ev(bank, cbi):
                kc = g * 2 + cbi
                P.op("act", lambda e: e.activation(out=M[:, kc, 0:ntok], in_=pb[bank][:, 0:ntok], func=AF.Sigmoid),
                     reads=[("pb", bank)], writes=[("M", kc)])
            formB(w_in_l, OFF_MGA + g * 256, 8, 256, hT_rhs, hT_res, ev)

        def mgb_group(g):
            def ev(bank, cbi):
                kc = g * 2 + cbi
                P.op("act", lambda e: e.activation(out=M2[:, kc * TT:kc * TT + ntok], in_=pb[bank][:, 0:ntok], func=AF.Sigmoid),
                     reads=[("pb", bank)], writes=[("va", kc // 2, (kc % 2) * 2), ("va", kc // 2, (kc % 2) * 2 + 1)])
            formB(w_in_l, OFF_MGB + g * 256, 8, 256, hT_rhs, hT_res, ev)

        Ua_rhs = lambda kc: Ua[:, kc, 0:ntok]
        Ua_res = lambda kc: ("Ua", kc)
        Ub_rhs = lambda kc: Ub[:, kc, 0:ntok]
        Ub_res = lambda kc: ("Ub", kc)

        def bra_group(g):
            def ev(bank, cbi):
                kc = g * 2 + cbi
                P.op("dve", lambda e: e.tensor_tensor(out=M[:, kc, 0:ntok], in0=pb[bank][:, 0:ntok], in1=M[:, kc, 0:ntok], op=ALU.mult),
                     reads=[("pb", bank), ("M", kc)], writes=[("M", kc)])
            formB(wba[l], g * 256, 8, 256, Ua_rhs, Ua_res, ev)

        P.cur_tag = "inprojB"
        for g in range(4):
            qk_group(OFF_QB, g, True)
        for g in range(4):
            qk_group(OFF_KB, g, False)
        qk_flush()
        fill1 = []
        for g in range(4):
            def evv(bank, tb, nt_b, g=g):
                copy_op(evac_eng(), Vcur[0:nt_b, tb, :].rearrange("p (h e) -> p h e", e=66)[:, g * 4:(g + 1) * 4, 0:64],
                        pb[bank][0:nt_b, 0:256].rearrange("p (h e) -> p h e", e=64),
                        reads=[("pb", bank)], writes=[("Vcur", tb, g)])
            fill1.extend(formA_thunks(w_in_l, OFF_VB + g * 256, 256, ntok, evv))
        interleave(gla_gen(), fill1, 7 * nblk, "gla")
        qk_flush()

        P.cur_tag = "kvout"
        if is_sample or last_tile:
            dst = (sgs if is_sample else sgp)[l].rearrange("h d v -> d h v")
            key = ("Sout", l, is_sample)
            P.op("sp", lambda e, dst=dst: e.dma_start(out=dst, in_=S[l][:]), reads=[("S", l)], dma_sem=key)
            out_sems.append(key)
            if is_sample:
                P.op("dve", lambda e: e.memset(S[l][:], 0.0), writes=[("S", l)])
                P.op("dve", lambda e: e.memset(Sbf[l][:], 0.0), writes=[("Sbf", l, hq) for hq in range(4)])

        Kcur_all = [("Kcur", hp) for hp in range(8)]
        Vcur_all = [("Vcur", tb, g) for tb in range(4) for g in range(4)]
        if (not is_sample) and (not last_tile):
            P.op("sp", lambda e: e.dma_start(out=kspill[l], in_=Kcur[:].rearrange("p a b -> p (a b)")),
                 reads=Kcur_all, writes=[("kspill", l)], dma_sem=("kst", l))
            P.op("sp", lambda e: e.dma_start(out=vspill[l], in_=Vcur[:].rearrange("p a b -> p (a b)")),
                 reads=Vcur_all, writes=[("vspill", l)], dma_sem=("vst", l))
        if is_sample or last_tile:
            kdst = kbs if is_sample else kbp
            vdst = vbs if is_sample else vbp
            r0 = 448 if is_sample else 0
            for tb, (c0, nt) in enumerate(chunks):
                bank = nb()
                for hp in range(8):
                    tp(pbv[bank][0:nt, hp * 128:(hp + 1) * 128], Kcur[:, hp, c0:c0 + nt], identb, reads=[("Kcur", hp), "init_pool"], writes=[("pb", bank)])
                copy_op("act", xin[0][0:nt, :], pbv[bank][0:nt, :], reads=[("pb", bank)], writes=[("xin", 0)])
                key = ("kout", l, is_sample)
                P.op("sp", lambda e, c0=c0, nt=nt: e.dma_start(out=kdst[l, r0 + c0:r0 + c0 + nt, :], in_=xin[0][0:nt, :]),
                     reads=[("xin", 0)], dma_sem=key)
                copy_op("dve", xin[1][0:nt, :].rearrange("p (h e) -> p h e", e=64),
                        Vcur[0:nt, tb, :].rearrange("p (h e) -> p h e", e=66)[:, :, 0:64],
                        reads=[("Vcur", tb, g) for g in range(4)], writes=[("xin", 1)])
                key2 = ("vout", l, is_sample)
                P.op("sp", lambda e, c0=c0, nt=nt: e.dma_start(out=vdst[l, r0 + c0:r0 + c0 + nt, :], in_=xin[1][0:nt, :]),
                     reads=[("xin", 1)], dma_sem=key2)
            out_sems.extend([("kout", l, is_sample), ("vout", l, is_sample)])

        def band_gen():
            for pi, (q0, nq) in enumerate(chunks):
                blocks = []
                for C in range(4, -1, -1):
                    b = pi - C
                    if b >= 0:
                        nk = min(128, ntok - b * 128)
                        blocks.append((C, Kcur, "cur", b, nk))
                    elif have_prev:
                        blocks.append((C, Kprev, "prev", 4 + b, 128))

                def qk_exp(hg):
                    for (C, Kb, which, b, nk) in blocks:
                        bank = nb()
                        Kres = (lambda hp: ("Kcur", hp)) if which == "cur" else (lambda hp: "Kprev")
                        if C <= 1:
                            mm(pb[bank][0:nk, 0:4 * nq], Jb[:, 0:nk], Gp[:, l * 2 + C, hg * 4:(hg + 1) * 4, 0:nq], True, False,
                               reads=[("Gp", l), "init_pool"], writes=[("pb", bank)])
                        for h2 in range(2):
                            hp = hg * 2 + h2
                            mm(pb[bank][0:nk, h2 * 2 * nq:(h2 + 1) * 2 * nq], Kb[:, hp, b * 128:b * 128 + nk], Qz[:, hp, :, q0:q0 + nq],
                               C > 1, h2 == 1, reads=[Kres(hp), ("Qz", hp)], writes=[("pb", bank)])
                        src3 = pb[bank][:, 0:4 * nq].rearrange("p (h q) -> p h q", h=4)
                        P.op("act", lambda e, src3=src3, C=C, hg=hg, nk=nk: e.activation(
                            out=PT[hg % 2][C][0:nk, :, 0:nq], in_=src3[0:nk, :, 0:nq], func=AF.Exp),
                            reads=[("pb", bank)], writes=[("PT", hg % 2, C)])
                        if nq == 128 and C == 4:
                            P.op("dve", lambda e, C=C, hg=hg: e.memset(PT[hg % 2][C][0:64, :, 64:128], 0.0), writes=[("PT", hg % 2, C)])
                        elif nq == 128 and C == 0:
                            P.op("dve", lambda e, C=C, hg=hg: e.memset(PT[hg % 2][C][64:128, :, 0:64], 0.0), writes=[("PT", hg % 2, C)])
                        yield

                def pv_norm(hg):
                    ob_bank = nb()
                    for hh in range(4):
                        h = hg * 4 + hh
                        for bi, (C, Kb, which, b, nk) in enumerate(blocks):
                            Vb = Vcur if which == "cur" else Vprev
                            vres = [("Vcur", b, h // 4)] if which == "cur" else ["Vprev"]
                            mm(pb[ob_bank][0:nq, hh * 66:(hh + 1) * 66], PT[hg % 2][C][0:nk, hh, 0:nq], Vb[0:nk, b, h * 66:(h + 1) * 66],
                               bi == 0, bi == len(blocks) - 1, reads=[("PT", hg % 2, C)] + vres, writes=[("pb", ob_bank)])
                    ob3 = pb[ob_bank][0:nq, 0:264].rearrange("p (h e) -> p h e", e=66)
                    P.op("dve", lambda e, ob3=ob3, hg=hg: e.reciprocal(out=rec[0:nq, hg * 4:(hg + 1) * 4].unsqueeze(2), in_=ob3[:, :, 64:65]),
                         reads=[("pb", ob_bank)], writes=["rec"])
                    recb = rec[0:nq, hg * 4:(hg + 1) * 4].unsqueeze(2).broadcast_to([nq, 4, 64])
                    P.op("dve", lambda e, ob3=ob3, hg=hg, recb=recb: e.tensor_tensor(
                        out=on[0:nq, hg * 256:(hg + 1) * 256].rearrange("p (h e) -> p h e", e=64), in0=ob3[:, :, 0:64], in1=recb, op=ALU.mult),
                        reads=[("pb", ob_bank), "rec"], writes=["on"])

                for hg in range(5):
                    if hg < 4:
                        yield from qk_exp(hg)
                    if hg >= 1:
                        pv_norm(hg - 1)
                        yield
                bI = nb()
                for blk in range(8):
                    tp(pbv[bI][:, blk * 128:blk * 128 + nq], on[0:nq, blk * 128:(blk + 1) * 128], identb[0:nq, 0:nq],
                       reads=["on", "init_pool"], writes=[("pb", bI)])
                P.op("dve", lambda e, bI=bI, q0=q0, nq=nq: e.tensor_tensor(
                    out=Ub[:, :, q0:q0 + nq], in0=pbv[bI][:].rearrange("p (b t) -> p b t", b=8)[:, :, 0:nq], in1=Ub[:, :, q0:q0 + nq], op=ALU.mult),
                    reads=[("pb", bI)] + [("Ub", kc) for kc in range(8)], writes=[("Ub", kc) for kc in range(8)])
                yield

        fill2 = []
        for g in range(4):
            def evg(bank, cbi, g=g):
                kc = g * 2 + cbi
                i = kc % 2
                P.op("act", lambda e: e.activation(out=gt[i][:, 0:ntok], in_=pb[bank][:, 0:ntok], func=AF.Tanh, scale=0.5),
                     reads=[("pb", bank)], writes=[("sq", i)])
                P.op("dve", lambda e: e.scalar_tensor_tensor(out=Ub[:, kc, 0:ntok], in0=gt[i][:, 0:ntok], scalar=1.0, in1=pb[bank][:, 0:ntok],
                                                             op0=ALU.add, op1=ALU.mult),
                     reads=[("sq", i), ("pb", bank)], writes=[("Ub", kc)])
            fill2.extend(formB_thunks(w_in_l, OFF_GB + g * 256, 8, 256, hT_rhs, hT_res, evg))
        n_front = len(fill2)
        for g in range(4):
            def evm(bank, cbi, g=g):
                kc = g * 2 + cbi
                P.op("act", lambda e: e.activation(out=M[:, kc, 0:ntok], in_=pb[bank][:, 0:ntok], func=AF.Tanh, scale=0.5),
                     reads=[("pb", bank)], writes=[("M", kc)])
            fill2.extend(formB_thunks(w_in_l, OFF_MGA + g * 256, 8, 256, hT_rhs, hT_res, evm))
        for g in range(4):
            def evb(bank, cbi, g=g):
                kc = g * 2 + cbi
                P.op("act", lambda e: e.activation(out=M2[:, kc * TT:kc * TT + ntok], in_=pb[bank][:, 0:ntok], func=AF.Tanh, scale=0.5),
                     reads=[("pb", bank)], writes=[("va", kc // 2, (kc % 2) * 2), ("va", kc // 2, (kc % 2) * 2 + 1)])
            fill2.extend(formB_thunks(w_in_l, OFF_MGB + g * 256, 8, 256, hT_rhs, hT_res, evb))
        for g in range(4):
            def eva(bank, cbi, g=g):
                kc = g * 2 + cbi
                P.op("dve", lambda e: e.scalar_tensor_tensor(out=M[:, kc, 0:ntok], in0=M[:, kc, 0:ntok], scalar=1.0, in1=pb[bank][:, 0:ntok],
                                                             op0=ALU.add, op1=ALU.mult),
                     reads=[("pb", bank), ("M", kc)], writes=[("M", kc)])
            fill2.extend(formB_thunks(wba[l], g * 256, 8, 256, Ua_rhs, Ua_res, eva))
        ny = 0
        for pi in range(nblk):
            nbk = sum(1 for C in range(5) if (pi - C >= 0) or have_prev)
            ny += 4 * (nbk + 1) + 1
        interleave(band_gen(), fill2, ny, "band", front=n_front)

        P.cur_tag = "branchB"
        for g in range(4):
            def ev(bank, cbi, g=g):
                kc = g * 2 + cbi
                P.op("dve", lambda e: e.scalar_tensor_tensor(out=ftmp[:, 0:ntok], in0=M2[:, kc * TT:kc * TT + ntok], scalar=1.0, in1=pb[bank][:, 0:ntok],
                                                             op0=ALU.add, op1=ALU.mult),
                     reads=[("pb", bank), ("va", kc // 2, (kc % 2) * 2), ("va", kc // 2, (kc % 2) * 2 + 1)], writes=["ftmp"])
                P.op("dve", lambda e: e.scalar_tensor_tensor(out=M[:, kc, 0:ntok], in0=ftmp[:, 0:ntok], scalar=0.5, in1=M[:, kc, 0:ntok],
                                                             op0=ALU.mult, op1=ALU.add),
                     reads=["ftmp", ("M", kc)], writes=[("M", kc)])
            formB(wbb[l], g * 256, 8, 256, Ub_rhs, Ub_res, ev)

        if nxt is not None and l == DEPTH - 1:
            nsrc, ntok0, nntok = nxt
            for tb in range((nntok + 127) // 128):
                nt2 = min(128, nntok - tb * 128)
                dstv, dres = xstage(tb)
                P.op("sp", lambda e, tb=tb, nt2=nt2, dstv=dstv: e.dma_start(out=dstv[0:nt2, :], in_=nsrc[ntok0 + tb * 128:ntok0 + tb * 128 + nt2, :]),
                     writes=dres, dma_sem=("xpre", tb))
        P.cur_tag = "pload"
        for tb, (c0, nt) in enumerate(chunks):
            psb, pres = pin[tb % 2], ("pin", tb % 2)
            if tb >= 2:
                if p_stage_xin:
                    psb, pres = xin[tb % 2], ("xin", tb % 2)
                else:
                    P.op("sp", lambda e, tb=tb, c0=c0, nt=nt: e.dma_start(out=pin[tb % 2][0:nt, :], in_=psrc[l, tok0 + c0:tok0 + c0 + nt, :]),
                         writes=[("pin", tb % 2)], dma_sem=("pin", tb % 2))
            bank = nb()
            for j in range(2):
                tp(pb[bank][:, j * 128:j * 128 + nt], psb[0:nt, j * 128:(j + 1) * 128], identf[0:nt, 0:nt],
                   reads=[pres, "init_sp"], writes=[("pb", bank)])
            copy_op("act", pT[:, 0:2, c0:c0 + nt], pb[bank][:, 0:256].rearrange("p (j t) -> p j t", j=2)[:, :, 0:nt],
                    reads=[("pb", bank)], writes=["pT"])
        P.cur_tag = "wout"
        M_rhs = lambda kc: M[:, kc, 0:ntok]
        M_res = lambda kc: ("M", kc)
        for g in range(4):
            def ev(bank, cbi, g=g):
                kc = g * 2 + cbi
                P.op("dve", lambda e: e.scalar_tensor_tensor(out=xT[:, kc, 0:ntok], in0=pb[bank][:, 0:ntok], scalar=0.5, in1=xT[:, kc, 0:ntok],
                                                             op0=ALU.mult, op1=ALU.add),
                     reads=[("pb", bank), ("xT", kc)], writes=[("xT", kc)])
            formB(wo[l], g * 256, 8, 256, M_rhs, M_res, ev)

        P.cur_tag = "ple"
        rmsnorm_to_hT(l, ntok, l * NG + 8)
        for g in range(4):
            def ev(bank, cbi, g=g):
                kc = g * 2 + cbi
                P.op("act", lambda e: e.activation(out=M[:, kc, 0:ntok], in_=pb[bank][:, 0:ntok], func=AF.Sigmoid),
                     reads=[("pb", bank)], writes=[("M", kc)])
            formB(wpg[l], g * 256, 8, 256, hT_rhs, hT_res, ev)
        pT_rhs = lambda kc: pT[:, kc, 0:ntok]
        pT_res = lambda kc: "pT"
        for g in range(4):
            def ev(bank, cbi, g=g):
                kc = g * 2 + cbi
                P.op("dve", lambda e: e.tensor_tensor(out=ftmp[:, 0:ntok], in0=pb[bank][:, 0:ntok], in1=M[:, kc, 0:ntok], op=ALU.mult),
                     reads=[("pb", bank), ("M", kc)], writes=["ftmp"])
                P.op("dve", lambda e: e.tensor_tensor(out=xT[:, kc, 0:ntok], in0=ftmp[:, 0:ntok], in1=xT[:, kc, 0:ntok], op=ALU.add),
                     reads=["ftmp", ("xT", kc)], writes=[("xT", kc)])
            formB(wp[l], g * 256, 2, 256, pT_rhs, pT_res, ev)

        P.cur_tag = "yout"
        if l == DEPTH - 1:
            ydst = ys if is_sample else yp
            for tb, (c0, nt) in enumerate(chunks):
                buf = tb % 2
                for g in range(2):
                    bank = nb()
                    for kk in range(4):
                        kc = g * 4 + kk
                        tp(pb[bank][0:nt, kk * 128:(kk + 1) * 128], xT[:, kc, c0:c0 + nt], identf, reads=[("xT", kc), "init_sp"], writes=[("pb", bank)])
                    copy_op("dve", xin[buf][0:nt, g * 512:(g + 1) * 512], pb[bank][0:nt, :], reads=[("pb", bank)], writes=[("xin", buf)])
                key = ("yout", buf, is_sample)
                P.op("sp", lambda e, c0=c0, nt=nt, buf=buf: e.dma_start(out=ydst[tok0 + c0:tok0 + c0 + nt, :], in_=xin[buf][0:nt, :]),
                     reads=[("xin", buf)], dma_sem=key)
                if key not in out_sems:
                    out_sems.append(key)

    tiles = [(TT, t * TT, xp, pp, False, t, t == NT - 1) for t in range(NT)]
    if do_sample:
        tiles.append((64, 0, xs, ps, True, 0, False))
    for ti, (ntok_, tok0_, xsrc_, psrc_, iss_, t_, last_) in enumerate(tiles):
        nxt = None
        if ti + 1 < len(tiles):
            n2 = tiles[ti + 1]
            nxt = (n2[2], n2[1], n2[0])
        for l in range(DEPTH):
            tile_layer(l, ntok_, tok0_, xsrc_, psrc_, iss_, t_, last_, prefetched=(ti > 0), nxt=nxt)
    if do_sample:
        out_sems.extend(["kroll", "vroll"])
    P.emit(final_wait_sems=out_sems)
    return nc


_CACHE = {}


def _consts():
    c = np.zeros((128, 6, 128), np.float32)
    c[:, 0, :] = np.eye(128)
    c[:, 1, :] = np.eye(128)[::-1]
    j = np.arange(128)[:, None]
    i = np.arange(128)[None, :]
    c[:, 2, :] = (j <= i)
    c[:, 3, :] = (j <= i) * (-1.0 / 16.0)
    blk = np.zeros((128, 128), np.float32)
    blk[0:64, 0:64] = 1
    blk[64:128, 64:128] = 1
    c[:, 4, :] = blk
    c[:, 5, :] = 1
    return c


def make_in_maps(inp, SEQ, DEPTH, n_cores=8):
    f = lambda a: np.ascontiguousarray(np.asarray(a, dtype=np.float32))
    gv = np.zeros((128, DEPTH * NG), np.float32)
    for l in range(DEPTH):
        gv[:, l * NG:l * NG + 8] = f(inp["norm_g"])[l].reshape(8, 128).T
        gv[:, l * NG + 8:l * NG + 16] = f(inp["ple_norm_g"])[l].reshape(8, 128).T
        gv[:, l * NG + 16] = np.tile(f(inp["q_norm_g"])[l], 2)
        gv[:, l * NG + 17] = np.tile(f(inp["k_norm_g"])[l], 2)
    gA = np.ascontiguousarray(np.broadcast_to(f(inp["gla_norm_g"])[None, :, :], (128, DEPTH, 256)))
    rb = f(inp["rel_bias"])
    tabext = np.ascontiguousarray(np.concatenate([rb, np.repeat(rb[..., -1:], 127, axis=-1)], axis=-1))
    cb = np.ascontiguousarray(np.broadcast_to(rb[None, :, :, 256], (128, DEPTH, 16)))
    wgu = np.ascontiguousarray(np.concatenate([f(inp["w_gate_up"]), f(inp["b_gate"])[:, None, :]], axis=1))
    common = {
        "w_in": f(inp["w_in"]), "wgu": wgu, "wba": f(inp["w_branch_a"]), "wbb": f(inp["w_branch_b"]),
        "wo": f(inp["w_out"]), "wpg": f(inp["w_ple_gate"]), "wp": f(inp["w_ple"]),
        "gv": gv, "gA": gA, "tabext": tabext, "cb": cb, "cst": _consts(),
    }
    xp = f(inp["x_prompt"]); pp = f(inp["p_prompt"]); xs = f(inp["x_sample"]); ps = f(inp["p_sample"])
    sg = f(inp["state_gla"]); ck = f(inp["cache_band_k"]); cv = f(inp["cache_band_v"])
    maps = []
    for c in range(n_cores):
        b = c % xp.shape[0]
        m = dict(common)
        m["xp"] = np.ascontiguousarray(xp[b])
        m["pp"] = np.ascontiguousarray(pp[:, b])
        m["xs"] = np.ascontiguousarray(xs[c])
        m["ps"] = np.ascontiguousarray(ps[:, c])
        m["sg"] = np.ascontiguousarray(sg[:, c])
        m["ck"] = np.ascontiguousarray(ck[:, c].reshape(DEPTH, 512, 1024))
        m["cv"] = np.ascontiguousarray(cv[:, c].reshape(DEPTH, 512, 1024))
        maps.append(m)
    return maps


def run(inp, SEQ, DEPTH):
    key = (SEQ, DEPTH)
    if key not in _CACHE:
        _CACHE[key] = build_program(SEQ, DEPTH)
    nc = _CACHE[key]
    maps = make_in_maps(inp, SEQ, DEPTH)
    res = run_bass_kernel_spmd(nc, maps, core_ids=list(range(8)))
    r = res.results
    B = np.asarray(inp["x_prompt"]).shape[0]
    y_prompt = np.stack([r[b]["yp"] for b in range(B)])
    y_sample = np.stack([r[c]["ys"] for c in range(8)])
    sgp = np.stack([r[b]["sgp"] for b in range(B)], axis=1)
    kbp = np.stack([r[b]["kbp"] for b in range(B)], axis=1).reshape(DEPTH, B, 512, 16, 64)
    vbp = np.stack([r[b]["vbp"] for b in range(B)], axis=1).reshape(DEPTH, B, 512, 16, 64)
    sgs = np.stack([r[c]["sgs"] for c in range(8)], axis=1)
    kbs = np.stack([r[c]["kbs"] for c in range(8)], axis=1).reshape(DEPTH, 8, 512, 16, 64)
    vbs = np.stack([r[c]["vbs"] for c in range(8)], axis=1).reshape(DEPTH, 8, 512, 16, 64)
    return (y_prompt, y_sample, sgp, kbp, vbp, sgs, kbs, vbs)


def kernel(**inputs):
    return run(inputs, 4096, 2)
```

```python
import numpy as np
import concourse.bass as bass
import concourse.mybir as mybir
from concourse.bass_utils import run_bass_kernel_spmd

F32 = mybir.dt.float32
BF16 = mybir.dt.bfloat16
AF = mybir.ActivationFunctionType
ALU = mybir.AluOpType

D = 1024
N_IN = 9232
OFF_QA, OFF_KA, OFF_VA, OFF_RA, OFF_GA = 0, 512, 1024, 2048, 2064
OFF_QB, OFF_KB, OFF_VB, OFF_GB, OFF_MGA, OFF_MGB = 3088, 4112, 5136, 6160, 7184, 8208
EPS = 1e-6
TT = 512
QK_DEPTH = 1
import os
GLA_PAT = 'SPSSPSP'
NG = 18

ENGS = ("pe", "act", "dve", "pool", "sp")


class Op:
    __slots__ = ("eng", "fn", "deps", "inc", "count", "dma_sem", "dma_val", "tag", "meta")

    def __init__(self, eng, fn):
        self.eng = eng
        self.fn = fn
        self.deps = []
        self.inc = False
        self.count = 0
        self.dma_sem = None
        self.dma_val = 0


class Prog:
    def __init__(self, nc):
        self.nc = nc
        self.ops = {e: [] for e in ENGS}
        self.last_w = {}
        self.readers = {}
        self.dma_cnt = {}
        self.cur_tag = ""

    def op(self, eng, fn, reads=(), writes=(), dma_sem=None):
        o = Op(eng, fn)
        o.tag = self.cur_tag
        o.meta = None
        is_dma = dma_sem is not None
        deps = {}
        for r in reads:
            w = self.last_w.get(r)
            if w is not None:
                deps[id(w)] = (w, "raw")
        for r in writes:
            w = self.last_w.get(r)
            if w is not None and id(w) not in deps:
                deps[id(w)] = (w, "waw")
            for rd in self.readers.get(r, ()):
                if id(rd) not in deps:
                    deps[id(rd)] = (rd, "war")
        for d, kind in deps.values():
            d_is_dma = d.dma_sem is not None
            if d.eng == eng and not d_is_dma and not is_dma:
                if eng == "pe" or kind != "raw":
                    continue
            if not d_is_dma:
                d.inc = True
            o.deps.append(d)
        for r in reads:
            self.readers.setdefault(r, []).append(o)
        for r in writes:
            self.last_w[r] = o
            self.readers[r] = []
        if is_dma:
            o.dma_sem = dma_sem
            c = self.dma_cnt.get(dma_sem, 0) + 16
            self.dma_cnt[dma_sem] = c
            o.dma_val = c
        self.ops[eng].append(o)
        return o

    def emit(self, final_wait_sems=()):
        nc = self.nc
        esem = {e: nc.alloc_semaphore("es_" + e) for e in ENGS}
        dsem = {}
        for i, k in enumerate(self.dma_cnt):
            dsem[k] = nc.alloc_semaphore("ds%d" % i)
        for e in ENGS:
            c = 0
            for o in self.ops[e]:
                if o.dma_sem is None and o.inc:
                    c += 1
                    o.count = c
        ops = self.ops
        dma_cnt = self.dma_cnt

        def run(e, eng):
            known = {}
            for o in ops[e]:
                need = {}
                for d in o.deps:
                    if d.dma_sem is not None:
                        key = ("d", d.dma_sem)
                        val = d.dma_val
                    else:
                        key = ("e", d.eng)
                        val = d.count
                    if val > need.get(key, 0):
                        need[key] = val
                for key, val in need.items():
                    if known.get(key, 0) >= val:
                        continue
                    known[key] = val
                    sem = dsem[key[1]] if key[0] == "d" else esem[key[1]]
                    eng.wait_ge(sem, val)
                ins = o.fn(eng)
                if o.dma_sem is not None:
                    ins.then_inc(dsem[o.dma_sem], 16)
                elif o.inc:
                    ins.then_inc(esem[e], 1)
            if e == "sp":
                for k in final_wait_sems:
                    eng.wait_ge(dsem[k], dma_cnt[k])

        with nc.Block() as block:
            @block.tensor
            def _(eng):
                run("pe", eng)

            @block.scalar
            def _(eng):
                run("act", eng)

            @block.vector
            def _(eng):
                run("dve", eng)

            @block.gpsimd
            def _(eng):
                run("pool", eng)

            @block.sync
            def _(eng):
                run("sp", eng)


def build_program(SEQ, DEPTH, do_sample=True):
    nc = bass.Bass("TRN2", target_bir_lowering=False)
    P = Prog(nc)
    NT = SEQ // TT

    def din(name, shape):
        return nc.dram_tensor(name, list(shape), F32, kind="ExternalInput").ap()

    def dout(name, shape):
        return nc.dram_tensor(name, list(shape), F32, kind="ExternalOutput").ap()

    xp = din("xp", [SEQ, D]); pp = din("pp", [DEPTH, SEQ, 256])
    xs = din("xs", [64, D]); ps = din("ps", [DEPTH, 64, 256])
    sg = din("sg", [DEPTH, 4, 128, 256])
    ck = din("ck", [DEPTH, 512, D]); cv = din("cv", [DEPTH, 512, D])
    w_in = din("w_in", [DEPTH, D, N_IN])
    wgu_d = din("wgu", [DEPTH, 17, 512])
    wba = din("wba", [DEPTH, D, D]); wbb = din("wbb", [DEPTH, D, D])
    wo = din("wo", [DEPTH, D, D]); wpg = din("wpg", [DEPTH, D, D])
    wp = din("wp", [DEPTH, 256, D])
    gv_d = din("gv", [128, DEPTH * NG])
    gA_d = din("gA", [128, DEPTH, 256])
    tabext = nc.dram_tensor("tabext", [DEPTH, 16, 384], F32, kind="ExternalInput")
    cb_d = din("cb", [128, DEPTH, 16])
    cst_d = din("cst", [128, 6, 128])

    yp = dout("yp", [SEQ, D]); ys = dout("ys", [64, D])
    sgp = dout("sgp", [DEPTH, 4, 128, 256])
    kbp = dout("kbp", [DEPTH, 512, D]); vbp = dout("vbp", [DEPTH, 512, D])
    sgs = dout("sgs", [DEPTH, 4, 128, 256])
    kbs = dout("kbs", [DEPTH, 512, D]); vbs = dout("vbs", [DEPTH, 512, D])

    kspill = [nc.dram_tensor("kspill%d" % l, [128, 8 * 512], BF16, kind="Internal").ap() for l in range(DEPTH)]
    vspill = [nc.dram_tensor("vspill%d" % l, [128, 4 * 1056], BF16, kind="Internal").ap() for l in range(DEPTH)]

    def sb(name, shape, dt):
        return nc.alloc_sbuf_tensor("s_" + name, list(shape), dt)

    xT = sb("xT", [128, 8, TT], F32)
    hT = sb("hT", [128, 8, TT], BF16)
    R1 = sb("R1", [128, 8, TT], BF16)
    Qz = sb("Qz", [128, 8, 2, TT], BF16)
    va = sb("va", [128, 4, 1024], BF16)
    M2 = va[:].rearrange("p a b -> p (a b)")
    Ua = sb("Ua", [128, 8, TT], BF16)
    Ub = sb("Ub", [128, 8, TT], BF16)
    M = sb("M", [128, 8, TT], BF16)
    Kcur = sb("Kcur", [128, 8, TT], BF16)
    Kprev = sb("Kprev", [128, 8, TT], BF16)
    Vcur = sb("Vcur", [128, 4, 1056], BF16)
    Vprev = sb("Vprev", [128, 4, 1056], BF16)
    S = [sb("S%d" % l, [128, 4, 256], F32) for l in range(DEPTH)]
    Sbf = [sb("Sbf%d" % l, [128, 4, 256], BF16) for l in range(DEPTH)]
    NW = 3
    W = [sb("W%d" % i, [128, 8, 256], BF16) for i in range(NW)]
    Gp = sb("Gp", [128, DEPTH * 2, 16, 128], BF16)
    PT = [[sb("PT%d_%d" % (i, k), [128, 4, 128], BF16) for i in range(5)] for k in range(2)]
    e1 = sb("e1", [128, 512], F32)
    eb = sb("eb", [128, 4, 128], F32)
    enb = sb("enb", [128, 4, 128], F32)
    qt = sb("qt", [128, 4, 128], BF16)
    kt = sb("kt", [128, 4, 128], BF16)
    am = sb("am", [128, 4, 128], BF16)
    ktTs = sb("ktTs", [128, 512], BF16)
    on = sb("on", [128, 1024], BF16)
    ssm = sb("ssm", [128, 8], F32)
    rec = sb("rec", [128, 16], F32)
    sq = [sb("sq%d" % i, [128, TT], BF16) for i in range(3)]
    rstd = [sb("rstd%d" % i, [128, TT], F32) for i in range(2)]
    gt = [sq[0], sq[1]]
    junk = sq[2]
    qt2 = [qt, sb("qtB", [128, 4, 128], BF16)]
    am2 = [am, sb("amB", [128, 4, 128], BF16)]
    ktTs2 = [ktTs, sb("ktTsB", [128, 512], BF16)]
    ebl = sb("ebl", [128, 2, 4], F32)
    ftmp = sb("ftmp", [128, TT], F32)
    xin = [sb("xin%d" % i, [128, 1024], F32) for i in range(2)]
    pin = [sb("pin%d" % i, [128, 256], F32) for i in range(2)]
    pT = sb("pT", [128, 2, TT], BF16)
    cst = sb("cst", [128, 2, 128], F32)
    cstb = sb("cstb", [128, 6, 128], BF16)
    gvec = sb("gvec", [128, DEPTH * NG], F32)
    gq = sb("gq", [128, DEPTH], F32)
    gA = sb("gA", [128, DEPTH, 256], F32)
    cbt = sb("cbt", [128, DEPTH, 16], F32)
    wgu = sb("wgu", [32, DEPTH, 512], BF16)
    raT = sb("raT", [32, TT], BF16)
    pb = [nc.alloc_psum_tensor("pb%d" % i, [128, 512], F32) for i in range(8)]
    pbv = [b.bitcast(BF16) for b in pb]
    UaF = Ua[:].rearrange("p a b -> p (a b)").bitcast(F32)
    UbF = Ub[:].rearrange("p a b -> p (a b)").bitcast(F32)
    def xstage(tb):
        v = UaF if tb < 2 else UbF
        return v[:, (tb % 2) * 1024:(tb % 2 + 1) * 1024], [(("Ua" if tb < 2 else "Ub"), (tb % 2) * 4 + k) for k in range(4)]

    identf = cst[:, 0, :]
    tri = cst[:, 1, :]
    identb = cstb[:, 0, :]
    Jb = cstb[:, 1, :]
    cmaskb = cstb[:, 2, :]
    bonesb = cstb[:, 4, :]
    onesb = cstb[:, 5, :]

    st = {"bank": 0, "ws": 0, "ev": 0, "fb": 0, "mb": 0, "split": False, "kind": "mix"}
    out_sems = []

    def nb(kind=None):
        if st["split"]:
            if (kind or st["kind"]) == "fill":
                b = 5 + st["fb"] % 3
                st["fb"] += 1
            else:
                b = st["mb"] % 5
                st["mb"] += 1
            return b
        b = st["bank"]
        st["bank"] = (b + 1) % 8
        return b

    def interleave(gen, fillers, n_yields, name="", front=0):
        st["split"] = True
        P.cur_tag = "mix:" + name
        nf = len(fillers)
        done = 0
        y = 0
        for _ in gen:
            y += 1
            if y <= front:
                want = min(nf, y)
            else:
                want = min(nf, max(front, front + ((y - front) * (nf - front) + (n_yields - front) - 1) // max(1, n_yields - front)))
            while done < want:
                st["kind"] = "fill"
                P.cur_tag = "fill:" + name
                fillers[done]()
                P.cur_tag = "mix:" + name
                st["kind"] = "mix"
                done += 1
        while done < nf:
            st["kind"] = "fill"
            P.cur_tag = "fill:" + name
            fillers[done]()
            st["kind"] = "mix"
            done += 1
        st["split"] = False

    def evac_eng():
        st["ev"] += 1
        return "act" if st["ev"] % 2 == 0 else "dve"

    def copy_op(eng, out, in_, reads, writes, scale=None):
        if eng == "act":
            if scale is None:
                P.op("act", lambda e: e.activation(out=out, in_=in_, func=AF.Copy), reads, writes)
            else:
                P.op("act", lambda e: e.activation(out=out, in_=in_, func=AF.Copy, scale=scale), reads, writes)
        else:
            if scale is None:
                P.op("dve", lambda e: e.tensor_copy(out=out, in_=in_), reads, writes)
            else:
                P.op("dve", lambda e: e.tensor_scalar(out=out, in0=in_, scalar1=scale, scalar2=None, op0=ALU.mult), reads, writes)

    def mm(out, lhsT, rhs, start, stop, reads, writes):
        o = P.op("pe", lambda e: e.matmul(out, lhsT=lhsT, rhs=rhs, start=start, stop=stop, skip_group_check=True), reads, writes)
        o.meta = int(np.prod(rhs.shape[1:]))

    def tp(out, in_, ident, reads, writes):
        o = P.op("pe", lambda e: e.transpose(out, in_, ident), reads, writes)
        o.meta = int(np.prod(ident.shape[1:]))

    wcache = {}

    def wload(w2d, r0, nk, c0, ncols):
        s = st["ws"] % NW
        st["ws"] += 1
        dst = W[s][:, 0:nk, 0:ncols]
        key = (w2d.name, int(w2d.offset), r0, nk, c0, ncols)
        if key not in wcache:
            src = w2d[r0:r0 + nk * 128, c0:c0 + ncols].rearrange("(k p) c -> p k c", p=128)
            P.op("pool", lambda e: e.dma_start(out=dst, in_=src), writes=[("W", s)], dma_sem=("w", s))
            sc = nc.dram_tensor("wsc%d" % len(wcache), [128, nk * ncols], BF16, kind="Internal").ap()
            wcache[key] = sc
            P.op("sp", lambda e: e.dma_start(out=sc.rearrange("p (k c) -> p k c", k=nk), in_=dst),
                 reads=[("W", s)], writes=[("wsc", key)], dma_sem=("wst", s))
        else:
            sc = wcache[key]
            P.op("pool", lambda e: e.dma_start(out=dst, in_=sc.rearrange("p (k c) -> p k c", k=nk)),
                 reads=[("wsc", key)], writes=[("W", s)], dma_sem=("w", s))
        return s

    def rstd_from_psum(bank, i, ntok, n):
        P.op("act", lambda e: e.activation(out=rstd[i][:, 0:ntok], in_=pb[bank][:, 0:ntok], func=AF.Ln, scale=1.0 / n, bias=EPS),
             reads=[("pb", bank)], writes=[("rstd", i)])
        P.op("act", lambda e: e.activation(out=rstd[i][:, 0:ntok], in_=rstd[i][:, 0:ntok], func=AF.Exp, scale=-0.5),
             reads=[("rstd", i)], writes=[("rstd", i)])

    P.op("sp", lambda e: e.dma_start(out=cst[:, 0, :], in_=cst_d[:, 0, :]), writes=["init_sp"], dma_sem="init_sp")
    P.op("sp", lambda e: e.dma_start(out=cst[:, 1, :], in_=cst_d[:, 3, :]), writes=["init_sp"], dma_sem="init_sp")
    P.op("sp", lambda e: e.dma_start(out=gvec[:], in_=gv_d), writes=["init_sp"], dma_sem="init_sp")
    P.op("sp", lambda e: e.dma_start(out=gA[:], in_=gA_d), writes=["init_sp"], dma_sem="init_sp")
    P.op("sp", lambda e: e.dma_start(out=cbt[:], in_=cb_d), writes=["init_sp"], dma_sem="init_sp")
    P.op("pool", lambda e: e.dma_start(out=cstb[:], in_=cst_d), writes=["init_pool"], dma_sem="init_pool")
    P.op("pool", lambda e: e.dma_start(out=wgu[0:17, :, :], in_=wgu_d.rearrange("l k c -> k l c")), writes=["init_pool"], dma_sem="init_pool")
    P.op("dve", lambda e: e.memset(raT[:], 1.0), writes=["raT"])
    P.op("dve", lambda e: e.memset(Vcur[:], 1.0), writes=[("Vcur", tb, g) for tb in range(4) for g in range(4)])
    P.op("dve", lambda e: e.memset(Vprev[:], 1.0), writes=["Vprev"])
    P.op("dve", lambda e: e.memset(Qz[:], 0.0), writes=[("Qz", hp) for hp in range(8)])
    for k in range(2):
        for i in range(5):
            P.op("dve", lambda e, i=i, k=k: e.memset(PT[k][i][:], 0.0), writes=[("PT", k, i)])
    for l in range(DEPTH):
        P.op("dve", lambda e, l=l: e.memset(S[l][:], 0.0), writes=[("S", l)])
        P.op("dve", lambda e, l=l: e.memset(Sbf[l][:], 0.0), writes=[("Sbf", l, hq) for hq in range(4)])
        P.op("dve", lambda e, l=l: e.tensor_scalar(out=gq[:, l:l + 1], in0=gvec[:, l * NG + 16:l * NG + 17], scalar1=0.125, scalar2=None, op0=ALU.mult),
             reads=["init_sp"], writes=["gq"])
        for C in range(2):
            stg = xin[C][:].rearrange("p (h i) -> p h i", h=8)
            for half in range(2):
                src = bass.AP(tabext, l * 16 * 384 + half * 8 * 384 + 1 + 128 * C, [[1, 128], [384, 8], [1, 128]])
                P.op("sp", lambda e, stg=stg, src=src: e.dma_start(out=stg, in_=src), writes=[("xin", C)], dma_sem=("ginit", C))
                cbb = cbt[:, l, half * 8:(half + 1) * 8].unsqueeze(2).broadcast_to([128, 8, 128])
                P.op("dve", lambda e, stg=stg, cbb=cbb, l=l, C=C, half=half: e.tensor_tensor(
                    out=Gp[:, l * 2 + C, half * 8:(half + 1) * 8, :], in0=stg, in1=cbb, op=ALU.subtract),
                    reads=[("xin", C), "init_sp"], writes=[("Gp", l)])

    def rmsnorm_to_hT(l, ntok, gcol0):
        bank = nb()
        for kc in range(8):
            if kc % 2 == 0:
                P.op("act", lambda e, kc=kc: e.activation(out=sq[kc % 2][:, 0:ntok], in_=xT[:, kc, 0:ntok], func=AF.Square),
                     reads=[("xT", kc)], writes=[("sq", kc % 2)])
            else:
                P.op("dve", lambda e, kc=kc: e.tensor_tensor(out=sq[kc % 2][:, 0:ntok], in0=xT[:, kc, 0:ntok], in1=xT[:, kc, 0:ntok], op=ALU.mult),
                     reads=[("xT", kc)], writes=[("sq", kc % 2)])
            mm(pb[bank][:, 0:ntok], onesb, sq[kc % 2][:, 0:ntok], kc == 0, kc == 7,
               reads=[("sq", kc % 2), "init_pool"], writes=[("pb", bank)])
        rstd_from_psum(bank, 0, ntok, D)
        for kc in range(8):
            P.op("dve", lambda e, kc=kc: e.scalar_tensor_tensor(
                out=hT[:, kc, 0:ntok], in0=xT[:, kc, 0:ntok], scalar=gvec[:, gcol0 + kc:gcol0 + kc + 1],
                in1=rstd[0][:, 0:ntok], op0=ALU.mult, op1=ALU.mult),
                reads=[("xT", kc), ("rstd", 0), "init_sp"], writes=[("hT", kc)])

    def formB(w2d, c0, nk, ncols, rhs_fn, rhs_res, evac):
        s = wload(w2d, 0, nk, c0, ncols)
        for cbi in range((ncols + 127) // 128):
            m = min(128, ncols - cbi * 128)
            bank = nb()
            for kc in range(nk):
                rhs = rhs_fn(kc)
                mm(pb[bank][0:m, 0:rhs.shape[-1]], W[s][:, kc, cbi * 128:cbi * 128 + m], rhs, kc == 0, kc == nk - 1,
                   reads=[("W", s), rhs_res(kc)], writes=[("pb", bank)])
            evac(bank, cbi)

    def formA(w2d, c0, ncols, ntok, evac):
        s = wload(w2d, 0, 8, c0, ncols)
        for tb in range((ntok + 127) // 128):
            nt_b = min(128, ntok - tb * 128)
            bank = nb()
            for kc in range(8):
                mm(pb[bank][0:nt_b, 0:ncols], hT[:, kc, tb * 128:tb * 128 + nt_b], W[s][:, kc, 0:ncols], kc == 0, kc == 7,
                   reads=[("W", s), ("hT", kc)], writes=[("pb", bank)])
            evac(bank, tb, nt_b)

    def formB_thunks(w2d, c0, nk, ncols, rhs_fn, rhs_res, evac):
        box = {}
        th = []
        ncb = (ncols + 127) // 128
        for cbi in range(ncb):
            def t(cbi=cbi):
                if cbi == 0:
                    box["s"] = wload(w2d, 0, nk, c0, ncols)
                s_ = box["s"]
                m = min(128, ncols - cbi * 128)
                bank = nb()
                for kc in range(nk):
                    rhs = rhs_fn(kc)
                    mm(pb[bank][0:m, 0:rhs.shape[-1]], W[s_][:, kc, cbi * 128:cbi * 128 + m], rhs, kc == 0, kc == nk - 1,
                       reads=[("W", s_), rhs_res(kc)], writes=[("pb", bank)])
                evac(bank, cbi)
            th.append(t)
        return th

    def formA_thunks(w2d, c0, ncols, ntok, evac):
        box = {}
        th = []
        for tb in range((ntok + 127) // 128):
            def t(tb=tb):
                if tb == 0:
                    box["s"] = wload(w2d, 0, 8, c0, ncols)
                s_ = box["s"]
                nt_b = min(128, ntok - tb * 128)
                bank = nb()
                for kc in range(8):
                    mm(pb[bank][0:nt_b, 0:ncols], hT[:, kc, tb * 128:tb * 128 + nt_b], W[s_][:, kc, 0:ncols], kc == 0, kc == 7,
                       reads=[("W", s_), ("hT", kc)], writes=[("pb", bank)])
                evac(bank, tb, nt_b)
            th.append(t)
        return th

    def tile_layer(l, ntok, tok0, xsrc, psrc, is_sample, t, last_tile, prefetched=False, nxt=None):
        nblk = (ntok + 127) // 128
        chunks = [(c * 128, min(128, ntok - c * 128)) for c in range(nblk)]
        w_in_l = w_in[l]
        hT_res = lambda kc: ("hT", kc)
        hT_rhs = lambda kc: hT[:, kc, 0:ntok]

        P.cur_tag = "pload"
        p_stage_xin = (not is_sample) and (not last_tile) and (prefetched or l > 0)
        for tb, (c0, nt) in enumerate(chunks):
            if tb < 2:
                P.op("sp", lambda e, tb=tb, c0=c0, nt=nt: e.dma_start(out=pin[tb % 2][0:nt, :], in_=psrc[l, tok0 + c0:tok0 + c0 + nt, :]),
                     writes=[("pin", tb % 2)], dma_sem=("pin", tb % 2))
            elif p_stage_xin:
                P.op("sp", lambda e, tb=tb, c0=c0, nt=nt: e.dma_start(out=xin[tb % 2][0:nt, 0:256], in_=psrc[l, tok0 + c0:tok0 + c0 + nt, :]),
                     writes=[("xin", tb % 2)], dma_sem=("pinx", tb % 2))
        P.cur_tag = "cache"
        have_prev = False
        if is_sample:
            have_prev = True
            P.op("sp", lambda e: e.dma_start(out=S[l][:], in_=sg[l].rearrange("h d v -> d h v")), writes=[("S", l)], dma_sem=("Sin", l))
            copy_op("act", Sbf[l][:], S[l][:], reads=[("S", l)], writes=[("Sbf", l, hq) for hq in range(4)])
            P.op("pool", lambda e: e.dma_start(out=va[:], in_=ck[l].rearrange("(b p) c -> p b c", p=128)),
                 writes=[("va", tb, g) for tb in range(4) for g in range(4)], dma_sem="ckld")
            for tb in range(4):
                bank = nb()
                for hp in range(8):
                    tp(pbv[bank][:, hp * 128:(hp + 1) * 128], va[:, tb, hp * 128:(hp + 1) * 128], identb,
                       reads=[("va", tb, 0), "init_pool"], writes=[("pb", bank)])
                copy_op(evac_eng(), Kprev[:, :, tb * 128:(tb + 1) * 128], pbv[bank][:].rearrange("p (h t) -> p h t", h=8),
                        reads=[("pb", bank)], writes=["Kprev"])
            Uv = Ub[:].rearrange("p a b -> p (a b)").rearrange("p (t c) -> p t c", c=1024)
            P.op("pool", lambda e: e.dma_start(out=Uv, in_=cv[l].rearrange("(b p) c -> p b c", p=128)),
                 writes=[("Ub", kc) for kc in range(8)], dma_sem="cvld")
            for tb in range(4):
                copy_op("dve", Vprev[:, tb, :].rearrange("p (h e) -> p h e", e=66)[:, :, 0:64],
                        Uv[:, tb, :].rearrange("p (h e) -> p h e", e=64),
                        reads=[("Ub", kc) for kc in range(8)], writes=["Vprev"])
            P.op("sp", lambda e: e.dma_start(out=kbs[l, 0:448, :], in_=ck[l, 64:512, :]), dma_sem="kroll")
            P.op("sp", lambda e: e.dma_start(out=vbs[l, 0:448, :], in_=cv[l, 64:512, :]), dma_sem="vroll")
        elif t > 0:
            have_prev = True
            P.op("sp", lambda e: e.dma_start(out=Kprev[:].rearrange("p a b -> p (a b)"), in_=kspill[l]),
                 reads=[("kspill", l)], writes=["Kprev"], dma_sem="kprev")
            P.op("sp", lambda e: e.dma_start(out=Vprev[:].rearrange("p a b -> p (a b)"), in_=vspill[l]),
                 reads=[("vspill", l)], writes=["Vprev"], dma_sem="vprev")

        P.cur_tag = "xload"
        if l == 0:
            for tb, (c0, nt) in enumerate(chunks):
                if prefetched:
                    xsrc_sb, xres = xstage(tb)
                else:
                    P.op("pool", lambda e, tb=tb, c0=c0, nt=nt: e.dma_start(out=xin[tb % 2][0:nt, :], in_=xsrc[tok0 + c0:tok0 + c0 + nt, :]),
                         writes=[("xin", tb % 2)], dma_sem=("xin", tb % 2))
                    xsrc_sb, xres = xin[tb % 2], [("xin", tb % 2)]
                for g in range(2):
                    bank = nb()
                    for kk in range(4):
                        kc = g * 4 + kk
                        tp(pb[bank][:, kk * 128:kk * 128 + nt], xsrc_sb[0:nt, kc * 128:(kc + 1) * 128], identf[0:nt, 0:nt],
                           reads=xres + ["init_sp"], writes=[("pb", bank)])
                    copy_op("dve", xT[:, g * 4:(g + 1) * 4, c0:c0 + nt],
                            pb[bank][:].rearrange("p (k t) -> p k t", k=4)[:, :, 0:nt],
                            reads=[("pb", bank)], writes=[("xT", g * 4 + kk) for kk in range(4)])

        P.cur_tag = "norm1"
        rmsnorm_to_hT(l, ntok, l * NG)

        P.cur_tag = "inprojA"
        for g in range(2):
            def ev(bank, cbi, g=g):
                h = g * 2 + cbi
                copy_op("act", R1[:, h, 0:ntok], pb[bank][:, 0:ntok], reads=[("pb", bank)], writes=[("R1", h)], scale=128.0 ** -0.5)
            formB(w_in_l, OFF_QA + g * 256, 8, 256, hT_rhs, hT_res, ev)
        for g in range(2):
            def ev(bank, cbi, g=g):
                h = g * 2 + cbi
                copy_op("dve", R1[:, 4 + h, 0:ntok], pb[bank][:, 0:ntok], reads=[("pb", bank)], writes=[("R1", 4 + h)])
            formB(w_in_l, OFF_KA + g * 256, 8, 256, hT_rhs, hT_res, ev)

        def ev_ra(bank, cbi):
            copy_op("dve", raT[0:16, 0:ntok], pb[bank][0:16, 0:ntok], reads=[("pb", bank)], writes=["raT"])
        formB(w_in_l, OFF_RA, 8, 16, hT_rhs, hT_res, ev_ra)
        for g in range(4):
            def ev(bank, tb, nt_b, g=g):
                copy_op(evac_eng(), va[0:nt_b, tb, g * 256:(g + 1) * 256], pb[bank][0:nt_b, 0:256],
                        reads=[("pb", bank)], writes=[("va", tb, g)])
            formA(w_in_l, OFF_VA + g * 256, 256, ntok, ev)

        def gate_silu_group(c_off, g, Ux, Uname):
            def ev(bank, cbi):
                kc = g * 2 + cbi
                P.op("act", lambda e: e.activation(out=Ux[:, kc, 0:ntok], in_=pb[bank][:, 0:ntok], func=AF.Silu),
                     reads=[("pb", bank)], writes=[(Uname, kc)])
            formB(w_in_l, c_off + g * 256, 8, 256, hT_rhs, hT_res, ev)

        for g in range(4):
            gate_silu_group(OFF_GA, g, Ua, "Ua")

        def gla_gen():
            def prep(ci):
                c0, nt = chunks[ci]
                par = ci % 2
                mm(pb[0][0:nt, 0:512], raT[0:17, c0:c0 + nt], wgu[0:17, l, :], True, True,
                   reads=["raT", "init_pool"], writes=[("pb", 0)])
                P.op("act", lambda e: e.activation(out=e1[0:nt, :], in_=pb[0][0:nt, :], func=AF.Exp, scale=-1.0),
                     reads=[("pb", 0)], writes=["e1"])
                P.op("act", lambda e: e.activation(out=e1[0:nt, :], in_=e1[0:nt, :], func=AF.Ln, bias=1.0),
                     reads=["e1"], writes=["e1"])
                yield
                for h in range(4):
                    mm(pb[1][:, h * 128:h * 128 + nt], e1[0:nt, h * 128:(h + 1) * 128], tri[0:nt, 0:nt], True, True,
                       reads=["e1", "init_sp"], writes=[("pb", 1)])
                bB3 = pb[1][:].rearrange("p (h t) -> p h t", h=4)[:, :, 0:nt]
                P.op("act", lambda e: e.activation(out=eb[:, :, 0:nt], in_=bB3, func=AF.Exp), reads=[("pb", 1)], writes=["eb"])
                P.op("act", lambda e: e.activation(out=enb[:, :, 0:nt], in_=bB3, func=AF.Exp, scale=-1.0), reads=[("pb", 1)], writes=["enb"])
                P.op("dve", lambda e: e.tensor_copy(out=ebl[:, par, :].unsqueeze(2), in_=eb[:, :, nt - 1:nt]),
                     reads=["eb"], writes=[("ebl", par)])
                P.op("dve", lambda e: e.tensor_tensor(out=qt2[par][:, :, 0:nt], in0=R1[:, 0:4, c0:c0 + nt], in1=eb[:, :, 0:nt], op=ALU.mult),
                     reads=["eb"] + [("R1", h) for h in range(4)], writes=[("qt", par)])
                P.op("dve", lambda e: e.tensor_tensor(out=kt[:, :, 0:nt], in0=R1[:, 4:8, c0:c0 + nt], in1=enb[:, :, 0:nt], op=ALU.mult),
                     reads=["enb"] + [("R1", 4 + h) for h in range(4)], writes=["kt"])
                yield
                for h in range(4):
                    mm(pb[0][0:nt, h * 128:h * 128 + nt], kt[:, h, 0:nt], qt2[par][:, h, 0:nt], True, True,
                       reads=["kt", ("qt", par)], writes=[("pb", 0)])
                for h in range(4):
                    tp(pbv[1][0:nt, h * 128:(h + 1) * 128], kt[:, h, 0:nt], identb, reads=["kt", "init_pool"], writes=[("pb", 1)])
                cm = cmaskb[0:nt, 0:nt].unsqueeze(1).broadcast_to([nt, 4, nt])
                P.op("dve", lambda e: e.tensor_tensor(
                    out=am2[par][0:nt, :, 0:nt], in0=pb[0][0:nt, :].rearrange("p (h t) -> p h t", h=4)[:, :, 0:nt], in1=cm, op=ALU.mult),
                    reads=[("pb", 0), "init_pool"], writes=[("am", par)])
                copy_op("act", ktTs2[par][0:nt, :], pbv[1][0:nt, 0:512], reads=[("pb", 1)], writes=[("ktTs", par)])
                yield

            def seq(ci):
                c0, nt = chunks[ci]
                par = ci % 2
                bO = [2, 3]
                for h in range(4):
                    bank = bO[h // 2]
                    reg = pb[bank][0:nt, (h % 2) * 256:(h % 2 + 1) * 256]
                    mm(reg, am2[par][0:nt, h, 0:nt], va[0:nt, ci, h * 256:(h + 1) * 256], True, False,
                       reads=[("am", par), ("va", ci, h)], writes=[("pb", bank)])
                    mm(reg, qt2[par][:, h, 0:nt], Sbf[l][:, h, :], False, True, reads=[("qt", par), ("Sbf", l, h)], writes=[("pb", bank)])
                yield
                eblb = ebl[:, par, :].unsqueeze(2).broadcast_to([128, 4, 256])
                for hh in range(2):
                    bS = 4 if hh == 0 else 1
                    for h in (hh * 2, hh * 2 + 1):
                        mm(pb[bS][:, (h % 2) * 256:(h % 2 + 1) * 256], ktTs2[par][0:nt, h * 128:(h + 1) * 128], va[0:nt, ci, h * 256:(h + 1) * 256], True, True,
                           reads=[("ktTs", par), ("va", ci, h)], writes=[("pb", bS)])
                    Sv = S[l][:, hh * 2:(hh + 1) * 2, :].rearrange("p h v -> p (h v)")
                    P.op("dve", lambda e, Sv=Sv, bS=bS: e.tensor_tensor(out=Sv, in0=pb[bS][:, :], in1=Sv, op=ALU.add),
                         reads=[("pb", bS), ("S", l)], writes=[("S", l), ("S", l, hh)])
                    for h in (hh * 2, hh * 2 + 1):
                        P.op("act", lambda e, h=h: e.activation(out=Sbf[l][:, h, :], in_=S[l][:, h, :], func=AF.Copy, scale=ebl[:, par, h:h + 1]),
                             reads=[("S", l, hh), ("ebl", par)], writes=[("Sbf", l, h)])
                P.op("dve", lambda e: e.tensor_tensor(out=S[l][:], in0=S[l][:], in1=eblb, op=ALU.mult),
                     reads=[("ebl", par), ("S", l)], writes=[("S", l), ("S", l, 0), ("S", l, 1)])
                yield
                for h in range(4):
                    bank = bO[h // 2]
                    P.op("act", lambda e, bank=bank, h=h: e.activation(
                        out=junk[0:nt, 0:256], in_=pb[bank][0:nt, (h % 2) * 256:(h % 2 + 1) * 256], func=AF.Square, accum_out=ssm[0:nt, h:h + 1]),
                        reads=[("pb", bank)], writes=["ssm", ("sq", 2)])
                P.op("act", lambda e: e.activation(out=ssm[0:nt, 4:8], in_=ssm[0:nt, 0:4], func=AF.Ln, scale=1.0 / 256, bias=EPS),
                     reads=["ssm"], writes=["ssr"])
                P.op("act", lambda e: e.activation(out=ssm[0:nt, 4:8], in_=ssm[0:nt, 4:8], func=AF.Exp, scale=-0.5),
                     reads=["ssr"], writes=["ssr"])
                for h in range(4):
                    bank = bO[h // 2]
                    P.op("dve", lambda e, bank=bank, h=h: e.scalar_tensor_tensor(
                        out=on[0:nt, h * 256:(h + 1) * 256], in0=pb[bank][0:nt, (h % 2) * 256:(h % 2 + 1) * 256],
                        scalar=ssm[0:nt, 4 + h:5 + h], in1=gA[0:nt, l, :], op0=ALU.mult, op1=ALU.mult),
                        reads=[("pb", bank), "ssr", "init_sp"], writes=[("on", h)])
                yield
                for blk in range(8):
                    tp(pbv[4][:, blk * 128:blk * 128 + nt], on[0:nt, blk * 128:(blk + 1) * 128], identb[0:nt, 0:nt],
                       reads=[("on", blk // 2), "init_pool"], writes=[("pb", 4)])
                P.op("dve", lambda e: e.tensor_tensor(
                    out=Ua[:, :, c0:c0 + nt], in0=pbv[4][:].rearrange("p (b t) -> p b t", b=8)[:, :, 0:nt], in1=Ua[:, :, c0:c0 + nt], op=ALU.mult),
                    reads=[("pb", 4)] + [("Ua", kc) for kc in range(8)], writes=[("Ua", kc) for kc in range(8)])
                yield

            n = len(chunks)
            for _ in prep(0):
                yield
            for ci in range(n):
                sg_ = seq(ci)
                pg_ = prep(ci + 1) if ci + 1 < n else None
                for ch in GLA_PAT:
                    if ch == "S":
                        next(sg_)
                        yield
                    elif pg_ is not None:
                        next(pg_)
                        yield

        def qk_flush(keep=0):
            q = st.setdefault("qk_q", [])
            while len(q) > keep:
                q.pop(0)()

        def qk_group(c_off, g, is_q):
            def ev(bank, cbi):
                hp = g * 2 + cbi
                st["qk_n"] = st.get("qk_n", 0) + 1
                i = st["qk_n"] % 3
                P.op("act", lambda e: e.activation(out=sq[i][:, 0:ntok], in_=pb[bank][:, 0:ntok], func=AF.Square),
                     reads=[("pb", bank)], writes=[("sq", i)])
                qk_flush(QK_DEPTH - 1)

                def finish():
                    b2 = nb()
                    mm(pb[b2][:, 0:ntok], bonesb, sq[i][:, 0:ntok], True, True, reads=[("sq", i), "init_pool"], writes=[("pb", b2)])
                    ri = i % 2
                    rstd_from_psum(b2, ri, ntok, 64)
                    if is_q:
                        for par in range(2):
                            ps_ = slice(par * 64, (par + 1) * 64)
                            P.op("dve", lambda e, ps_=ps_, par=par: e.scalar_tensor_tensor(
                                out=Qz[ps_, hp, par, 0:ntok], in0=pb[bank][ps_, 0:ntok], scalar=gq[ps_, l:l + 1],
                                in1=rstd[ri][ps_, 0:ntok], op0=ALU.mult, op1=ALU.mult),
                                reads=[("pb", bank), ("rstd", ri), "gq"], writes=[("Qz", hp)])
                    else:
                        P.op("dve", lambda e: e.scalar_tensor_tensor(
                            out=Kcur[:, hp, 0:ntok], in0=pb[bank][:, 0:ntok], scalar=gvec[:, l * NG + 17:l * NG + 18],
                            in1=rstd[ri][:, 0:ntok], op0=ALU.mult, op1=ALU.mult),
                            reads=[("pb", bank), ("rstd", ri), "init_sp"], writes=[("Kcur", hp)])
                st["qk_q"].append(finish)
            formB(w_in_l, c_off + g * 256, 8, 256, hT_rhs, hT_res, ev)

        def vb_group(g):
            def ev(bank, tb, nt_b):
                copy_op(evac_eng(), Vcur[0:nt_b, tb, :].rearrange("p (h e) -> p h e", e=66)[:, g * 4:(g + 1) * 4, 0:64],
                        pb[bank][0:nt_b, 0:256].rearrange("p (h e) -> p h e", e=64),
                        reads=[("pb", bank)], writes=[("Vcur", tb, g)])
            formA(w_in_l, OFF_VB + g * 256, 256, ntok, ev)

        def mga_group(g):
            def ev(bank, cbi):
                kc = g * 2 + cbi
                P.op("act", lambda e: e.activation(out=M[:, kc, 0:ntok], in_=pb[bank][:, 0:ntok], func=AF.Sigmoid),
                     reads=[("pb", bank)], writes=[("M", kc)])
            formB(w_in_l, OFF_MGA + g * 256, 8, 256, hT_rhs, hT_res, ev)

        def mgb_group(g):
            def ev(bank, cbi):
                kc = g * 2 + cbi
                P.op("act", lambda e: e.activation(out=M2[:, kc * TT:kc * TT + ntok], in_=pb[bank][:, 0:ntok], func=AF.Sigmoid),
                     reads=[("pb", bank)], writes=[("va", kc // 2, (kc % 2) * 2), ("va", kc // 2, (kc % 2) * 2 + 1)])
            formB(w_in_l, OFF_MGB + g * 256, 8, 256, hT_rhs, hT_res, ev)

        Ua_rhs = lambda kc: Ua[:, kc, 0:ntok]
        Ua_res = lambda kc: ("Ua", kc)
        Ub_rhs = lambda kc: Ub[:, kc, 0:ntok]
        Ub_res = lambda kc: ("Ub", kc)

        def bra_group(g):
            def ev(bank, cbi):
                kc = g * 2 + cbi
                P.op("dve", lambda e: e.tensor_tensor(out=M[:, kc, 0:ntok], in0=pb[bank][:, 0:ntok], in1=M[:, kc, 0:ntok], op=ALU.mult),
                     reads=[("pb", bank), ("M", kc)], writes=[("M", kc)])
            formB(wba[l], g * 256, 8, 256, Ua_rhs, Ua_res, ev)

        P.cur_tag = "inprojB"
        for g in range(4):
            qk_group(OFF_QB, g, True)
        for g in range(4):
            qk_group(OFF_KB, g, False)
        qk_flush()
        fill1 = []
        for g in range(4):
            def evv(bank, tb, nt_b, g=g):
                copy_op(evac_eng(), Vcur[0:nt_b, tb, :].rearrange("p (h e) -> p h e", e=66)[:, g * 4:(g + 1) * 4, 0:64],
                        pb[bank][0:nt_b, 0:256].rearrange("p (h e) -> p h e", e=64),
                        reads=[("pb", bank)], writes=[("Vcur", tb, g)])
            fill1.extend(formA_thunks(w_in_l, OFF_VB + g * 256, 256, ntok, evv))
        interleave(gla_gen(), fill1, 7 * nblk, "gla")
        qk_flush()

        P.cur_tag = "kvout"
        if is_sample or last_tile:
            dst = (sgs if is_sample else sgp)[l].rearrange("h d v -> d h v")
            key = ("Sout", l, is_sample)
            P.op("sp", lambda e, dst=dst: e.dma_start(out=dst, in_=S[l][:]), reads=[("S", l)], dma_sem=key)
            out_sems.append(key)
            if is_sample:
                P.op("dve", lambda e: e.memset(S[l][:], 0.0), writes=[("S", l)])
                P.op("dve", lambda e: e.memset(Sbf[l][:], 0.0), writes=[("Sbf", l, hq) for hq in range(4)])

        Kcur_all = [("Kcur", hp) for hp in range(8)]
        Vcur_all = [("Vcur", tb, g) for tb in range(4) for g in range(4)]
        if (not is_sample) and (not last_tile):
            P.op("sp", lambda e: e.dma_start(out=kspill[l], in_=Kcur[:].rearrange("p a b -> p (a b)")),
                 reads=Kcur_all, writes=[("kspill", l)], dma_sem=("kst", l))
            P.op("sp", lambda e: e.dma_start(out=vspill[l], in_=Vcur[:].rearrange("p a b -> p (a b)")),
                 reads=Vcur_all, writes=[("vspill", l)], dma_sem=("vst", l))
        if is_sample or last_tile:
            kdst = kbs if is_sample else kbp
            vdst = vbs if is_sample else vbp
            r0 = 448 if is_sample else 0
            for tb, (c0, nt) in enumerate(chunks):
                bank = nb()
                for hp in range(8):
                    tp(pbv[bank][0:nt, hp * 128:(hp + 1) * 128], Kcur[:, hp, c0:c0 + nt], identb, reads=[("Kcur", hp), "init_pool"], writes=[("pb", bank)])
                copy_op("act", xin[0][0:nt, :], pbv[bank][0:nt, :], reads=[("pb", bank)], writes=[("xin", 0)])
                key = ("kout", l, is_sample)
                P.op("sp", lambda e, c0=c0, nt=nt: e.dma_start(out=kdst[l, r0 + c0:r0 + c0 + nt, :], in_=xin[0][0:nt, :]),
                     reads=[("xin", 0)], dma_sem=key)
                copy_op("dve", xin[1][0:nt, :].rearrange("p (h e) -> p h e", e=64),
                        Vcur[0:nt, tb, :].rearrange("p (h e) -> p h e", e=66)[:, :, 0:64],
                        reads=[("Vcur", tb, g) for g in range(4)], writes=[("xin", 1)])
                key2 = ("vout", l, is_sample)
                P.op("sp", lambda e, c0=c0, nt=nt: e.dma_start(out=vdst[l, r0 + c0:r0 + c0 + nt, :], in_=xin[1][0:nt, :]),
                     reads=[("xin", 1)], dma_sem=key2)
            out_sems.extend([("kout", l, is_sample), ("vout", l, is_sample)])

        def band_gen():
            for pi, (q0, nq) in enumerate(chunks):
                blocks = []
                for C in range(4, -1, -1):
                    b = pi - C
                    if b >= 0:
                        nk = min(128, ntok - b * 128)
                        blocks.append((C, Kcur, "cur", b, nk))
                    elif have_prev:
                        blocks.append((C, Kprev, "prev", 4 + b, 128))

                def qk_exp(hg):
                    for (C, Kb, which, b, nk) in blocks:
                        bank = nb()
                        Kres = (lambda hp: ("Kcur", hp)) if which == "cur" else (lambda hp: "Kprev")
                        if C <= 1:
                            mm(pb[bank][0:nk, 0:4 * nq], Jb[:, 0:nk], Gp[:, l * 2 + C, hg * 4:(hg + 1) * 4, 0:nq], True, False,
                               reads=[("Gp", l), "init_pool"], writes=[("pb", bank)])
                        for h2 in range(2):
                            hp = hg * 2 + h2
                            mm(pb[bank][0:nk, h2 * 2 * nq:(h2 + 1) * 2 * nq], Kb[:, hp, b * 128:b * 128 + nk], Qz[:, hp, :, q0:q0 + nq],
                               C > 1, h2 == 1, reads=[Kres(hp), ("Qz", hp)], writes=[("pb", bank)])
                        src3 = pb[bank][:, 0:4 * nq].rearrange("p (h q) -> p h q", h=4)
                        P.op("act", lambda e, src3=src3, C=C, hg=hg, nk=nk: e.activation(
                            out=PT[hg % 2][C][0:nk, :, 0:nq], in_=src3[0:nk, :, 0:nq], func=AF.Exp),
                            reads=[("pb", bank)], writes=[("PT", hg % 2, C)])
                        if nq == 128 and C == 4:
                            P.op("dve", lambda e, C=C, hg=hg: e.memset(PT[hg % 2][C][0:64, :, 64:128], 0.0), writes=[("PT", hg % 2, C)])
                        elif nq == 128 and C == 0:
                            P.op("dve", lambda e, C=C, hg=hg: e.memset(PT[hg % 2][C][64:128, :, 0:64], 0.0), writes=[("PT", hg % 2, C)])
                        yield

                def pv_norm(hg):
                    ob_bank = nb()
                    for hh in range(4):
                        h = hg * 4 + hh
                        for bi, (C, Kb, which, b, nk) in enumerate(blocks):
                            Vb = Vcur if which == "cur" else Vprev
                            vres = [("Vcur", b, h // 4)] if which == "cur" else ["Vprev"]
                            mm(pb[ob_bank][0:nq, hh * 66:(hh + 1) * 66], PT[hg % 2][C][0:nk, hh, 0:nq], Vb[0:nk, b, h * 66:(h + 1) * 66],
                               bi == 0, bi == len(blocks) - 1, reads=[("PT", hg % 2, C)] + vres, writes=[("pb", ob_bank)])
                    ob3 = pb[ob_bank][0:nq, 0:264].rearrange("p (h e) -> p h e", e=66)
                    P.op("dve", lambda e, ob3=ob3, hg=hg: e.reciprocal(out=rec[0:nq, hg * 4:(hg + 1) * 4].unsqueeze(2), in_=ob3[:, :, 64:65]),
                         reads=[("pb", ob_bank)], writes=["rec"])
                    recb = rec[0:nq, hg * 4:(hg + 1) * 4].unsqueeze(2).broadcast_to([nq, 4, 64])
                    P.op("dve", lambda e, ob3=ob3, hg=hg, recb=recb: e.tensor_tensor(
                        out=on[0:nq, hg * 256:(hg + 1) * 256].rearrange("p (h e) -> p h e", e=64), in0=ob3[:, :, 0:64], in1=recb, op=ALU.mult),
                        reads=[("pb", ob_bank), "rec"], writes=[("onb", hg)])

                for hg in range(5):
                    if hg < 4:
                        yield from qk_exp(hg)
                    if hg >= 1:
                        pv_norm(hg - 1)
                        yield
                bI = nb()
                for blk in range(8):
                    tp(pbv[bI][:, blk * 128:blk * 128 + nq], on[0:nq, blk * 128:(blk + 1) * 128], identb[0:nq, 0:nq],
                       reads=[("onb", blk // 2), "init_pool"], writes=[("pb", bI)])
                P.op("dve", lambda e, bI=bI, q0=q0, nq=nq: e.tensor_tensor(
                    out=Ub[:, :, q0:q0 + nq], in0=pbv[bI][:].rearrange("p (b t) -> p b t", b=8)[:, :, 0:nq], in1=Ub[:, :, q0:q0 + nq], op=ALU.mult),
                    reads=[("pb", bI)] + [("Ub", kc) for kc in range(8)], writes=[("Ub", kc) for kc in range(8)])
                yield

        fill2 = []
        for g in range(4):
            def evg(bank, cbi, g=g):
                kc = g * 2 + cbi
                i = kc % 2
                P.op("act", lambda e: e.activation(out=gt[i][:, 0:ntok], in_=pb[bank][:, 0:ntok], func=AF.Tanh, scale=0.5),
                     reads=[("pb", bank)], writes=[("sq", i)])
                P.op("dve", lambda e: e.scalar_tensor_tensor(out=Ub[:, kc, 0:ntok], in0=gt[i][:, 0:ntok], scalar=1.0, in1=pb[bank][:, 0:ntok],
                                                             op0=ALU.add, op1=ALU.mult),
                     reads=[("sq", i), ("pb", bank)], writes=[("Ub", kc)])
            fill2.extend(formB_thunks(w_in_l, OFF_GB + g * 256, 8, 256, hT_rhs, hT_res, evg))
        n_front = len(fill2)
        for g in range(4):
            def evm(bank, cbi, g=g):
                kc = g * 2 + cbi
                P.op("act", lambda e: e.activation(out=M[:, kc, 0:ntok], in_=pb[bank][:, 0:ntok], func=AF.Tanh, scale=0.5),
                     reads=[("pb", bank)], writes=[("M", kc)])
            fill2.extend(formB_thunks(w_in_l, OFF_MGA + g * 256, 8, 256, hT_rhs, hT_res, evm))
        for g in range(4):
            def evb(bank, cbi, g=g):
                kc = g * 2 + cbi
                P.op("act", lambda e: e.activation(out=M2[:, kc * TT:kc * TT + ntok], in_=pb[bank][:, 0:ntok], func=AF.Tanh, scale=0.5),
                     reads=[("pb", bank)], writes=[("va", kc // 2, (kc % 2) * 2), ("va", kc // 2, (kc % 2) * 2 + 1)])
            fill2.extend(formB_thunks(w_in_l, OFF_MGB + g * 256, 8, 256, hT_rhs, hT_res, evb))
        for g in range(4):
            def eva(bank, cbi, g=g):
                kc = g * 2 + cbi
                P.op("dve", lambda e: e.scalar_tensor_tensor(out=M[:, kc, 0:ntok], in0=M[:, kc, 0:ntok], scalar=1.0, in1=pb[bank][:, 0:ntok],
                                                             op0=ALU.add, op1=ALU.mult),
                     reads=[("pb", bank), ("M", kc)], writes=[("M", kc)])
            fill2.extend(formB_thunks(wba[l], g * 256, 8, 256, Ua_rhs, Ua_res, eva))
        ny = 0
        for pi in range(nblk):
            nbk = sum(1 for C in range(5) if (pi - C >= 0) or have_prev)
            ny += 4 * (nbk + 1) + 1
        interleave(band_gen(), fill2, ny, "band", front=n_front)

        P.cur_tag = "branchB"
        for g in range(4):
            def ev(bank, cbi, g=g):
                kc = g * 2 + cbi
                P.op("dve", lambda e: e.scalar_tensor_tensor(out=ftmp[:, 0:ntok], in0=M2[:, kc * TT:kc * TT + ntok], scalar=1.0, in1=pb[bank][:, 0:ntok],
                                                             op0=ALU.add, op1=ALU.mult),
                     reads=[("pb", bank), ("va", kc // 2, (kc % 2) * 2), ("va", kc // 2, (kc % 2) * 2 + 1)], writes=["ftmp"])
                P.op("dve", lambda e: e.scalar_tensor_tensor(out=M[:, kc, 0:ntok], in0=ftmp[:, 0:ntok], scalar=0.5, in1=M[:, kc, 0:ntok],
                                                             op0=ALU.mult, op1=ALU.add),
                     reads=["ftmp", ("M", kc)], writes=[("M", kc)])
            formB(wbb[l], g * 256, 8, 256, Ub_rhs, Ub_res, ev)

        if nxt is not None and l == DEPTH - 1:
            nsrc, ntok0, nntok = nxt
            for tb in range((nntok + 127) // 128):
                nt2 = min(128, nntok - tb * 128)
                dstv, dres = xstage(tb)
                P.op("sp", lambda e, tb=tb, nt2=nt2, dstv=dstv: e.dma_start(out=dstv[0:nt2, :], in_=nsrc[ntok0 + tb * 128:ntok0 + tb * 128 + nt2, :]),
                     writes=dres, dma_sem=("xpre", tb))
        P.cur_tag = "pload"
        for tb, (c0, nt) in enumerate(chunks):
            psb, pres = pin[tb % 2], ("pin", tb % 2)
            if tb >= 2:
                if p_stage_xin:
                    psb, pres = xin[tb % 2], ("xin", tb % 2)
                else:
                    P.op("sp", lambda e, tb=tb, c0=c0, nt=nt: e.dma_start(out=pin[tb % 2][0:nt, :], in_=psrc[l, tok0 + c0:tok0 + c0 + nt, :]),
                         writes=[("pin", tb % 2)], dma_sem=("pin", tb % 2))
            bank = nb()
            for j in range(2):
                tp(pb[bank][:, j * 128:j * 128 + nt], psb[0:nt, j * 128:(j + 1) * 128], identf[0:nt, 0:nt],
                   reads=[pres, "init_sp"], writes=[("pb", bank)])
            copy_op("act", pT[:, 0:2, c0:c0 + nt], pb[bank][:, 0:256].rearrange("p (j t) -> p j t", j=2)[:, :, 0:nt],
                    reads=[("pb", bank)], writes=["pT"])
        P.cur_tag = "wout"
        M_rhs = lambda kc: M[:, kc, 0:ntok]
        M_res = lambda kc: ("M", kc)
        for g in range(4):
            def ev(bank, cbi, g=g):
                kc = g * 2 + cbi
                P.op("dve", lambda e: e.scalar_tensor_tensor(out=xT[:, kc, 0:ntok], in0=pb[bank][:, 0:ntok], scalar=0.5, in1=xT[:, kc, 0:ntok],
                                                             op0=ALU.mult, op1=ALU.add),
                     reads=[("pb", bank), ("xT", kc)], writes=[("xT", kc)])
            formB(wo[l], g * 256, 8, 256, M_rhs, M_res, ev)

        P.cur_tag = "ple"
        rmsnorm_to_hT(l, ntok, l * NG + 8)
        for g in range(4):
            def ev(bank, cbi, g=g):
                kc = g * 2 + cbi
                P.op("act", lambda e: e.activation(out=M[:, kc, 0:ntok], in_=pb[bank][:, 0:ntok], func=AF.Sigmoid),
                     reads=[("pb", bank)], writes=[("M", kc)])
            formB(wpg[l], g * 256, 8, 256, hT_rhs, hT_res, ev)
        pT_rhs = lambda kc: pT[:, kc, 0:ntok]
        pT_res = lambda kc: "pT"
        for g in range(4):
            def ev(bank, cbi, g=g):
                kc = g * 2 + cbi
                P.op("dve", lambda e: e.tensor_tensor(out=ftmp[:, 0:ntok], in0=pb[bank][:, 0:ntok], in1=M[:, kc, 0:ntok], op=ALU.mult),
                     reads=[("pb", bank), ("M", kc)], writes=["ftmp"])
                P.op("dve", lambda e: e.tensor_tensor(out=xT[:, kc, 0:ntok], in0=ftmp[:, 0:ntok], in1=xT[:, kc, 0:ntok], op=ALU.add),
                     reads=["ftmp", ("xT", kc)], writes=[("xT", kc)])
            formB(wp[l], g * 256, 2, 256, pT_rhs, pT_res, ev)

        P.cur_tag = "yout"
        if l == DEPTH - 1:
            ydst = ys if is_sample else yp
            for tb, (c0, nt) in enumerate(chunks):
                buf = tb % 2
                for g in range(2):
                    bank = nb()
                    for kk in range(4):
                        kc = g * 4 + kk
                        tp(pb[bank][0:nt, kk * 128:(kk + 1) * 128], xT[:, kc, c0:c0 + nt], identf, reads=[("xT", kc), "init_sp"], writes=[("pb", bank)])
                    copy_op("dve", xin[buf][0:nt, g * 512:(g + 1) * 512], pb[bank][0:nt, :], reads=[("pb", bank)], writes=[("xin", buf)])
                key = ("yout", buf, is_sample)
                P.op("sp", lambda e, c0=c0, nt=nt, buf=buf: e.dma_start(out=ydst[tok0 + c0:tok0 + c0 + nt, :], in_=xin[buf][0:nt, :]),
                     reads=[("xin", buf)], dma_sem=key)
                if key not in out_sems:
                    out_sems.append(key)

    tiles = [(TT, t * TT, xp, pp, False, t, t == NT - 1) for t in range(NT)]
    if do_sample:
        tiles.append((64, 0, xs, ps, True, 0, False))
    for ti, (ntok_, tok0_, xsrc_, psrc_, iss_, t_, last_) in enumerate(tiles):
        nxt = None
        if ti + 1 < len(tiles):
            n2 = tiles[ti + 1]
            nxt = (n2[2], n2[1], n2[0])
        for l in range(DEPTH):
            tile_layer(l, ntok_, tok0_, xsrc_, psrc_, iss_, t_, last_, prefetched=(ti > 0), nxt=nxt)
    if do_sample:
        out_sems.extend(["kroll", "vroll"])
    P.emit(final_wait_sems=out_sems)
    return nc


_CACHE = {}


def _consts():
    c = np.zeros((128, 6, 128), np.float32)
    c[:, 0, :] = np.eye(128)
    c[:, 1, :] = np.eye(128)[::-1]
    j = np.arange(128)[:, None]
    i = np.arange(128)[None, :]
    c[:, 2, :] = (j <= i)
    c[:, 3, :] = (j <= i) * (-1.0 / 16.0)
    blk = np.zeros((128, 128), np.float32)
    blk[0:64, 0:64] = 1
    blk[64:128, 64:128] = 1
    c[:, 4, :] = blk
    c[:, 5, :] = 1
    return c


def make_in_maps(inp, SEQ, DEPTH, n_cores=8):
    f = lambda a: np.ascontiguousarray(np.asarray(a, dtype=np.float32))
    gv = np.zeros((128, DEPTH * NG), np.float32)
    for l in range(DEPTH):
        gv[:, l * NG:l * NG + 8] = f(inp["norm_g"])[l].reshape(8, 128).T
        gv[:, l * NG + 8:l * NG + 16] = f(inp["ple_norm_g"])[l].reshape(8, 128).T
        gv[:, l * NG + 16] = np.tile(f(inp["q_norm_g"])[l], 2)
        gv[:, l * NG + 17] = np.tile(f(inp["k_norm_g"])[l], 2)
    gA = np.ascontiguousarray(np.broadcast_to(f(inp["gla_norm_g"])[None, :, :], (128, DEPTH, 256)))
    rb = f(inp["rel_bias"])
    tabext = np.ascontiguousarray(np.concatenate([rb, np.repeat(rb[..., -1:], 127, axis=-1)], axis=-1))
    cb = np.ascontiguousarray(np.broadcast_to(rb[None, :, :, 256], (128, DEPTH, 16)))
    wgu = np.ascontiguousarray(np.concatenate([f(inp["w_gate_up"]), f(inp["b_gate"])[:, None, :]], axis=1))
    common = {
        "w_in": f(inp["w_in"]), "wgu": wgu, "wba": f(inp["w_branch_a"]), "wbb": f(inp["w_branch_b"]),
        "wo": f(inp["w_out"]), "wpg": f(inp["w_ple_gate"]), "wp": f(inp["w_ple"]),
        "gv": gv, "gA": gA, "tabext": tabext, "cb": cb, "cst": _consts(),
    }
    xp = f(inp["x_prompt"]); pp = f(inp["p_prompt"]); xs = f(inp["x_sample"]); ps = f(inp["p_sample"])
    sg = f(inp["state_gla"]); ck = f(inp["cache_band_k"]); cv = f(inp["cache_band_v"])
    maps = []
    for c in range(n_cores):
        b = c % xp.shape[0]
        m = dict(common)
        m["xp"] = np.ascontiguousarray(xp[b])
        m["pp"] = np.ascontiguousarray(pp[:, b])
        m["xs"] = np.ascontiguousarray(xs[c])
        m["ps"] = np.ascontiguousarray(ps[:, c])
        m["sg"] = np.ascontiguousarray(sg[:, c])
        m["ck"] = np.ascontiguousarray(ck[:, c].reshape(DEPTH, 512, 1024))
        m["cv"] = np.ascontiguousarray(cv[:, c].reshape(DEPTH, 512, 1024))
        maps.append(m)
    return maps


def run(inp, SEQ, DEPTH):
    key = (SEQ, DEPTH)
    if key not in _CACHE:
        _CACHE[key] = build_program(SEQ, DEPTH)
    nc = _CACHE[key]
    maps = make_in_maps(inp, SEQ, DEPTH)
    res = run_bass_kernel_spmd(nc, maps, core_ids=list(range(8)))
    r = res.results
    B = np.asarray(inp["x_prompt"]).shape[0]
    y_prompt = np.stack([r[b]["yp"] for b in range(B)])
    y_sample = np.stack([r[c]["ys"] for c in range(8)])
    sgp = np.stack([r[b]["sgp"] for b in range(B)], axis=1)
    kbp = np.stack([r[b]["kbp"] for b in range(B)], axis=1).reshape(DEPTH, B, 512, 16, 64)
    vbp = np.stack([r[b]["vbp"] for b in range(B)], axis=1).reshape(DEPTH, B, 512, 16, 64)
    sgs = np.stack([r[c]["sgs"] for c in range(8)], axis=1)
    kbs = np.stack([r[c]["kbs"] for c in range(8)], axis=1).reshape(DEPTH, 8, 512, 16, 64)
    vbs = np.stack([r[c]["vbs"] for c in range(8)], axis=1).reshape(DEPTH, 8, 512, 16, 64)
    return (y_prompt, y_sample, sgp, kbp, vbp, sgs, kbs, vbs)


def kernel(**inputs):
    return run(inputs, 4096, 2)
```
